# Optimizing a Trainium2 kernel written in Bass

```python
import math
import jax, jax.numpy as jnp
from jax import lax
import numpy as np

D_MODEL = 1024
BATCH = 8
SEQ = 2048
DEPTH = 2

NORM_EPS = 1e-6
N_BRANCH = 3
N_MOD = 6

RW_HEADS = 8
RW_HEAD_DIM = 64
RW_WIDTH = RW_HEADS * RW_HEAD_DIM
RW_DECAY_LORA = 64
RW_AAA_LORA = 64
RW_GATE_LORA = 128
RW_GN_EPS = 64e-5
RW_IN = 3 * RW_WIDTH + RW_DECAY_LORA + RW_AAA_LORA + RW_GATE_LORA

SB_HEADS = 8
SB_HEAD_DIM = 64
SB_WIDTH = SB_HEADS * SB_HEAD_DIM
SB_BLOCK = 128
SB_IN = 3 * SB_WIDTH

M2_HEADS = 16
M2_HEAD_DIM = 64
M2_WIDTH = M2_HEADS * M2_HEAD_DIM
M2_STATE = 128
M2_GROUPS = 2
M2_HEADS_PER_GROUP = M2_HEADS // M2_GROUPS
M2_CONV = 4
M2_CHUNK = 128
M2_CONV_DIM = M2_WIDTH + 2 * M2_GROUPS * M2_STATE
M2_IN = M2_WIDTH + M2_CONV_DIM + M2_HEADS

GATE_IN = N_BRANCH * D_MODEL
N_IN = RW_IN + SB_IN + M2_IN + GATE_IN

D_FF = 2816
FFN_CONV = 3

kernel_name = 'hybrid_rwkv7_stickbreak_mamba2_convffn'


def split_sizes(x, sizes):
    return jnp.split(x, [int(i) for i in np.cumsum(sizes)[:-1]], axis=-1)


def rms_norm(x, g):
    xf = x.astype(jnp.float32)
    y = xf * lax.rsqrt(jnp.mean(xf * xf, axis=-1, keepdims=True) + NORM_EPS)
    return (y * g.astype(jnp.float32)).astype(x.dtype)


def token_shift(x):
    return jnp.pad(x, ((0, 0), (1, 0), (0, 0)))[:, :-1]


def causal_dwconv(x, w, b):
    k_w, seq = w.shape[0], x.shape[1]
    xp = jnp.pad(x, ((0, 0), (k_w - 1, 0), (0, 0)))
    return b + sum(w[i] * xp[:, i:i + seq] for i in range(k_w))


def rwkv7_time_mix(p, mu, w0, w2, a0, a2, g2, k_k, k_a, r_k, ln_g, ln_b):
    bsz, seq, _ = p.shape
    f32 = jnp.float32
    p = p.astype(f32)
    p = p + (token_shift(p) - p) * mu
    r, k, v, w_lo, a_lo, g_lo = split_sizes(
        p, [RW_WIDTH, RW_WIDTH, RW_WIDTH, RW_DECAY_LORA, RW_AAA_LORA, RW_GATE_LORA])
    log_w = -jax.nn.softplus(-(w0 + jnp.tanh(w_lo) @ w2)) - 0.5
    decay = jnp.exp(-jnp.exp(log_w))
    a = jax.nn.sigmoid(a0 + a_lo @ a2)
    g = jax.nn.sigmoid(g_lo) @ g2
    heads = lambda t: t.reshape(bsz, seq, RW_HEADS, RW_HEAD_DIM)
    kk = heads(k * k_k)
    kk = kk * lax.rsqrt(jnp.maximum(jnp.sum(kk * kk, axis=-1, keepdims=True), 1e-24))
    k = k * (1.0 + (a - 1.0) * k_a)
    r, k, v, decay, a = heads(r), heads(k), heads(v), heads(decay), heads(a)

    def step(state, inp):
        r_t, w_t, k_t, v_t, kk_t, a_t = inp
        sa = jnp.einsum('bhij,bhj->bhi', state, -kk_t)
        state = (state * w_t[:, :, None, :]
                 + sa[..., :, None] * (kk_t * a_t)[..., None, :]
                 + v_t[..., :, None] * k_t[..., None, :])
        return state, jnp.einsum('bhij,bhj->bhi', state, r_t)

    seq_first = lambda t: jnp.swapaxes(t, 0, 1)
    init = jnp.zeros((bsz, RW_HEADS, RW_HEAD_DIM, RW_HEAD_DIM), f32)
    _, y = lax.scan(step, init, (seq_first(r), seq_first(decay), seq_first(k),
                                 seq_first(v), seq_first(kk), seq_first(a)))
    y = seq_first(y)
    mean = jnp.mean(y, axis=-1, keepdims=True)
    var = jnp.mean(jnp.square(y - mean), axis=-1, keepdims=True)
    y = ((y - mean) * lax.rsqrt(var + RW_GN_EPS)).reshape(bsz, seq, RW_WIDTH) * ln_g + ln_b
    bonus = jnp.sum(r * k * r_k, axis=-1, keepdims=True) * v
    y = y + bonus.reshape(bsz, seq, RW_WIDTH)
    return y * g


def stick_breaking_attention(q, k, v):
    bsz, seq, n_h, d_h = q.shape
    n_blk = seq // SB_BLOCK
    scale = d_h ** -0.5
    q_blocks = jnp.moveaxis(q.reshape(bsz, n_blk, SB_BLOCK, n_h, d_h), 1, 0)
    k_pos = jnp.arange(seq)

    def block(args):
        q_blk, blk_idx = args
        q_pos = blk_idx * SB_BLOCK + jnp.arange(SB_BLOCK)
        z = jnp.einsum('bqhd,bshd->bhqs', q_blk, k).astype(jnp.float32) * scale
        causal = k_pos[None, :] < q_pos[:, None]
        log_beta = jax.nn.log_sigmoid(z)
        log_1m = jnp.where(causal, jax.nn.log_sigmoid(-z), 0.0)
        later = lax.cumsum(log_1m, axis=3, reverse=True) - log_1m
        att = jnp.where(causal, jnp.exp(log_beta + later), 0.0)
        return jnp.einsum('bhqs,bshd->bqhd', att.astype(v.dtype), v)

    out = lax.map(block, (q_blocks, jnp.arange(n_blk)))
    return jnp.moveaxis(out, 0, 1).reshape(bsz, seq, n_h * d_h)


def segsum(a):
    t = a.shape[-1]
    x = jnp.broadcast_to(a[..., :, None], a.shape + (t,))
    strict = jnp.tril(jnp.ones((t, t), dtype=bool), -1)
    s = jnp.cumsum(jnp.where(strict, x, 0.0), axis=-2)
    return jnp.where(jnp.tril(jnp.ones((t, t), dtype=bool)), s, -jnp.inf)


def ssd_chunked_scan(xs, log_a, b_mat, c_mat):
    bsz, seq = xs.shape[:2]
    nc = seq // M2_CHUNK
    xs = xs.reshape(bsz, nc, M2_CHUNK, M2_GROUPS, M2_HEADS_PER_GROUP, M2_HEAD_DIM)
    b_mat = b_mat.reshape(bsz, nc, M2_CHUNK, M2_GROUPS, M2_STATE)
    c_mat = c_mat.reshape(bsz, nc, M2_CHUNK, M2_GROUPS, M2_STATE)
    log_a = log_a.reshape(bsz, nc, M2_CHUNK, M2_GROUPS, M2_HEADS_PER_GROUP).transpose(0, 3, 4, 1, 2)
    a_cum = jnp.cumsum(log_a, axis=-1)
    decay_in = jnp.exp(segsum(log_a))
    cb = jnp.einsum('bclgn,bcsgn->bgcls', c_mat, b_mat)
    y_diag = jnp.einsum('bghcls,bcsghp->bclghp', cb[:, :, None] * decay_in, xs)
    decay_to_end = jnp.exp(a_cum[..., -1:] - a_cum)
    states = jnp.einsum('bclgn,bghcl,bclghp->bcghpn', b_mat, decay_to_end, xs)
    states = jnp.concatenate([jnp.zeros_like(states[:, :1]), states], axis=1)
    chunk_decay = jnp.exp(segsum(jnp.pad(a_cum[..., -1], ((0, 0), (0, 0), (0, 0), (1, 0)))))
    states = jnp.einsum('bghzc,bcghpn->bzghpn', chunk_decay, states)[:, :-1]
    y_off = jnp.einsum('bclgn,bcghpn,bghcl->bclghp', c_mat, states, jnp.exp(a_cum))
    return (y_diag + y_off).reshape(bsz, seq, M2_HEADS, M2_HEAD_DIM)


def mamba2_mix(p, conv_w, conv_b, dt_bias, a_log, d_skip, norm_g):
    bsz, seq, _ = p.shape
    f32 = jnp.float32
    p = p.astype(f32)
    z, xbc, dt = split_sizes(p, [M2_WIDTH, M2_CONV_DIM, M2_HEADS])
    xbc = jax.nn.silu(causal_dwconv(xbc, conv_w.astype(f32), conv_b.astype(f32)))
    xs, b_mat, c_mat = split_sizes(xbc, [M2_WIDTH, M2_GROUPS * M2_STATE, M2_GROUPS * M2_STATE])
    xs = xs.reshape(bsz, seq, M2_HEADS, M2_HEAD_DIM)
    dt = jax.nn.softplus(dt + dt_bias)
    log_a = dt * -jnp.exp(a_log.astype(f32))
    grp = (bsz, seq, M2_GROUPS, M2_HEADS_PER_GROUP)
    y = ssd_chunked_scan((xs * dt[..., None]).reshape(grp + (M2_HEAD_DIM,)),
                         log_a.reshape(grp),
                         b_mat.reshape(bsz, seq, M2_GROUPS, M2_STATE),
                         c_mat.reshape(bsz, seq, M2_GROUPS, M2_STATE))
    y = y + d_skip[:, None] * xs
    y = y.reshape(bsz, seq, M2_WIDTH) * jax.nn.silu(z)
    yg = y.reshape(bsz, seq, M2_GROUPS, M2_WIDTH // M2_GROUPS)
    yg = yg * lax.rsqrt(jnp.mean(yg * yg, axis=-1, keepdims=True) + NORM_EPS)
    return yg.reshape(bsz, seq, M2_WIDTH) * norm_g


def setup_inputs(seed: int = 0) -> dict:
    key = jax.random.key(seed)
    ks = iter(jax.random.split(key, 40))
    nrm = lambda shape, scale: jax.random.normal(next(ks), shape, jnp.float32) * scale
    unif = lambda shape, lo, hi: jax.random.uniform(next(ks), shape, jnp.float32, lo, hi)
    L, D = DEPTH, D_MODEL
    dt0 = jnp.exp(unif((L, M2_HEADS), math.log(1e-3), math.log(1e-1)))
    return {
        'x': nrm((BATCH, SEQ, D), 1.0),
        'c': nrm((BATCH, D), 1.0),
        'ada_w': nrm((L, D, N_MOD * D), D ** -0.5),
        'ada_b': nrm((L, N_MOD * D), 0.02),
        'norm1_g': 1.0 + nrm((L, D), 0.02),
        'norm2_g': 1.0 + nrm((L, D), 0.02),
        'w_in': nrm((L, D, N_IN), D ** -0.5),
        'rw_mu': unif((L, RW_IN), 0.0, 1.0),
        'rw_w0': unif((L, RW_WIDTH), -7.0, -2.0),
        'rw_w2': nrm((L, RW_DECAY_LORA, RW_WIDTH), 0.5 * RW_DECAY_LORA ** -0.5),
        'rw_a0': nrm((L, RW_WIDTH), 0.1),
        'rw_a2': nrm((L, RW_AAA_LORA, RW_WIDTH), 0.5 * RW_AAA_LORA ** -0.5),
        'rw_g2': nrm((L, RW_GATE_LORA, RW_WIDTH), RW_GATE_LORA ** -0.5),
        'rw_k_k': 0.85 + nrm((L, RW_WIDTH), 0.02),
        'rw_k_a': 1.0 + nrm((L, RW_WIDTH), 0.02),
        'rw_r_k': nrm((L, RW_HEADS, RW_HEAD_DIM), 0.1),
        'rw_ln_g': 1.0 + nrm((L, RW_WIDTH), 0.02),
        'rw_ln_b': nrm((L, RW_WIDTH), 0.02),
        'rw_wo': nrm((L, RW_WIDTH, D), RW_WIDTH ** -0.5),
        'sb_wo': nrm((L, SB_WIDTH, D), SB_WIDTH ** -0.5),
        'm2_conv_w': nrm((L, M2_CONV, M2_CONV_DIM), M2_CONV ** -0.5),
        'm2_conv_b': nrm((L, M2_CONV_DIM), 0.02),
        'm2_dt_bias': dt0 + jnp.log(-jnp.expm1(-dt0)),
        'm2_a_log': jnp.log(unif((L, M2_HEADS), 1.0, 16.0)),
        'm2_d': 1.0 + nrm((L, M2_HEADS), 0.02),
        'm2_norm_g': 1.0 + nrm((L, M2_WIDTH), 0.02),
        'm2_wo': nrm((L, M2_WIDTH, D), M2_WIDTH ** -0.5),
        'w_out': nrm((L, D, D), D ** -0.5),
        'ffn_w_up': nrm((L, D, 2 * D_FF), D ** -0.5),
        'ffn_conv_w': nrm((L, FFN_CONV, 2 * D_FF), FFN_CONV ** -0.5),
        'ffn_conv_b': nrm((L, 2 * D_FF), 0.02),
        'ffn_w_down': nrm((L, D_FF, D), D_FF ** -0.5),
        'final_norm_g': 1.0 + nrm((D,), 0.02),
    }


def reference(x, c, ada_w, ada_b, norm1_g, norm2_g, w_in, rw_mu, rw_w0, rw_w2, rw_a0,
              rw_a2, rw_g2, rw_k_k, rw_k_a, rw_r_k, rw_ln_g, rw_ln_b, rw_wo, sb_wo,
              m2_conv_w, m2_conv_b, m2_dt_bias, m2_a_log, m2_d, m2_norm_g, m2_wo, w_out,
              ffn_w_up, ffn_conv_w, ffn_conv_b, ffn_w_down, final_norm_g):
    bsz, seq, _ = x.shape
    c_act = jax.nn.silu(c)
    for l in range(DEPTH):
        mod = (c_act @ ada_w[l] + ada_b[l])[:, None, :]
        shift1, scale1, gate1, shift2, scale2, gate2 = jnp.split(mod, N_MOD, axis=-1)

        h = rms_norm(x, norm1_g[l]) * (1.0 + scale1) + shift1
        p_rw, p_sb, p_m2, p_gate = split_sizes(h @ w_in[l], [RW_IN, SB_IN, M2_IN, GATE_IN])
        y_rw = rwkv7_time_mix(p_rw, rw_mu[l], rw_w0[l], rw_w2[l], rw_a0[l], rw_a2[l],
                              rw_g2[l], rw_k_k[l], rw_k_a[l], rw_r_k[l], rw_ln_g[l],
                              rw_ln_b[l]).astype(h.dtype) @ rw_wo[l]
        q, k, v = [t.reshape(bsz, seq, SB_HEADS, SB_HEAD_DIM) for t in jnp.split(p_sb, 3, axis=-1)]
        y_sb = stick_breaking_attention(q, k, v) @ sb_wo[l]
        y_m2 = mamba2_mix(p_m2, m2_conv_w[l], m2_conv_b[l], m2_dt_bias[l], m2_a_log[l],
                          m2_d[l], m2_norm_g[l]).astype(h.dtype) @ m2_wo[l]
        gates = jax.nn.sigmoid(p_gate).reshape(bsz, seq, N_BRANCH, D_MODEL)
        merged = gates[:, :, 0] * y_rw + gates[:, :, 1] * y_sb + gates[:, :, 2] * y_m2
        x = x + gate1 * (merged @ w_out[l])

        h = rms_norm(x, norm2_g[l]) * (1.0 + scale2) + shift2
        u = causal_dwconv(h @ ffn_w_up[l], ffn_conv_w[l], ffn_conv_b[l])
        u_gate, u_val = jnp.split(u, 2, axis=-1)
        x = x + gate2 * ((jax.nn.silu(u_gate) * u_val) @ ffn_w_down[l])
    return rms_norm(x, final_norm_g)
```

```python
import numpy as np
import concourse.bass as bass
import concourse.mybir as mybir

F32 = mybir.dt.float32
BF16 = mybir.dt.bfloat16
F32R = mybir.dt.float32r
AF = mybir.ActivationFunctionType
ALU = mybir.AluOpType
AX = mybir.AxisListType

SAME_ENG_SYNC = True
NDSEM = 8


class KB:
    def __init__(self):
        self.nc = bass.Bass("TRN2", target_bir_lowering=False)
        nc = self.nc
        self.eng = {"pe": nc.tensor, "dve": nc.vector, "act": nc.scalar, "pool": nc.gpsimd, "sp": nc.sync}
        self._ctx = []
        self.sem = {}
        self.cnt = {}
        for e in self.eng:
            self.sem[e] = self._enter(nc.semaphore("c_" + e))
            self.cnt[e] = 0
        self.dsem = {}
        self.dcnt = {}
        for q in ("sp", "act", "pool"):
            self.dsem[q] = [self._enter(nc.semaphore("d_%s%d" % (q, i))) for i in range(NDSEM)]
            self.dcnt[q] = 0
        self.semname = {}
        for e in self.eng:
            self.semname[id(self.sem[e])] = e
        self.seen = {e: {} for e in self.eng}
        self.acc = {}
        self.semobj = {}
        for e in self.eng:
            self.semobj[e] = self.sem[e]
        for q in self.dsem:
            for i, s in enumerate(self.dsem[q]):
                self.semobj["d_%s%d" % (q, i)] = s
        self.n_inst = 0
        self.n_wait = 0
        self.K = {}
        self.out_events = []

    def _enter(self, cm):
        v = cm.__enter__()
        self._ctx.append(cm)
        return v

    def sb(self, name, shape, dt=F32):
        return self._enter(self.nc.sbuf_tensor(name, list(shape), dt))

    def ps(self, name, shape, dt=F32):
        return self._enter(self.nc.psum_tensor(name, list(shape), dt))

    def dram(self, name, shape, dt=F32, kind="Internal"):
        return self.nc.dram_tensor(name, list(shape), dt, kind=kind).ap()

    @staticmethod
    def region(ap):
        t = ap.tensor
        name = t.name
        space = str(ap.space)
        pat = ap.ap
        off = ap.offset
        ds = 2 if t.dtype == BF16 else 4
        lo = 0
        hi = 0
        if "DRAM" in space:
            for st, n in pat:
                if st >= 0:
                    hi += st * (n - 1)
                else:
                    lo += st * (n - 1)
            return name, 0, 1, (off + lo) * ds, (off + hi + 1) * ds
        row = 1
        for d in t.shape[1:]:
            row *= d
        p0 = off // row
        pst, pn = pat[0]
        pn_eff = pn if pst != 0 else 1
        foff = off - p0 * row
        for st, n in pat[1:]:
            if st >= 0:
                hi += st * (n - 1)
            else:
                lo += st * (n - 1)
        assert 0 <= foff + lo and foff + hi < row, (name, off, pat, p0, row)
        if "PSUM" in space:
            b0 = ((foff + lo) * ds) // 2048
            b1 = ((foff + hi + 1) * ds - 1) // 2048 + 1
            return name, 0, 128, b0 * 2048, b1 * 2048
        return name, p0, p0 + pn_eff, (foff + lo) * ds, (foff + hi + 1) * ds

    def _deps(self, reads, writes, e=None):
        ev = {}

        def add(k, v):
            if ev.get(k, 0) < v:
                ev[k] = v

        regs = []
        for ap in reads:
            regs.append((self.region(ap), False, "PSUM" in str(ap.space)))
        for ap in writes:
            regs.append((self.region(ap), True, "PSUM" in str(ap.space)))
        for (name, p0, p1, f0, f1), isw, isps in regs:
            for a in self.acc.get(name, ()):
                if a[1] <= p0 or a[0] >= p1 or a[3] <= f0 or a[2] >= f1:
                    continue
                if a[4] == e:
                    if a[6] and not isw:
                        add(a[4], a[5])
                elif isw or a[6] or isps:
                    add(a[4], a[5])
        return ev, regs

    def _record(self, regs, semkey, val):
        for (name, p0, p1, f0, f1), isw, isps in regs:
            lst = self.acc.setdefault(name, [])
            if isw or isps:
                keep = []
                for a in lst:
                    cov = a[0] >= p0 and a[1] <= p1 and a[2] >= f0 and a[3] <= f1
                    if cov and (isw or not a[6] or a[4] == semkey):
                        continue
                    keep.append(a)
                lst[:] = keep
            else:
                lst[:] = [a for a in lst if not (a[4] == semkey and not a[6] and a[0] >= p0 and a[1] <= p1 and a[2] >= f0 and a[3] <= f1)]
            lst.append([p0, p1, f0, f1, semkey, val, isw])

    def _waits(self, e, ev):
        engine = self.eng[e]
        seen = self.seen[e]
        for k, v in sorted(ev.items(), key=lambda kv: -kv[1]):
            if k == e and (not SAME_ENG_SYNC or e == 'pe'):
                continue
            if seen.get(k, 0) >= v:
                continue
            engine.wait_ge(self.semobj[k], v)
            self.n_wait += 1
            seen[k] = v
            snap = self.K.get((k, v))
            if snap:
                for k2, v2 in snap.items():
                    if seen.get(k2, 0) < v2:
                        seen[k2] = v2

    def op(self, e, fn, reads, writes):
        ev, regs = self._deps(reads, writes, e)
        self._waits(e, ev)
        ins = fn(self.eng[e])
        self.cnt[e] += 1
        ins.then_inc(self.sem[e], 1)
        self.K[(e, self.cnt[e])] = dict(self.seen[e])
        self._record(regs, e, self.cnt[e])
        self.n_inst += 1
        return ins

    def dma(self, q, out, in_, is_output=False, **kw):
        ev, regs = self._deps([in_], [out])
        i = self.dcnt[q]
        slot = i % NDSEM
        key = "d_%s%d" % (q, slot)
        if i >= NDSEM:
            ev_prev = {key: 16 * (i // NDSEM)}
            self._waits(q, ev_prev)
        self._waits(q, ev)
        ins = self.eng[q].dma_start(out=out, in_=in_, **kw)
        val = 16 * (i // NDSEM + 1)
        ins.then_inc(self.semobj[key], 16)
        self.K[(key, val)] = dict(self.seen[q])
        self.dcnt[q] += 1
        self._record(regs, key, val)
        self.n_inst += 1
        if is_output:
            self.out_events.append((key, val))
        return ins

    def finish(self):
        ev = {}
        for k, v in self.out_events:
            if ev.get(k, 0) < v:
                ev[k] = v
        for q in self.dsem:
            i = self.dcnt[q]
            for s in range(min(i, NDSEM)):
                n_uses = (i - 1 - s) // NDSEM + 1
                k = "d_%s%d" % (q, s)
                if ev.get(k, 0) < 16 * n_uses:
                    ev[k] = 16 * n_uses
        for e in self.eng:
            if e != "sp" and self.cnt[e] > 0:
                ev[e] = self.cnt[e]
        self._waits("sp", ev)

    def mm(self, out, lhsT, rhs, start=True, stop=True, **kw):
        return self.op("pe", lambda en: en.matmul(out, lhsT, rhs, start=start, stop=stop, **kw), [lhsT, rhs] + ([] if start else [out]), [out])

    def transpose(self, out, in_, ident):
        return self.op("pe", lambda en: en.transpose(out, in_, ident), [in_, ident], [out])

    def act(self, out, in_, func, bias=None, scale=None, accum_out=None, e="act"):
        kw = {}
        reads = [in_]
        writes = [out]
        if bias is not None:
            kw["bias"] = bias
            if not isinstance(bias, (int, float)):
                reads.append(bias)
        if scale is not None:
            kw["scale"] = scale
            if not isinstance(scale, (int, float)):
                reads.append(scale)
        if accum_out is not None:
            kw["accum_out"] = accum_out
            writes.append(accum_out)
        return self.op("act", lambda en: en.activation(out, in_, func, **kw), reads, writes)

    def tt(self, out, a, b, op, e="dve"):
        return self.op(e, lambda en: en.tensor_tensor(out, a, b, op), [a, b], [out])

    def ts(self, out, a, s1, s2=None, op0=ALU.mult, op1=None, e="dve", accum_out=None):
        reads = [a]
        if not isinstance(s1, (int, float)):
            reads.append(s1)
        if s2 is not None and not isinstance(s2, (int, float)):
            reads.append(s2)
        kw = {}
        writes = [out]
        if op1 is not None:
            kw["op1"] = op1
        if accum_out is not None:
            kw["accum_out"] = accum_out
            writes.append(accum_out)
        return self.op(e, lambda en: en.tensor_scalar(out, a, s1, s2, op0, **kw), reads, writes)

    def stt(self, out, a, s, b, op0, op1, accum_out=None):
        reads = [a, b]
        if not isinstance(s, (int, float)):
            reads.append(s)
        kw = {}
        writes = [out]
        if accum_out is not None:
            kw["accum_out"] = accum_out
            writes.append(accum_out)
        return self.op("dve", lambda en: en.scalar_tensor_tensor(out, a, s, b, op0, op1, **kw), reads, writes)

    def scan(self, out, d0, d1, init, op0, op1):
        reads = [d0, d1]
        if not isinstance(init, (int, float)):
            reads.append(init)
        return self.op("dve", lambda en: en.tensor_tensor_scan(out, d0, d1, init, op0, op1), reads, [out])

    def copy(self, out, in_, e="dve"):
        if e == "act":
            return self.op("act", lambda en: en.copy(out, in_), [in_], [out])
        return self.op(e, lambda en: en.tensor_copy(out, in_), [in_], [out])

    def memset(self, out, val, e="dve"):
        return self.op(e, lambda en: en.memset(out, val), [], [out])

    def reduce(self, out, in_, op=ALU.add, axis=AX.X):
        return self.op("dve", lambda en: en.tensor_reduce(out, in_, axis, op), [in_], [out])

    def recip(self, out, in_):
        return self.op("dve", lambda en: en.reciprocal(out, in_), [in_], [out])


S = 2048
KC = 8
NG = 4
NT = 16
C_M2 = 3328
EPS = 1e-6


def bcl(ap, m):
    pat = [list(p) for p in ap.ap]
    return bass.AP(ap.tensor, ap.offset, pat + [[0, m]])


def bcm(ap, m):
    pat = [list(p) for p in ap.ap]
    return bass.AP(ap.tensor, ap.offset, [pat[0], [0, m]] + pat[1:])


def phase_m2(env):
    k = env["k"]; I = env["I"]; ar = env["ar"]; hT = env["hT"]; bank = env["bank"]; l = env["l"]
    load_w = env["load_w"]; load_cols = env["load_cols"]; DBG = env["DBG"]; evac = env["evac"]
    ident = env["ident"]; m_siu = env["m_siu"]; m_sl = env["m_sl"]; ones = env["ones"]; ybr_d = env["ybr_d"]
    bcast = env["bcast"]; epsc = env["epsc"]
    W = I["w_in"][l]
    ar.reset()
    BT = ar.alloc([2, S], BF16)
    CT = ar.alloc([2, S], BF16)
    xs_tok = ar.alloc([NT, 1024], BF16)
    B_tok = ar.alloc([NT, 256], BF16)
    wz = ar.alloc([KC, 1024], BF16)
    u = ar.alloc([S + 3])
    t1 = ar.alloc([S])
    wts = [ar.alloc([KC, 128], BF16), ar.alloc([KC, 128], BF16)]
    wdt = ar.alloc([KC, 16], BF16)
    cw = ar.alloc([48])
    cb = ar.alloc([12])
    dtb_bc = ar.alloc([16])
    A_bc = ar.alloc([16])
    D_bc = ar.alloc([16])
    ng_bc = ar.alloc([1024])
    sel127 = ar.alloc([128])
    dt_tok = ar.alloc([NT, 16])
    la_tok = ar.alloc([NT, 16])
    acum = ar.alloc([NT, 16])
    aend = ar.alloc([NT, 16])
    dec = ar.alloc([NT, 16])
    ea = ar.alloc([NT, 16])
    dte = ar.alloc([NT, 16])
    rhsD = ar.alloc([16, 128])
    E = ar.alloc([16, 128])
    G = ar.alloc([16, 128], BF16)
    CBm = ar.alloc([2, 128])
    xdt = ar.alloc([1024], BF16)
    xdte = ar.alloc([1024], BF16)
    y = ar.alloc([1024])
    y2 = ar.alloc([1024])
    zs = ar.alloc([1024])
    S32 = ar.alloc([2, 512])
    Sbf = ar.alloc([2, 512], BF16)
    ytT = ar.alloc([8, 128], BF16)
    ssq = ar.alloc([4])
    junk = ar.alloc([512])

    load_cols(cw, I["m2_conv_w"][l].rearrange("a b -> (a b)"), 48)
    load_cols(cb, I["m2_conv_b"][l], 12)
    bcast(dtb_bc, I["m2_dt_bias"][l], 16)
    bcast(A_bc, I["m2_a_log"][l], 16)
    bcast(D_bc, I["m2_d"][l], 16)
    bcast(ng_bc, I["m2_norm_g"][l], 1024)
    k.act(A_bc, A_bc, AF.Exp)
    k.ts(A_bc, A_bc, -1.0, None, ALU.mult)
    k.op("pool", lambda en: en.affine_select(sel127, ones[:], [[0, 128]], ALU.is_equal, 0.0, base=-127,
                                               channel_multiplier=1), [ones[:]], [sel127])
    load_w(wz, W[:, C_M2:C_M2 + 1024])
    load_w(wdt, W[:, C_M2 + 2560:C_M2 + 2576])
    k.memset(u[:, 0:3], 0.0)

    for cc in range(12):
        wt = wts[cc % 2]
        c0 = C_M2 + 1024 + cc * 128
        load_w(wt, W[:, c0:c0 + 128])
        for n in range(NG):
            pb = bank()
            for kc in range(KC):
                k.mm(pb, wt[:, kc, :], hT[:, kc, n * 512:(n + 1) * 512], start=(kc == 0), stop=(kc == KC - 1))
            k.copy(u[:, 3 + n * 512:3 + (n + 1) * 512], pb, e="act")
            k.act(t1[:, n * 512:(n + 1) * 512], pb, AF.Identity, bias=cb[:, cc:cc + 1], scale=cw[:, 36 + cc:37 + cc])
        k.stt(t1, u[:, 2:S + 2], cw[:, 24 + cc:25 + cc], t1, ALU.mult, ALU.add)
        k.stt(t1, u[:, 1:S + 1], cw[:, 12 + cc:13 + cc], t1, ALU.mult, ALU.add)
        k.stt(t1, u[:, 0:S], cw[:, cc:cc + 1], t1, ALU.mult, ALU.add)
        if cc < 10:
            k.act(t1, t1, AF.Silu)
            if cc >= 8:
                k.copy(BT[:, cc - 8, :], t1, e="pool")
            for tq in range(4):
                pb = bank()
                for j in range(4):
                    t = tq * 4 + j
                    k.transpose(pb[:, j * 128:(j + 1) * 128], t1[:, t * 128:(t + 1) * 128], ident[:])
                src = pb.rearrange("p (a b) -> p a b", a=4)
                if cc < 8:
                    evac(xs_tok[:, tq * 4:tq * 4 + 4, cc * 128:(cc + 1) * 128], src)
                else:
                    evac(B_tok[:, tq * 4:tq * 4 + 4, (cc - 8) * 128:(cc - 7) * 128], src)
        else:
            k.act(CT[:, cc - 10, :], t1, AF.Silu)

    pb = bank()
    for t in range(NT):
        for kc in range(KC):
            k.mm(pb[:, t * 16:(t + 1) * 16], hT[:, kc, t * 128:(t + 1) * 128], wdt[:, kc, :],
                 start=(kc == 0), stop=(kc == KC - 1))
    dt2 = dt_tok.rearrange("p a b -> p (a b)")
    la2 = la_tok.rearrange("p a b -> p (a b)")
    k.tt(dt_tok, pb[:, 0:256].rearrange("p (a b) -> p a b", a=NT), bcm(dtb_bc, NT), ALU.add)
    k.act(dt2, dt2, AF.Exp)
    k.act(dt2, dt2, AF.Ln, bias=1.0)
    k.tt(la_tok, dt_tok, bcm(A_bc, NT), ALU.mult)
    pb = bank()
    k.mm(pb[:, 0:256], m_siu[:], la2)
    ac2 = acum.rearrange("p a b -> p (a b)")
    k.copy(ac2, pb[:, 0:256])
    pb = bank()
    k.mm(pb[:, 0:256], sel127, ac2)
    ae2 = aend.rearrange("p a b -> p (a b)")
    k.copy(ae2, pb[:, 0:256])
    k.act(dec.rearrange("p a b -> p (a b)"), ae2, AF.Exp)
    k.act(ea.rearrange("p a b -> p (a b)"), ac2, AF.Exp)
    k.tt(ae2, ae2, ac2, ALU.subtract)
    k.act(dte.rearrange("p a b -> p (a b)"), ae2, AF.Exp)

    psum = env["psum"]

    def PB(i):
        return psum[:, i, :]

    def xs3_(t):
        return xs_tok[:, t, :].rearrange("p (h d) -> p h d", h=16)

    def prologue(t):
        ts_ = slice(t * 128, (t + 1) * 128)
        pb = PB(0)
        for g in range(2):
            k.mm(pb[:, g * 128:(g + 1) * 128], BT[:, g, ts_], CT[:, g, ts_])
        for g in range(2):
            k.tt(CBm[:, g, :], pb[:, g * 128:(g + 1) * 128], m_siu[:], ALU.mult)
        k.tt(rhsD, bcm(m_siu[:], 16), bcl(la_tok[:, t, :], 128), ALU.mult)
        for b4 in range(4):
            pb = PB(1 + (b4 % 2))
            k.mm(pb, m_sl[:], rhsD[:, b4 * 4:b4 * 4 + 4, :].rearrange("p a b -> p (a b)"))
            k.act(E[:, b4 * 4:b4 * 4 + 4, :].rearrange("p a b -> p (a b)"), pb, AF.Exp)
        for g in range(2):
            k.tt(G[:, g * 8:(g + 1) * 8, :], E[:, g * 8:(g + 1) * 8, :], bcm(CBm[:, g, :], 8), ALU.mult)
        k.tt(xdt.rearrange("p (h d) -> p h d", h=16), xs3_(t), bcl(dt_tok[:, t, :], 64), ALU.mult)
        if t < NT - 1:
            k.tt(xdte.rearrange("p (h d) -> p h d", h=16), xdt.rearrange("p (h d) -> p h d", h=16),
                 bcl(dte[:, t, :], 64), ALU.mult, e="pool")

    prologue(0)
    for t in range(NT):
        ts_ = slice(t * 128, (t + 1) * 128)
        py = [PB(3), PB(4)]
        for h in range(16):
            k.mm(py[h // 8][:, (h % 8) * 64:(h % 8 + 1) * 64], G[:, h, :], xdt[:, h * 64:(h + 1) * 64])
        po = [PB(5), PB(6)]
        if t > 0:
            for g in range(2):
                k.mm(po[g], CT[:, g, ts_], Sbf[:, g, :])
        for half in range(2):
            pz = PB(7) if half == 0 else PB(1)
            for kc in range(KC):
                k.mm(pz, hT[:, kc, ts_], wz[:, kc, half * 512:(half + 1) * 512], start=(kc == 0), stop=(kc == KC - 1))
            k.act(zs[:, half * 512:(half + 1) * 512], pz, AF.Silu)
        if t > 0:
            for g in range(2):
                k.tt(y2[:, g * 512:(g + 1) * 512].rearrange("p (h d) -> p h d", h=8),
                     po[g].rearrange("p (h d) -> p h d", h=8), bcl(ea[:, t, g * 8:(g + 1) * 8], 64), ALU.mult)
                k.tt(y[:, g * 512:(g + 1) * 512], y2[:, g * 512:(g + 1) * 512], py[g], ALU.add)
        else:
            for g in range(2):
                k.copy(y[:, g * 512:(g + 1) * 512], py[g])
        pst = [PB(3), PB(4)]
        if t < NT - 1:
            for g in range(2):
                k.mm(pst[g], B_tok[:, t, g * 128:(g + 1) * 128], xdte[:, g * 512:(g + 1) * 512])
        if t + 1 < NT:
            prologue(t + 1)
        if t < NT - 1:
            for g in range(2):
                if t == 0:
                    k.copy(S32[:, g, :], pst[g])
                else:
                    k.tt(S32[:, g, :].rearrange("p (h d) -> p h d", h=8),
                         S32[:, g, :].rearrange("p (h d) -> p h d", h=8),
                         bcl(dec[:, t, g * 8:(g + 1) * 8], 64), ALU.mult)
                    k.tt(S32[:, g, :], S32[:, g, :], pst[g], ALU.add)
                k.copy(Sbf[:, g, :], S32[:, g, :], e="act")
        k.tt(y2.rearrange("p (h d) -> p h d", h=16), xs3_(t), bcl(D_bc, 64), ALU.mult, e="pool")
        k.tt(y, y, y2, ALU.add)
        k.tt(y, y, zs, ALU.mult)
        for g in range(2):
            k.act(junk, y[:, g * 512:(g + 1) * 512], AF.Square, accum_out=ssq[:, g:g + 1])
        k.act(ssq[:, 2:4], ssq[:, 0:2], AF.Sqrt, bias=epsc[:], scale=1.0 / 512)
        k.recip(ssq[:, 2:4], ssq[:, 2:4])
        for g in range(2):
            k.stt(y[:, g * 512:(g + 1) * 512], y[:, g * 512:(g + 1) * 512], ssq[:, 2 + g:3 + g],
                  ng_bc[:, g * 512:(g + 1) * 512], ALU.mult, ALU.mult)
        if t == 3:
            DBG("ym2_%d" % l, y, [128, 1024])
        for half in range(2):
            pb = PB(5 + half)
            for j in range(4):
                c = half * 4 + j
                k.transpose(pb[:, j * 128:(j + 1) * 128], y[:, c * 128:(c + 1) * 128], ident[:])
            evac(ytT[:, half * 4:half * 4 + 4, :], pb.rearrange("p (a b) -> p a b", a=4))
        k.dma("sp", ybr_d["m2"][:, :, ts_].rearrange("c p s -> p c s"), ytT)


S = 2048
KC = 8
NG = 4
NT = 16
C_SB = 1792


def phase_sb(env):
    k = env["k"]; I = env["I"]; ar = env["ar"]; hT = env["hT"]; bank = env["bank"]; l = env["l"]
    load_w = env["load_w"]; DBG = env["DBG"]; evac = env["evac"]
    identb = env["identb"]; m_sl = env["m_sl"]; ybr_d = env["ybr_d"]
    W = I["w_in"][l]
    ar.reset()
    qT = ar.alloc([4, S], BF16)
    kT = ar.alloc([4, S], BF16)
    v_tok = ar.alloc([NT, 512], BF16)
    ysT = ar.alloc([4, S], BF16)
    wraw = [ar.alloc([S]), ar.alloc([S])]
    wts = [w_.bitcast(BF16).rearrange("p (a b) -> p a b", a=KC) for w_ in wraw]
    SBUFS = []
    for s_ in range(2):
        SBUFS.append(dict(Eb=[ar.alloc([S]), wraw[s_]], R=ar.alloc([S]), Wd=ar.alloc([S]), att=ar.alloc([S], BF16),
                          attT=ar.alloc([NT, 128], BF16)))
    Eb = SBUFS[0]["Eb"][0]
    onesS = ar.alloc([S])
    k.memset(onesS, 1.0)
    for which, dst in ((0, qT), (1, kT)):
        wt = wts[which]
        load_w(wt, W[:, C_SB + which * 512:C_SB + (which + 1) * 512])
        for c in range(4):
            for n in range(NG):
                pb = bank()
                for kc in range(KC):
                    k.mm(pb, wt[:, kc, c * 128:(c + 1) * 128], hT[:, kc, n * 512:(n + 1) * 512],
                         start=(kc == 0), stop=(kc == KC - 1))
                evac(dst[:, c, n * 512:(n + 1) * 512], pb)
    wt = wts[0]
    load_w(wt, W[:, C_SB + 1024:C_SB + 1536])
    for t in range(NT):
        pb = bank()
        for kc in range(KC):
            k.mm(pb, hT[:, kc, t * 128:(t + 1) * 128], wt[:, kc, :], start=(kc == 0), stop=(kc == KC - 1))
        evac(v_tok[:, t, :], pb)

    def unit(h, i, B, par):
        Eb = B["Eb"][par]; R = B["R"]; L = B["Wd"]; att = B["att"]; attT = B["attT"]
        c = h // 2
        po = (h % 2) * 64
        N = 128 * (i + 1)
        nb = (N + 511) // 512
        for b in range(nb):
            cols = min(512, N - b * 512)
            pb = bank()
            k.mm(pb[:, 0:cols], qT[po:po + 64, c, i * 128:(i + 1) * 128], kT[po:po + 64, c, b * 512:b * 512 + cols])
            k.act(Eb[:, b * 512:b * 512 + cols], pb[:, 0:cols], AF.Exp, scale=0.125)
        yield
        k.act(L[:, 0:N], Eb[:, 0:N], AF.Ln, bias=1.0)
        k.tt(L[:, N - 128:N], L[:, N - 128:N], m_sl[:], ALU.mult, e="pool")
        yield
        k.scan(R[:, N - 1::-1] if N > 128 else R[:, 127::-1], onesS[:, 0:N],
               L[:, N - 1::-1] if N > 128 else L[:, 127::-1], 0.0, ALU.mult, ALU.add)
        yield
        k.act(R[:, 0:N], R[:, 0:N], AF.Exp, scale=-1.0)
        yield
        k.tt(att[:, 0:N], Eb[:, 0:N], R[:, 0:N], ALU.mult)
        k.tt(att[:, N - 128:N], att[:, N - 128:N], m_sl[:], ALU.mult, e="pool")
        yield
        for g4 in range((i + 4) // 4):
            nblk = min(4, i + 1 - g4 * 4)
            pb = bank().bitcast(BF16)
            for j in range(nblk):
                kb = g4 * 4 + j
                k.transpose(pb[:, j * 128:(j + 1) * 128], att[:, kb * 128:(kb + 1) * 128], identb[:])
            evac(attT[:, g4 * 4:g4 * 4 + nblk, :], pb[:, 0:nblk * 128].rearrange("p (a b) -> p a b", a=nblk))
            if g4 % 2 == 1:
                yield
        yield
        po_b = bank()
        for kb in range(i + 1):
            k.mm(po_b[po:po + 64, 0:128], v_tok[:, kb, h * 64:(h + 1) * 64], attT[:, kb, :],
                 start=(kb == 0), stop=(kb == i))
        evac(ysT[po:po + 64, c, i * 128:(i + 1) * 128], po_b[po:po + 64, 0:128])

    pending = [(2 * c2, 2 * c2 + 1, i) for c2 in range(4) for i in range(NT)]
    active = []
    step = 0
    npair = 0
    late = []
    while pending or active or late:
        if late:
            active.append(late.pop(0))
        if pending and step % 3 == 0 and len(active) < 6:
            hA, hB, i_ = pending.pop(0)
            active.append(unit(hA, i_, SBUFS[0], npair % 2))
            late.append(unit(hB, i_, SBUFS[1], npair % 2))
            npair += 1
        for g_ in list(active):
            try:
                next(g_)
            except StopIteration:
                active.remove(g_)
        step += 1
    if env["dbg"] and ("ysb_%d" % l) in env["dbg"]:
        k.copy(Eb, ysT[:, 0, :])
        DBG("ysb_%d" % l, Eb, [128, S])
    for c in range(4):
        k.dma("sp" if c % 2 == 0 else "act", ybr_d["sb"][c], ysT[:, c, :])


S = 2048
KC = 8
NG = 4
NT = 16
GN_EPS = 64e-5
NEG_E05 = -0.6065306597126334


def phase_rw(env):
    k = env["k"]; I = env["I"]; ar = env["ar"]; hT = env["hT"]; bank = env["bank"]; l = env["l"]
    load_w = env["load_w"]; load_cols = env["load_cols"]; DBG = env["DBG"]; evac = env["evac"]
    ident = env["ident"]; identb = env["identb"]; m_su = env["m_su"]; m_siu = env["m_siu"]; m_sl = env["m_sl"]
    blk2 = env["blk2"]; ybr_d = env["ybr_d"]; bcast = env["bcast"]
    W = I["w_in"][l]
    ar.reset()
    F = [ar.alloc([S]) for _ in range(7)]
    F.append(ar.alloc([S + 4]))
    u = F[7]
    Rt = ar.alloc([S], BF16); At = ar.alloc([S], BF16); Bt = ar.alloc([S], BF16); Kt = ar.alloc([S], BF16)
    Bhf = ar.alloc([S], BF16); Khf = ar.alloc([S], BF16); Vb = ar.alloc([S], BF16)
    gT = ar.alloc([S], BF16); bonus = ar.alloc([S], BF16)
    Atok = ar.alloc([NT, 128], BF16); Bhtok = ar.alloc([NT, 128], BF16)
    Khtok = ar.alloc([NT, 128], BF16); Vtok = ar.alloc([NT, 128], BF16)
    walo = ar.alloc([S], BF16); gsig = ar.alloc([S], BF16)
    cmask = ar.alloc([S], BF16)
    yrwT = ar.alloc([S], BF16)
    w2a2 = ar.alloc([512], BF16); g2w = ar.alloc([512], BF16)
    lng = ar.alloc([512]); lnb = ar.alloc([512])
    wts = [ar.alloc([KC, 128], BF16), ar.alloc([KC, 128], BF16)]
    pc = ar.alloc([40])
    wC = ar.alloc([NT])
    msu32 = ar.alloc([128]); msl32 = ar.alloc([128]); msl64o = ar.alloc([128]); msl128o = ar.alloc([128])
    D32 = ar.alloc([128]); D64 = ar.alloc([128]); dtmp = ar.alloc([128])
    ones_ = env["ones"]
    for Dm, bs in ((D32, 32), (D64, 64)):
        for gb in range(128 // bs):
            cs2 = slice(gb * bs, (gb + 1) * bs)
            k.op("pool", lambda en: en.affine_select(dtmp[:, cs2], ones_[:, cs2], [[0, bs]], ALU.is_ge, 0.0,
                                                       base=-gb * bs, channel_multiplier=1), [ones_[:, cs2]], [dtmp[:, cs2]])
            k.op("pool", lambda en: en.affine_select(Dm[:, cs2], dtmp[:, cs2], [[0, bs]], ALU.is_ge, 0.0,
                                                       base=gb * bs + bs - 1, channel_multiplier=-1), [dtmp[:, cs2]], [Dm[:, cs2]])
    k.tt(msu32, D32, m_su[:], ALU.mult)
    k.tt(msl32, D32, m_sl[:], ALU.mult)
    k.tt(msl64o, D64, m_sl[:], ALU.mult)
    k.tt(msl128o, m_sl[:], msl64o, ALU.subtract)
    k.tt(msl64o, msl64o, msl32, ALU.subtract)
    Qz = [ar.alloc([128], BF16) for _ in range(4)]
    Pz = [ar.alloc([64], BF16) for _ in range(4)]
    fin = ar.alloc([128], BF16)
    for _b in Qz + Pz:
        k.memset(_b, 0.0)
    ST = ar.alloc([64], BF16)
    ynbs = [ar.alloc([128], BF16), ar.alloc([128], BF16)]
    gneps = ar.alloc([1])
    k.memset(gneps, GN_EPS)

    mu = pc[:, 0:14]
    load_cols(mu, I["rw_mu"][l], 14)
    om = pc[:, 14:28]
    k.ts(om, mu, -1.0, 1.0, ALU.mult, ALU.add)
    pc2 = ar.alloc([24])
    load_cols(pc2[:, 0:4], I["rw_w0"][l], 4)
    load_cols(pc2[:, 4:8], I["rw_a0"][l], 4)
    load_cols(pc2[:, 8:12], I["rw_k_k"][l], 4)
    load_cols(pc2[:, 12:16], I["rw_k_a"][l], 4)
    load_cols(pc2[:, 16:20], I["rw_r_k"][l], 4)
    k.ts(pc2[:, 20:24], pc2[:, 12:16], -1.0, 1.0, ALU.mult, ALU.add)
    bcast(lng, I["rw_ln_g"][l], 512)
    bcast(lnb, I["rw_ln_b"][l], 512)
    k.dma("pool", w2a2[0:64, :], I["rw_w2"][l])
    k.dma("pool", w2a2[64:128, :], I["rw_a2"][l])
    k.dma("pool", g2w, I["rw_g2"][l])
    k.memset(cmask, 1.0)
    k.memset(cmask[:, 0::128], 0.0)
    k.memset(u[:, 0:1], 0.0)

    wi = [0]

    def proj_lerp(col0, mucol, dst):
        wt = wts[wi[0] % 2]
        wi[0] += 1
        load_w(wt, W[:, col0:col0 + 128])
        k.memset(u[:, 0:1], 0.0)
        for n in range(NG):
            pb = bank()
            for kc in range(KC):
                k.mm(pb, wt[:, kc, :], hT[:, kc, n * 512:(n + 1) * 512], start=(kc == 0), stop=(kc == KC - 1))
            k.copy(u[:, 1 + n * 512:1 + (n + 1) * 512], pb, e="act")
            k.act(dst[:, n * 512:(n + 1) * 512], pb, AF.Identity, scale=om[:, mucol:mucol + 1])
        k.stt(dst, u[:, 0:S], mu[:, mucol:mucol + 1], dst, ALU.mult, ALU.add)

    proj_lerp(1536, 12, F[0])
    k.act(F[0][0:64, :], F[0][0:64, :], AF.Tanh)
    k.copy(walo, F[0])
    proj_lerp(1664, 13, F[0])
    k.act(gsig, F[0], AF.Sigmoid)

    for c in range(4):
        cs_ = slice(c * 128, (c + 1) * 128)
        r32, k32, v32 = F[0], F[1], F[2]
        proj_lerp(c * 128, c, r32)
        proj_lerp(512 + c * 128, 4 + c, k32)
        proj_lerp(1024 + c * 128, 8 + c, v32)
        lws, lw, a32, g32 = F[3], F[4], F[5], F[6]
        for n in range(NG):
            ns = slice(n * 512, (n + 1) * 512)
            pb = bank()
            k.mm(pb, w2a2[0:64, cs_], walo[0:64, ns])
            k.act(lws[:, ns], pb, AF.Sigmoid, bias=pc2[:, c:c + 1])
            pb = bank()
            k.mm(pb, w2a2[64:128, cs_], walo[64:128, ns])
            k.act(a32[:, ns], pb, AF.Sigmoid, bias=pc2[:, 4 + c:5 + c])
            pb = bank()
            k.mm(pb, g2w[:, cs_], gsig[:, ns])
            k.copy(gT[:, ns], pb, e="act")
        k.ts(lws, lws, NEG_E05, None, ALU.mult)
        k.copy(Vb, v32, e="pool")
        kk = F[6]
        k.ts(kk, k32, pc2[:, 8 + c:9 + c], None, ALU.mult)
        k.act(u[:, 0:S], kk, AF.Square)
        for n in range(NG):
            ns = slice(n * 512, (n + 1) * 512)
            pb = bank()
            k.mm(pb, blk2[:], u[:, ns])
            k.ts(u[:, ns], pb, 1e-24, None, ALU.max)
        k.act(u[:, 0:S], u[:, 0:S], AF.Sqrt)
        k.recip(u[:, 0:S], u[:, 0:S])
        k.tt(kk, kk, u[:, 0:S], ALU.mult)
        kp = F[7][:, 0:S]
        k.ts(kp, a32, pc2[:, 12 + c:13 + c], pc2[:, 20 + c:21 + c], ALU.mult, ALU.add)
        k.tt(kp, kp, k32, ALU.mult)
        beta = F[1]
        k.tt(beta, kk, a32, ALU.mult)
        tmp = F[5]
        k.tt(tmp, r32, kp, ALU.mult)
        k.ts(tmp, tmp, pc2[:, 16 + c:17 + c], None, ALU.mult)
        for n in range(NG):
            ns = slice(n * 512, (n + 1) * 512)
            pb = bank()
            k.mm(pb, blk2[:], tmp[:, ns])
            k.tt(bonus[:, ns], pb, v32[:, ns], ALU.mult)
        k.scan(lw, cmask, lws, 0.0, ALU.mult, ALU.add)
        lwx = F[3]
        k.tt(lwx, lw, lws, ALU.subtract)
        e = F[5]
        k.act(e, lw, AF.Exp)
        k.tt(Rt, r32, e, ALU.mult)
        k.copy(wC, e[:, 127::128])
        k.act(e, lwx, AF.Exp)
        k.stt(At, kk, -1.0, e, ALU.mult, ALU.mult)
        k.act(e, lw, AF.Exp, scale=-1.0)
        k.tt(Bt, beta, e, ALU.mult)
        k.tt(Kt, kp, e, ALU.mult)
        e3 = F[0]
        k.tt(e3.rearrange("p (a b) -> p a b", a=NT), bcl(lw[:, 127::128], 128),
             lw.rearrange("p (a b) -> p a b", a=NT), ALU.subtract)
        k.act(e3, e3, AF.Exp)
        k.tt(Bhf, beta, e3, ALU.mult)
        k.tt(Khf, kp, e3, ALU.mult)
        for src, dst in ((At, Atok), (Bhf, Bhtok), (Khf, Khtok), (Vb, Vtok)):
            for tq in range(4):
                pb = bank().bitcast(BF16)
                for j in range(4):
                    t = tq * 4 + j
                    k.transpose(pb[:, j * 128:(j + 1) * 128], src[:, t * 128:(t + 1) * 128], identb[:])
                evac(dst[:, tq * 4:tq * 4 + 4, :], pb[:, 0:512].rearrange("p (a b) -> p a b", a=4))

        if not hasattr(k, "rwch"):
            k.rwch = k.sb("rwch", [128, 4 * 9 * 128], F32)
        carved = []

        def carve(Fb):
            o = [0]

            def f32(n):
                v = Fb[:, o[0]:o[0] + n]
                o[0] += n
                return v

            def b16(n):
                v = Fb[:, o[0]:o[0] + n // 2].bitcast(BF16)
                o[0] += n // 2
                return v
            B = {}
            si = len(carved)
            carved.append(1)
            base = si * 9 * 128
            ch = [k.rwch[:, base + j_ * 128:base + (j_ + 1) * 128] for j_ in range(9)]
            B["X"] = [ch[0], ch[1]]
            B["XT"] = [ch[2], ch[3]]
            for j_, nm in enumerate(("T", "TT", "M1", "No64", "No128")):
                B[nm] = ch[4 + j_]
            for nm in ("Tb", "Mbr", "Mkr", "Nkb", "AZ", "AV"):
                B[nm] = b16(128)
            B["yn"] = f32(64)
            B["st"] = f32(8)
            B["junk"] = f32(64)
            return B

        BS = [carve(F[3]), carve(F[4]), carve(F[5]), carve(F[6])]

        def unit(hh, tc):
            B = BS[hh + 2 * (tc % 2)]
            qi = hh + 2 * (tc % 2)
            ynb = ynbs[tc % 2]
            X = B["X"]; XT = B["XT"]; T = B["T"]; TT = B["TT"]; M1 = B["M1"]; No64 = B["No64"]; No128 = B["No128"]
            Tb = B["Tb"]; Mbr = B["Mbr"]; Mkr = B["Mkr"]; Nkb = B["Nkb"]; AZ = B["AZ"]; AV = B["AV"]
            yn = B["yn"]; st = B["st"]; junk = B["junk"]
            ts_ = slice(tc * 128, (tc + 1) * 128)
            h = 2 * c + hh
            po = hh * 64
            ps_ = slice(po, po + 64)
            p1 = bank()
            k.mm(p1[:, 0:128], Bt[ps_, ts_], At[ps_, ts_])
            k.mm(p1[:, 128:256], Bt[ps_, ts_], Rt[ps_, ts_])
            k.mm(p1[:, 256:384], Kt[ps_, ts_], At[ps_, ts_])
            k.mm(p1[:, 384:512], Kt[ps_, ts_], Rt[ps_, ts_])
            p2 = bank()
            k.mm(p2[:, 0:128], At[ps_, ts_], Bt[ps_, ts_])
            yield
            R_ = lambda ap_: ap_.bitcast(F32R)
            k.tt(R_(X[0]), p1[:, 0:128], msu32, ALU.mult)
            k.tt(R_(XT[0]), p2[:, 0:128], msl32, ALU.mult)
            k.tt(R_(T), X[0], ident[:], ALU.add)
            k.tt(R_(No64), p2[:, 0:128], msl64o, ALU.mult)
            k.tt(R_(No128), p2[:, 0:128], msl128o, ALU.mult)
            k.tt(Mbr, p1[:, 128:256], m_siu[:], ALU.mult)
            k.tt(Nkb, p1[:, 256:384], m_su[:], ALU.mult)
            k.tt(Mkr, p1[:, 384:512], m_siu[:], ALU.mult)
            yield
            cur = 0
            for lvl in range(4):
                nxt = 1 - cur
                pa = bank()
                if lvl < 3:
                    k.mm(pa[:, 0:128], R_(XT[cur]), R_(X[cur]))
                k.mm(pa[:, 128:256], R_(X[cur]), R_(XT[cur]))
                yield
                k.copy(R_(XT[nxt]), pa[:, 128:256], e="act")
                if lvl < 3:
                    k.copy(R_(X[nxt]), pa[:, 0:128], e="act")
                pt_ = bank()
                k.mm(pt_[:, 0:128], R_(XT[nxt]), R_(T))
                yield
                k.tt(R_(T), T, pt_[:, 0:128], ALU.add)
                cur = nxt
            for mi, No in enumerate((No64, No128)):
                ptt = bank()
                k.transpose(ptt[:, 0:128], T, ident[:])
                pm = bank()
                k.mm(pm[:, 0:128], R_(No), R_(T))
                yield
                k.copy(R_(TT), ptt[:, 0:128], e="act")
                k.copy(R_(M1), pm[:, 0:128], e="act")
                pt_ = bank()
                k.mm(pt_[:, 0:128], R_(TT), R_(M1))
                if mi == 1:
                    pzt = bank()
                    k.mm(pzt[:, 0:64], Nkb, Vtok[:, tc, ps_])
                yield
                if mi == 0:
                    k.tt(R_(T), T, pt_[:, 0:128], ALU.add)
                else:
                    k.tt(Tb, T, pt_[:, 0:128], ALU.add)
            k.copy(AZ[:, 64:128], pzt[:, 0:64], e="act")
            k.copy(AZ[:, 0:64], Atok[:, tc, ps_], e="pool")
            pav = bank()
            k.mm(pav[:, 0:128], Tb, AZ)
            yield
            k.copy(AV, pav[:, 0:128], e="act")
            pq = bank()
            k.mm(pq[ps_, 0:128], AV[:, 0:64], Mbr)
            pp = bank()
            k.mm(pp[ps_, 0:64], AV[:, 0:64], Bhtok[:, tc, ps_])
            yield
            k.tt(Qz[qi][ps_, :], pq[ps_, 0:128], Rt[ps_, ts_], ALU.add)
            k.stt(Pz[qi][ps_, :], ident[ps_, ps_], wC[ps_, tc:tc + 1], pp[ps_, 0:64], ALU.mult, ALU.add)
            py = bank()
            if tc > 0:
                k.mm(py[:, 0:64], Qz[qi], ST, start=True, stop=False)
            k.mm(py[:, 0:64], Mbr, AV[:, 64:128], start=(tc == 0), stop=False)
            k.mm(py[:, 0:64], Mkr, Vtok[:, tc, ps_], start=False, stop=True)
            if tc < NT - 1:
                pcn = bank()
                if tc > 0:
                    k.mm(pcn[ps_, 0:64], Pz[qi], ST, start=True, stop=False)
                k.mm(pcn[ps_, 0:64], Bhtok[:, tc, ps_], AV[:, 64:128], start=(tc == 0), stop=False)
                k.mm(pcn[ps_, 0:64], Khtok[:, tc, ps_], Vtok[:, tc, ps_], start=False, stop=True)
                k.copy(ST[ps_, :], pcn[ps_, 0:64], e="act")
            yield
            k.act(junk, py[:, 0:64], AF.Copy, accum_out=st[:, 0:1])
            k.ts(st[:, 1:2], st[:, 0:1], -1.0 / 64, None, ALU.mult)
            k.act(junk, py[:, 0:64], AF.Square, bias=st[:, 1:2], accum_out=st[:, 2:3])
            yield
            k.act(st[:, 3:4], st[:, 2:3], AF.Sqrt, bias=gneps, scale=1.0 / 64)
            k.recip(st[:, 4:5], st[:, 3:4])
            k.ts(yn, py[:, 0:64], st[:, 1:2], st[:, 4:5], ALU.add, ALU.mult)
            k.tt(yn, yn, lng[:, h * 64:(h + 1) * 64], ALU.mult, e="pool")
            k.tt(ynb[:, po:po + 64], yn, lnb[:, h * 64:(h + 1) * 64], ALU.add, e="pool")

        for tcp in range(0, NT, 2):
            gens = [unit(0, tcp), unit(1, tcp), unit(0, tcp + 1), unit(1, tcp + 1)]
            alive = [True] * 4
            while any(alive):
                for gi in range(4):
                    if alive[gi]:
                        try:
                            next(gens[gi])
                        except StopIteration:
                            alive[gi] = False
            for tc in (tcp, tcp + 1):
                ts_ = slice(tc * 128, (tc + 1) * 128)
                ptr = bank().bitcast(BF16)
                k.transpose(ptr[:, 0:128], ynbs[tc % 2], identb[:])
                k.tt(fin, ptr[:, 0:128], bonus[:, ts_], ALU.add)
                k.tt(yrwT[:, ts_], fin, gT[:, ts_], ALU.mult, e="pool")
        k.dma("sp", ybr_d["rw"][c], yrwT)
        if c == 0 and env["dbg"] and ("yrw_%d" % l) in env["dbg"]:
            k.copy(F[0], yrwT)
            DBG("yrw_%d" % l, F[0], [128, S])


S = 2048
KC = 8
NG = 4
DFF = 2816
C_GATE = 5904


def phase_merge(env):
    k = env["k"]; I = env["I"]; ar = env["ar"]; hT = env["hT"]; bank = env["bank"]; l = env["l"]
    mod = env["mod"]; xT_d = env["xT_d"]; ybr_d = env["ybr_d"]; load_w = env["load_w"]; DBG = env["DBG"]
    ar.reset()
    yb = {"rw": ar.alloc([4, S], BF16), "sb": ar.alloc([4, S], BF16), "m2": ar.alloc([8, S], BF16)}
    mergedT = ar.alloc([8, S], BF16)
    wg = [ar.alloc([KC, 384], BF16), ar.alloc([KC, 384], BF16)]
    wo = [ar.alloc([16, 128], BF16), ar.alloc([16, 128], BF16)]
    gsb = [ar.alloc([512]), ar.alloc([512])]
    acc = ar.alloc([512])
    xc = [ar.alloc([S]), ar.alloc([S])]
    for name, nch in (("rw", 4), ("sb", 4), ("m2", 8)):
        for c in range(nch):
            k.dma("sp" if c % 2 == 0 else "act", yb[name][:, c, :], ybr_d[name][c])
    wsrc = {"rw": I["rw_wo"][l], "sb": I["sb_wo"][l], "m2": I["m2_wo"][l]}
    names = ("rw", "sb", "m2")
    kcs = (4, 4, 8)
    offs = (0, 4, 8)
    gi = 0
    for f in range(8):
        wgt = wg[f % 2]
        wot = wo[f % 2]
        for b in range(3):
            load_w(wgt[:, :, b * 128:(b + 1) * 128],
                   I["w_in"][l][:, C_GATE + b * 1024 + f * 128:C_GATE + b * 1024 + (f + 1) * 128])
            load_w(wot[:, offs[b]:offs[b] + kcs[b], :], wsrc[names[b]][:, f * 128:(f + 1) * 128])
        for n in range(NG):
            ns = slice(n * 512, (n + 1) * 512)
            for b in range(3):
                pg = bank()
                for kc in range(KC):
                    k.mm(pg, wgt[:, kc, b * 128:(b + 1) * 128], hT[:, kc, ns], start=(kc == 0), stop=(kc == KC - 1))
                g = gsb[gi % 2]
                gi += 1
                k.act(g, pg, AF.Sigmoid)
                py = bank()
                for kc in range(kcs[b]):
                    k.mm(py, wot[:, offs[b] + kc, :], yb[names[b]][:, kc, ns], start=(kc == 0), stop=(kc == kcs[b] - 1))
                if b == 0:
                    k.tt(acc, g, py, ALU.mult)
                elif b == 1:
                    k.tt(g, g, py, ALU.mult)
                    k.tt(acc, acc, g, ALU.add)
                else:
                    k.tt(g, g, py, ALU.mult)
                    k.tt(mergedT[:, f, ns], acc, g, ALU.add)
    wt2 = [wg[0][:, :, 0:128], wg[1][:, :, 0:128]]
    for f in range(8):
        wt = wt2[f % 2]
        load_w(wt, I["w_out"][l][:, f * 128:(f + 1) * 128])
        x = xc[f % 2]
        k.dma("sp", x, xT_d[f])
        for n in range(NG):
            ns = slice(n * 512, (n + 1) * 512)
            pb = bank()
            for kc in range(KC):
                k.mm(pb, wt[:, kc, :], mergedT[:, kc, ns], start=(kc == 0), stop=(kc == KC - 1))
            k.stt(x[:, ns], pb, mod[:, 16 + f:17 + f], x[:, ns], ALU.mult, ALU.add)
        k.dma("sp", xT_d[f], x)
        if f == 0:
            DBG("x1_%d" % l, x, [128, S])


def phase_ffn(env):
    k = env["k"]; I = env["I"]; ar = env["ar"]; hT = env["hT"]; bank = env["bank"]; l = env["l"]
    mod = env["mod"]; xT_d = env["xT_d"]; load_w = env["load_w"]; load_cols = env["load_cols"]; DBG = env["DBG"]
    ar.reset()
    aT = ar.alloc([22, S], BF16)
    wts = [ar.alloc([KC, 256], BF16), ar.alloc([KC, 256], BF16), ar.alloc([KC, 256], BF16)]
    u = [ar.alloc([S + 2]), ar.alloc([S + 2])]
    tgs = [ar.alloc([S]), ar.alloc([S])]
    tvs = [ar.alloc([S]), ar.alloc([S])]
    cw = ar.alloc([132])
    cb = ar.alloc([44])
    load_cols(cw, I["ffn_conv_w"][l].rearrange("a b -> (a b)"), 132)
    load_cols(cb, I["ffn_conv_b"][l], 44)
    k.memset(u[0][:, 0:2], 0.0)
    k.memset(u[1][:, 0:2], 0.0)
    def ld_up(jj):
        w_ = wts[jj % 3]
        load_w(w_[:, :, 0:128], I["ffn_w_up"][l][:, jj * 128:(jj + 1) * 128])
        load_w(w_[:, :, 128:256], I["ffn_w_up"][l][:, DFF + jj * 128:DFF + (jj + 1) * 128])

    ld_up(0)
    ld_up(1)
    for j in range(22):
        wt = wts[j % 3]
        if j + 2 < 22:
            ld_up(j + 2)
        tg = tgs[j % 2]
        tv = tvs[j % 2]
        for half in range(2):
            cc = j + 22 * half
            uu = u[half]
            dst = tg if half == 0 else tv
            for n in range(NG):
                pb = bank()
                for kc in range(KC):
                    k.mm(pb, wt[:, kc, half * 128:(half + 1) * 128], hT[:, kc, n * 512:(n + 1) * 512],
                         start=(kc == 0), stop=(kc == KC - 1))
                k.copy(uu[:, 2 + n * 512:2 + (n + 1) * 512], pb, e="act")
                k.act(dst[:, n * 512:(n + 1) * 512], pb, AF.Identity, bias=cb[:, cc:cc + 1], scale=cw[:, 88 + cc:89 + cc])
            k.stt(dst, uu[:, 1:S + 1], cw[:, 44 + cc:45 + cc], dst, ALU.mult, ALU.add)
            k.stt(dst, uu[:, 0:S], cw[:, cc:cc + 1], dst, ALU.mult, ALU.add)
        k.act(tg, tg, AF.Silu)
        k.tt(aT[:, j, :], tg, tv, ALU.mult)
    if env["dbg"] and ("aT%d" % l) in env["dbg"]:
        k.copy(tgs[0], aT[:, 5, :])
        DBG("aT%d" % l, tgs[0], [128, S])
    wd = [tgs[0].bitcast(BF16)[:, 0:22 * 128].rearrange("p (a b) -> p a b", a=22),
          tvs[0].bitcast(BF16)[:, 0:22 * 128].rearrange("p (a b) -> p a b", a=22)]
    xc = [u[0][:, 0:S], u[1][:, 0:S]]
    for f in range(8):
        wt = wd[f % 2]
        load_w(wt, I["ffn_w_down"][l][:, f * 128:(f + 1) * 128])
        x = xc[f % 2]
        k.dma("sp", x, xT_d[f])
        for n in range(NG):
            ns = slice(n * 512, (n + 1) * 512)
            pb = bank()
            for kc in range(22):
                k.mm(pb, wt[:, kc, :], aT[:, kc, ns], start=(kc == 0), stop=(kc == 21))
            k.stt(x[:, ns], pb, mod[:, 40 + f:41 + f], x[:, ns], ALU.mult, ALU.add)
        k.dma("sp", xT_d[f], x)
        if f == 0:
            DBG("x2_%d" % l, x, [128, S])


L_ = 2
D = 1024
S = 2048
KC = 8
NT = 16
NG = 4
EPS = 1e-6
N_IN = 8976
C_RW = 0
C_SB = 1792
C_M2 = 3328
C_GATE = 5904
DFF = 2816

SHAPES = [
    ("x", [S, D]), ("c", [D]), ("ada_w", [L_, D, 6 * D]), ("ada_b", [L_, 6 * D]), ("norm1_g", [L_, D]),
    ("norm2_g", [L_, D]), ("w_in", [L_, D, N_IN]), ("rw_mu", [L_, 1792]), ("rw_w0", [L_, 512]),
    ("rw_w2", [L_, 64, 512]), ("rw_a0", [L_, 512]), ("rw_a2", [L_, 64, 512]), ("rw_g2", [L_, 128, 512]),
    ("rw_k_k", [L_, 512]), ("rw_k_a", [L_, 512]), ("rw_r_k", [L_, 512]), ("rw_ln_g", [L_, 512]),
    ("rw_ln_b", [L_, 512]), ("rw_wo", [L_, 512, D]), ("sb_wo", [L_, 512, D]), ("m2_conv_w", [L_, 4, 1536]),
    ("m2_conv_b", [L_, 1536]), ("m2_dt_bias", [L_, 16]), ("m2_a_log", [L_, 16]), ("m2_d", [L_, 16]),
    ("m2_norm_g", [L_, D]), ("m2_wo", [L_, D, D]), ("w_out", [L_, D, D]), ("ffn_w_up", [L_, D, 2 * DFF]),
    ("ffn_conv_w", [L_, 3, 2 * DFF]), ("ffn_conv_b", [L_, 2 * DFF]), ("ffn_w_down", [L_, DFF, D]),
    ("final_norm_g", [D]),
]


def prod(s):
    r = 1
    for v in s:
        r *= v
    return r


class Arena:
    def __init__(self, k, words):
        self.t = k.sb("arena", [128, words], F32)
        self.words = words
        self.off = 0

    def reset(self):
        self.off = 0

    def alloc(self, shape, dt=F32):
        n = prod(shape)
        w = n if dt == F32 else (n + 1) // 2
        w = (w + 3) // 4 * 4
        assert self.off + w <= self.words, ("arena overflow", self.off, w, self.words)
        v = self.t[:, self.off:self.off + w]
        self.off += w
        if dt == BF16:
            v = v.bitcast(BF16)
        v = v[:, 0:n]
        if len(shape) == 2:
            v = v.rearrange("p (a b) -> p a b", a=shape[0])
        elif len(shape) == 3:
            v = v.rearrange("p (a b c) -> p a b c", a=shape[0], b=shape[1])
        return v


def build(stop_after=None, nlayers=L_, dbg=None):
    k = KB()
    nc = k.nc
    I = {}
    for name, shape in SHAPES:
        I[name] = nc.dram_tensor(name, shape, F32, kind="ExternalInput").ap()
    out = nc.dram_tensor("out", [S, D], F32, kind="ExternalOutput").ap()
    dbg_out = {}

    def DBG(name, ap_sb, shape):
        if dbg is None or name not in dbg:
            return
        o = nc.dram_tensor("dbg_" + name, list(shape), F32, kind="ExternalOutput").ap()
        dbg_out[name] = o
        k.dma("sp", o, ap_sb, is_output=True)

    ar = Arena(k, 38 * 1024)
    hT = k.sb("hT", [128, KC, S], BF16)
    psum = k.ps("psum", [128, 8, 512], F32)
    ident = k.sb("ident", [128, 128], F32)
    identb = k.sb("identb", [128, 128], BF16)
    ones = k.sb("ones", [128, 128], F32)
    onesb = k.sb("onesb", [128, 128], BF16)
    m_su = k.sb("m_su", [128, 128], F32)
    m_siu = k.sb("m_siu", [128, 128], F32)
    m_sl = k.sb("m_sl", [128, 128], F32)
    blk2 = k.sb("blk2", [128, 128], F32)
    colst = k.sb("colst", [128, 128], F32)
    cs = k.sb("cs", [128, KC], F32)
    mod = k.sb("mod", [128, 48], F32)
    modb = k.sb("modb", [128, 48], F32)
    ncoef = k.sb("ncoef", [128, 4 * KC], F32)
    gcol = k.sb("gcol", [128, 3 * KC], F32)
    xT_d = k.dram("xT_d", [KC, 128, S], F32)
    ybr_d = {"rw": k.dram("yrw_d", [4, 128, S], BF16), "sb": k.dram("ysb_d", [4, 128, S], BF16),
             "m2": k.dram("ym2_d", [8, 128, S], BF16)}

    state = {"bank": 0, "ev": 0, "wq": 0}

    def bank():
        b = state["bank"]
        state["bank"] = (b + 1) % 8
        return psum[:, b, :]

    def evac(out_ap, in_ap):
        state["ev"] ^= 1
        if state["ev"]:
            k.copy(out_ap, in_ap, e="act")
        else:
            k.copy(out_ap, in_ap, e="dve")

    k.memset(ones[:], 1.0)
    k.copy(onesb[:], ones[:])

    def aff(dst, pattern_step, cmul, base, cmp):
        k.op("pool", lambda en: en.affine_select(dst, ones[:], [[pattern_step, 128]], cmp, 0.0, base=base,
                                                   channel_multiplier=cmul), [ones[:]], [dst])

    aff(ident[:], 1, -1, 0, ALU.is_equal)
    aff(m_su[:], 1, -1, 0, ALU.is_gt)
    aff(m_siu[:], 1, -1, 0, ALU.is_ge)
    aff(m_sl[:], -1, 1, 0, ALU.is_gt)
    k.copy(identb[:], ident[:])
    k.memset(blk2[:], 0.0)
    k.memset(blk2[0:64, 0:64], 1.0)
    k.memset(blk2[64:128, 64:128], 1.0)

    def load_cols(dst, src_flat, n, q="sp"):
        done = 0
        while done < n:
            m = min(128, n - done)
            k.dma(q, colst[0:m, :], src_flat[done * 128:(done + m) * 128].rearrange("(c p) -> c p", p=128))
            pb = bank()
            k.transpose(pb[:, 0:m], colst[0:m, :], ident[0:m, 0:m])
            k.copy(dst[:, done:done + m], pb[:, 0:m])
            done += m

    def bcast(dst, src_flat, n, q="sp"):
        k.dma(q, dst, src_flat.partition_broadcast(128))

    ar.reset()
    xs = ar.alloc([KC, S])
    for t in range(NT):
        xtile = ar.alloc([D]) if t == 0 else xtile
        k.dma("sp", xtile, I["x"][t * 128:(t + 1) * 128, :])
        for half in range(2):
            pb = bank()
            for j in range(4):
                c = half * 4 + j
                k.transpose(pb[:, j * 128:(j + 1) * 128], xtile[:, c * 128:(c + 1) * 128], ident[:])
            evac(xs[:, half * 4:half * 4 + 4, t * 128:(t + 1) * 128], pb.rearrange("p (a b) -> p a b", a=4))
    for c in range(KC):
        k.dma("sp" if c % 2 == 0 else "act", xT_d[c], xs[:, c, :])
    load_cols(cs[:], I["c"], KC)
    k.act(cs[:], cs[:], AF.Silu)
    load_cols(gcol[:, 16:24], I["final_norm_g"], KC)

    def norm_to_hT(acol, bcol, final=False):
        ar.reset()
        xin = ar.alloc([KC, S])
        sq = ar.alloc([S])
        rstd = ar.alloc([S])
        pbs = [bank() for _ in range(NG)]
        for c in range(KC):
            k.dma("sp", xin[:, c, :], xT_d[c])
            k.act(sq, xin[:, c, :], AF.Square)
            for n in range(NG):
                k.mm(pbs[n], ones[:], sq[:, n * 512:(n + 1) * 512], start=(c == 0), stop=(c == KC - 1))
        for n in range(NG):
            k.act(rstd[:, n * 512:(n + 1) * 512], pbs[n], AF.Sqrt, bias=epsc[:], scale=1.0 / D)
        DBG('sqrt', rstd, [128, S])
        k.recip(rstd, rstd)
        DBG('rstd', rstd, [128, S])
        DBG('xin', xin[:, 0, :], [128, S])
        for c in range(KC):
            k.tt(xin[:, c, :], xin[:, c, :], rstd, ALU.mult)
            if final:
                k.act(xin[:, c, :], xin[:, c, :], AF.Identity, scale=acol[:, c:c + 1])
            else:
                k.act(hT[:, c, :], xin[:, c, :], AF.Identity, bias=bcol[:, c:c + 1], scale=acol[:, c:c + 1])
        return xin

    epsc = k.sb("epsc", [128, 1], F32)
    k.memset(epsc[:], EPS)

    def load_w(dst, src_rows, q="pool"):
        k.dma(q, dst, src_rows.rearrange("(c p) n -> p c n", p=128))

    for l in range(nlayers):
        ar.reset()
        wts = [ar.alloc([KC, 512]) for _ in range(6)]
        load_cols(modb[:], I["ada_b"][l], 48)
        pbm = bank()

        def ld_ada(g_):
            k.dma(("sp", "act", "sp", "pool")[g_ % 4], wts[g_ % 6],
                  I["ada_w"][l][:, g_ * 512:(g_ + 1) * 512].rearrange("(c p) n -> p c n", p=128))

        for g_ in range(5):
            ld_ada(g_)
        for g in range(12):
            wt = wts[g % 6]
            if g + 5 < 12:
                ld_ada(g + 5)
            for j in range(4):
                oc = g * 4 + j
                for kc in range(KC):
                    k.mm(pbm[:, oc:oc + 1], wt[:, kc, j * 128:(j + 1) * 128], cs[:, kc:kc + 1],
                         start=(kc == 0), stop=(kc == KC - 1))
        k.tt(mod[:], pbm[:, 0:48], modb[:], ALU.add)
        load_cols(gcol[:, 0:8], I["norm1_g"][l], KC)
        load_cols(gcol[:, 8:16], I["norm2_g"][l], KC)
        k.ts(ncoef[:, 0:8], mod[:, 8:16], 1.0, None, ALU.add)
        k.tt(ncoef[:, 0:8], ncoef[:, 0:8], gcol[:, 0:8], ALU.mult)
        k.copy(ncoef[:, 8:16], mod[:, 0:8])
        k.ts(ncoef[:, 16:24], mod[:, 32:40], 1.0, None, ALU.add)
        k.tt(ncoef[:, 16:24], ncoef[:, 16:24], gcol[:, 8:16], ALU.mult)
        k.copy(ncoef[:, 24:32], mod[:, 24:32])
        DBG("mod%d" % l, mod[:], [128, 48])

        norm_to_hT(ncoef[:, 0:8], ncoef[:, 8:16])
        if dbg and ("hT%d" % l) in dbg:
            ar.reset()
            tmp = ar.alloc([KC, S])
            k.copy(tmp, hT[:])
            DBG("hT%d" % l, tmp, [128, KC, S])
        if stop_after == "norm1":
            break

        env = dict(k=k, nc=nc, I=I, ar=ar, hT=hT, psum=psum, bank=bank, evac=evac, ident=ident, identb=identb,
                   ones=ones, onesb=onesb, m_su=m_su, m_siu=m_siu, m_sl=m_sl, blk2=blk2, load_cols=load_cols,
                   bcast=bcast, load_w=load_w, mod=mod, xT_d=xT_d, ybr_d=ybr_d, DBG=DBG, l=l, dbg=dbg,
                   epsc=epsc)
        if stop_after not in ("sb", "rw"):
            phase_m2(env)
        if stop_after == "m2":
            break
        if stop_after != "rw":
            phase_sb(env)
        if stop_after == "sb":
            break
        phase_rw(env)
        if stop_after == "rw":
            break
        phase_merge(env)
        if stop_after == "merge":
            break
        norm_to_hT(ncoef[:, 16:24], ncoef[:, 24:32])
        phase_ffn(env)
        if stop_after == "ffn":
            break

    if stop_after is None:
        xin = norm_to_hT(gcol[:, 16:24], None, final=True)
        otile = [ar.alloc([D]), ar.alloc([D])]
        for t in range(NT):
            ot = otile[t % 2]
            for half in range(2):
                pb = bank()
                for j in range(4):
                    c = half * 4 + j
                    k.transpose(pb[:, j * 128:(j + 1) * 128], xin[:, c, t * 128:(t + 1) * 128], ident[:])
                evac(ot[:, half * 512:(half + 1) * 512], pb)
            k.dma("sp" if t % 2 == 0 else "act", out[t * 128:(t + 1) * 128, :], ot, is_output=True)
    else:
        ar.reset()
        z = ar.alloc([D])
        k.memset(z, 0.0)
        for t in range(NT):
            k.dma("sp", out[t * 128:(t + 1) * 128, :], z, is_output=True)
    k.finish()
    return k, dbg_out


from concourse.bass_utils import run_bass_kernel_spmd

_CACHE = {}


def kernel(**inputs):
    n = 8
    if "k" not in _CACHE:
        _CACHE["k"] = build()[0]
    kb_ = _CACHE["k"]
    shared = {}
    for name, shape in SHAPES:
        if name in ("x", "c"):
            continue
        a = np.asarray(inputs[name], dtype=np.float32)
        shared[name] = np.ascontiguousarray(a.reshape(shape))
    x = np.asarray(inputs["x"], dtype=np.float32)
    c = np.asarray(inputs["c"], dtype=np.float32)
    in_maps = []
    for b in range(n):
        m = dict(shared)
        m["x"] = np.ascontiguousarray(x[b])
        m["c"] = np.ascontiguousarray(c[b])
        in_maps.append(m)
    res = run_bass_kernel_spmd(kb_.nc, in_maps, core_ids=list(range(n)))
    return np.stack([np.asarray(r["out"], dtype=np.float32) for r in res.results], axis=0)
```

```python
import numpy as np
import concourse.bass as bass
import concourse.mybir as mybir

F32 = mybir.dt.float32
BF16 = mybir.dt.bfloat16
F32R = mybir.dt.float32r
AF = mybir.ActivationFunctionType
ALU = mybir.AluOpType
AX = mybir.AxisListType

SAME_ENG_SYNC = True
NDSEM = 8


class KB:
    def __init__(self):
        self.nc = bass.Bass("TRN2", target_bir_lowering=False)
        nc = self.nc
        self.eng = {"pe": nc.tensor, "dve": nc.vector, "act": nc.scalar, "pool": nc.gpsimd, "sp": nc.sync}
        self._ctx = []
        self.sem = {}
        self.cnt = {}
        for e in self.eng:
            self.sem[e] = self._enter(nc.semaphore("c_" + e))
            self.cnt[e] = 0
        self.dsem = {}
        self.dcnt = {}
        for q in ("sp", "act", "pool"):
            self.dsem[q] = [self._enter(nc.semaphore("d_%s%d" % (q, i))) for i in range(NDSEM)]
            self.dcnt[q] = 0
        self.semname = {}
        for e in self.eng:
            self.semname[id(self.sem[e])] = e
        self.seen = {e: {} for e in self.eng}
        self.acc = {}
        self.semobj = {}
        for e in self.eng:
            self.semobj[e] = self.sem[e]
        for q in self.dsem:
            for i, s in enumerate(self.dsem[q]):
                self.semobj["d_%s%d" % (q, i)] = s
        self.n_inst = 0
        self.n_wait = 0
        self.K = {}
        self.out_events = []

    def _enter(self, cm):
        v = cm.__enter__()
        self._ctx.append(cm)
        return v

    def sb(self, name, shape, dt=F32):
        return self._enter(self.nc.sbuf_tensor(name, list(shape), dt))

    def ps(self, name, shape, dt=F32):
        return self._enter(self.nc.psum_tensor(name, list(shape), dt))

    def dram(self, name, shape, dt=F32, kind="Internal"):
        return self.nc.dram_tensor(name, list(shape), dt, kind=kind).ap()

    @staticmethod
    def region(ap):
        t = ap.tensor
        name = t.name
        space = str(ap.space)
        pat = ap.ap
        off = ap.offset
        ds = 2 if t.dtype == BF16 else 4
        lo = 0
        hi = 0
        if "DRAM" in space:
            for st, n in pat:
                if st >= 0:
                    hi += st * (n - 1)
                else:
                    lo += st * (n - 1)
            return name, 0, 1, (off + lo) * ds, (off + hi + 1) * ds
        row = 1
        for d in t.shape[1:]:
            row *= d
        p0 = off // row
        pst, pn = pat[0]
        pn_eff = pn if pst != 0 else 1
        foff = off - p0 * row
        for st, n in pat[1:]:
            if st >= 0:
                hi += st * (n - 1)
            else:
                lo += st * (n - 1)
        assert 0 <= foff + lo and foff + hi < row, (name, off, pat, p0, row)
        if "PSUM" in space:
            b0 = ((foff + lo) * ds) // 2048
            b1 = ((foff + hi + 1) * ds - 1) // 2048 + 1
            return name, 0, 128, b0 * 2048, b1 * 2048
        return name, p0, p0 + pn_eff, (foff + lo) * ds, (foff + hi + 1) * ds

    def _deps(self, reads, writes, e=None):
        ev = {}

        def add(k, v):
            if ev.get(k, 0) < v:
                ev[k] = v

        regs = []
        for ap in reads:
            regs.append((self.region(ap), False, "PSUM" in str(ap.space)))
        for ap in writes:
            regs.append((self.region(ap), True, "PSUM" in str(ap.space)))
        for (name, p0, p1, f0, f1), isw, isps in regs:
            for a in self.acc.get(name, ()):
                if a[1] <= p0 or a[0] >= p1 or a[3] <= f0 or a[2] >= f1:
                    continue
                if a[4] == e:
                    if a[6] and not isw:
                        add(a[4], a[5])
                elif isw or a[6] or isps:
                    add(a[4], a[5])
        return ev, regs

    def _record(self, regs, semkey, val):
        for (name, p0, p1, f0, f1), isw, isps in regs:
            lst = self.acc.setdefault(name, [])
            if isw or isps:
                keep = []
                for a in lst:
                    cov = a[0] >= p0 and a[1] <= p1 and a[2] >= f0 and a[3] <= f1
                    if cov and (isw or not a[6] or a[4] == semkey):
                        continue
                    keep.append(a)
                lst[:] = keep
            else:
                lst[:] = [a for a in lst if not (a[4] == semkey and not a[6] and a[0] >= p0 and a[1] <= p1 and a[2] >= f0 and a[3] <= f1)]
            lst.append([p0, p1, f0, f1, semkey, val, isw])

    def _waits(self, e, ev):
        engine = self.eng[e]
        seen = self.seen[e]
        for k, v in sorted(ev.items(), key=lambda kv: -kv[1]):
            if k == e and (not SAME_ENG_SYNC or e == 'pe'):
                continue
            if seen.get(k, 0) >= v:
                continue
            engine.wait_ge(self.semobj[k], v)
            self.n_wait += 1
            seen[k] = v
            snap = self.K.get((k, v))
            if snap:
                for k2, v2 in snap.items():
                    if seen.get(k2, 0) < v2:
                        seen[k2] = v2

    def op(self, e, fn, reads, writes):
        ev, regs = self._deps(reads, writes, e)
        self._waits(e, ev)
        ins = fn(self.eng[e])
        self.cnt[e] += 1
        ins.then_inc(self.sem[e], 1)
        self.K[(e, self.cnt[e])] = dict(self.seen[e])
        self._record(regs, e, self.cnt[e])
        self.n_inst += 1
        return ins

    def dma(self, q, out, in_, is_output=False, **kw):
        ev, regs = self._deps([in_], [out])
        i = self.dcnt[q]
        slot = i % NDSEM
        key = "d_%s%d" % (q, slot)
        if i >= NDSEM:
            ev_prev = {key: 16 * (i // NDSEM)}
            self._waits(q, ev_prev)
        self._waits(q, ev)
        ins = self.eng[q].dma_start(out=out, in_=in_, **kw)
        val = 16 * (i // NDSEM + 1)
        ins.then_inc(self.semobj[key], 16)
        self.K[(key, val)] = dict(self.seen[q])
        self.dcnt[q] += 1
        self._record(regs, key, val)
        self.n_inst += 1
        if is_output:
            self.out_events.append((key, val))
        return ins

    def finish(self):
        ev = {}
        for k, v in self.out_events:
            if ev.get(k, 0) < v:
                ev[k] = v
        for q in self.dsem:
            i = self.dcnt[q]
            for s in range(min(i, NDSEM)):
                n_uses = (i - 1 - s) // NDSEM + 1
                k = "d_%s%d" % (q, s)
                if ev.get(k, 0) < 16 * n_uses:
                    ev[k] = 16 * n_uses
        for e in self.eng:
            if e != "sp" and self.cnt[e] > 0:
                ev[e] = self.cnt[e]
        self._waits("sp", ev)

    def mm(self, out, lhsT, rhs, start=True, stop=True, **kw):
        return self.op("pe", lambda en: en.matmul(out, lhsT, rhs, start=start, stop=stop, **kw), [lhsT, rhs] + ([] if start else [out]), [out])

    def transpose(self, out, in_, ident):
        return self.op("pe", lambda en: en.transpose(out, in_, ident), [in_, ident], [out])

    def act(self, out, in_, func, bias=None, scale=None, accum_out=None, e="act"):
        kw = {}
        reads = [in_]
        writes = [out]
        if bias is not None:
            kw["bias"] = bias
            if not isinstance(bias, (int, float)):
                reads.append(bias)
        if scale is not None:
            kw["scale"] = scale
            if not isinstance(scale, (int, float)):
                reads.append(scale)
        if accum_out is not None:
            kw["accum_out"] = accum_out
            writes.append(accum_out)
        return self.op("act", lambda en: en.activation(out, in_, func, **kw), reads, writes)

    def tt(self, out, a, b, op, e="dve"):
        return self.op(e, lambda en: en.tensor_tensor(out, a, b, op), [a, b], [out])

    def ts(self, out, a, s1, s2=None, op0=ALU.mult, op1=None, e="dve", accum_out=None):
        reads = [a]
        if not isinstance(s1, (int, float)):
            reads.append(s1)
        if s2 is not None and not isinstance(s2, (int, float)):
            reads.append(s2)
        kw = {}
        writes = [out]
        if op1 is not None:
            kw["op1"] = op1
        if accum_out is not None:
            kw["accum_out"] = accum_out
            writes.append(accum_out)
        return self.op(e, lambda en: en.tensor_scalar(out, a, s1, s2, op0, **kw), reads, writes)

    def stt(self, out, a, s, b, op0, op1, accum_out=None):
        reads = [a, b]
        if not isinstance(s, (int, float)):
            reads.append(s)
        kw = {}
        writes = [out]
        if accum_out is not None:
            kw["accum_out"] = accum_out
            writes.append(accum_out)
        return self.op("dve", lambda en: en.scalar_tensor_tensor(out, a, s, b, op0, op1, **kw), reads, writes)

    def scan(self, out, d0, d1, init, op0, op1):
        reads = [d0, d1]
        if not isinstance(init, (int, float)):
            reads.append(init)
        return self.op("dve", lambda en: en.tensor_tensor_scan(out, d0, d1, init, op0, op1), reads, [out])

    def copy(self, out, in_, e="dve"):
        if e == "act":
            return self.op("act", lambda en: en.copy(out, in_), [in_], [out])
        return self.op(e, lambda en: en.tensor_copy(out, in_), [in_], [out])

    def memset(self, out, val, e="dve"):
        return self.op(e, lambda en: en.memset(out, val), [], [out])

    def reduce(self, out, in_, op=ALU.add, axis=AX.X):
        return self.op("dve", lambda en: en.tensor_reduce(out, in_, axis, op), [in_], [out])

    def recip(self, out, in_):
        return self.op("dve", lambda en: en.reciprocal(out, in_), [in_], [out])


S = 2048
KC = 8
NG = 4
NT = 16
C_M2 = 3328
EPS = 1e-6


def bcl(ap, m):
    pat = [list(p) for p in ap.ap]
    return bass.AP(ap.tensor, ap.offset, pat + [[0, m]])


def bcm(ap, m):
    pat = [list(p) for p in ap.ap]
    return bass.AP(ap.tensor, ap.offset, [pat[0], [0, m]] + pat[1:])


def phase_m2(env):
    k = env["k"]; I = env["I"]; ar = env["ar"]; hT = env["hT"]; bank = env["bank"]; l = env["l"]
    load_w = env["load_w"]; load_cols = env["load_cols"]; DBG = env["DBG"]; evac = env["evac"]
    ident = env["ident"]; m_siu = env["m_siu"]; m_sl = env["m_sl"]; ones = env["ones"]; ybr_d = env["ybr_d"]
    bcast = env["bcast"]; epsc = env["epsc"]
    W = I["w_in"][l]
    ar.reset()
    BT = ar.alloc([2, S], BF16)
    CT = ar.alloc([2, S], BF16)
    xs_tok = ar.alloc([NT, 1024], BF16)
    B_tok = ar.alloc([NT, 256], BF16)
    wz = ar.alloc([KC, 1024], BF16)
    u = ar.alloc([S + 3])
    t1 = ar.alloc([S])
    wts = [ar.alloc([KC, 128], BF16), ar.alloc([KC, 128], BF16)]
    wdt = ar.alloc([KC, 16], BF16)
    cw = ar.alloc([48])
    cb = ar.alloc([12])
    dtb_bc = ar.alloc([16])
    A_bc = ar.alloc([16])
    D_bc = ar.alloc([16])
    ng_bc = ar.alloc([1024])
    sel127 = ar.alloc([128])
    dt_tok = ar.alloc([NT, 16])
    la_tok = ar.alloc([NT, 16])
    acum = ar.alloc([NT, 16])
    aend = ar.alloc([NT, 16])
    dec = ar.alloc([NT, 16])
    ea = ar.alloc([NT, 16])
    dte = ar.alloc([NT, 16])
    rhsD = ar.alloc([16, 128])
    E = ar.alloc([16, 128])
    G = ar.alloc([16, 128], BF16)
    CBm = ar.alloc([2, 128])
    xdt = ar.alloc([1024], BF16)
    xdte = ar.alloc([1024], BF16)
    y = ar.alloc([1024])
    y2 = ar.alloc([1024])
    zs = ar.alloc([1024])
    S32 = ar.alloc([2, 512])
    Sbf = ar.alloc([2, 512], BF16)
    ytT = ar.alloc([8, 128], BF16)
    ssq = ar.alloc([4])
    junk = ar.alloc([512])

    load_cols(cw, I["m2_conv_w"][l].rearrange("a b -> (a b)"), 48)
    load_cols(cb, I["m2_conv_b"][l], 12)
    bcast(dtb_bc, I["m2_dt_bias"][l], 16)
    bcast(A_bc, I["m2_a_log"][l], 16)
    bcast(D_bc, I["m2_d"][l], 16)
    bcast(ng_bc, I["m2_norm_g"][l], 1024)
    k.act(A_bc, A_bc, AF.Exp)
    k.ts(A_bc, A_bc, -1.0, None, ALU.mult)
    k.op("pool", lambda en: en.affine_select(sel127, ones[:], [[0, 128]], ALU.is_equal, 0.0, base=-127,
                                               channel_multiplier=1), [ones[:]], [sel127])
    load_w(wz, W[:, C_M2:C_M2 + 1024])
    load_w(wdt, W[:, C_M2 + 2560:C_M2 + 2576])
    k.memset(u[:, 0:3], 0.0)

    for cc in range(12):
        wt = wts[cc % 2]
        c0 = C_M2 + 1024 + cc * 128
        load_w(wt, W[:, c0:c0 + 128])
        for n in range(NG):
            pb = bank()
            for kc in range(KC):
                k.mm(pb, wt[:, kc, :], hT[:, kc, n * 512:(n + 1) * 512], start=(kc == 0), stop=(kc == KC - 1))
            k.copy(u[:, 3 + n * 512:3 + (n + 1) * 512], pb, e="act")
            k.act(t1[:, n * 512:(n + 1) * 512], pb, AF.Identity, bias=cb[:, cc:cc + 1], scale=cw[:, 36 + cc:37 + cc])
        k.stt(t1, u[:, 2:S + 2], cw[:, 24 + cc:25 + cc], t1, ALU.mult, ALU.add)
        k.stt(t1, u[:, 1:S + 1], cw[:, 12 + cc:13 + cc], t1, ALU.mult, ALU.add)
        k.stt(t1, u[:, 0:S], cw[:, cc:cc + 1], t1, ALU.mult, ALU.add)
        if cc < 10:
            k.act(t1, t1, AF.Silu)
            if cc >= 8:
                k.copy(BT[:, cc - 8, :], t1, e="pool")
            for tq in range(4):
                pb = bank()
                for j in range(4):
                    t = tq * 4 + j
                    k.transpose(pb[:, j * 128:(j + 1) * 128], t1[:, t * 128:(t + 1) * 128], ident[:])
                src = pb.rearrange("p (a b) -> p a b", a=4)
                if cc < 8:
                    evac(xs_tok[:, tq * 4:tq * 4 + 4, cc * 128:(cc + 1) * 128], src)
                else:
                    evac(B_tok[:, tq * 4:tq * 4 + 4, (cc - 8) * 128:(cc - 7) * 128], src)
        else:
            k.act(CT[:, cc - 10, :], t1, AF.Silu)

    pb = bank()
    for t in range(NT):
        for kc in range(KC):
            k.mm(pb[:, t * 16:(t + 1) * 16], hT[:, kc, t * 128:(t + 1) * 128], wdt[:, kc, :],
                 start=(kc == 0), stop=(kc == KC - 1))
    dt2 = dt_tok.rearrange("p a b -> p (a b)")
    la2 = la_tok.rearrange("p a b -> p (a b)")
    k.tt(dt_tok, pb[:, 0:256].rearrange("p (a b) -> p a b", a=NT), bcm(dtb_bc, NT), ALU.add)
    k.act(dt2, dt2, AF.Exp)
    k.act(dt2, dt2, AF.Ln, bias=1.0)
    k.tt(la_tok, dt_tok, bcm(A_bc, NT), ALU.mult)
    pb = bank()
    k.mm(pb[:, 0:256], m_siu[:], la2)
    ac2 = acum.rearrange("p a b -> p (a b)")
    k.copy(ac2, pb[:, 0:256])
    pb = bank()
    k.mm(pb[:, 0:256], sel127, ac2)
    ae2 = aend.rearrange("p a b -> p (a b)")
    k.copy(ae2, pb[:, 0:256])
    k.act(dec.rearrange("p a b -> p (a b)"), ae2, AF.Exp)
    k.act(ea.rearrange("p a b -> p (a b)"), ac2, AF.Exp)
    k.tt(ae2, ae2, ac2, ALU.subtract)
    k.act(dte.rearrange("p a b -> p (a b)"), ae2, AF.Exp)

    psum = env["psum"]

    def PB(i):
        return psum[:, i, :]

    def xs3_(t):
        return xs_tok[:, t, :].rearrange("p (h d) -> p h d", h=16)

    def prologue(t):
        ts_ = slice(t * 128, (t + 1) * 128)
        pb = PB(0)
        for g in range(2):
            k.mm(pb[:, g * 128:(g + 1) * 128], BT[:, g, ts_], CT[:, g, ts_])
        for g in range(2):
            k.tt(CBm[:, g, :], pb[:, g * 128:(g + 1) * 128], m_siu[:], ALU.mult)
        k.tt(rhsD, bcm(m_siu[:], 16), bcl(la_tok[:, t, :], 128), ALU.mult)
        for b4 in range(4):
            pb = PB(1 + (b4 % 2))
            k.mm(pb, m_sl[:], rhsD[:, b4 * 4:b4 * 4 + 4, :].rearrange("p a b -> p (a b)"))
            k.act(E[:, b4 * 4:b4 * 4 + 4, :].rearrange("p a b -> p (a b)"), pb, AF.Exp)
        for g in range(2):
            k.tt(G[:, g * 8:(g + 1) * 8, :], E[:, g * 8:(g + 1) * 8, :], bcm(CBm[:, g, :], 8), ALU.mult)
        k.tt(xdt.rearrange("p (h d) -> p h d", h=16), xs3_(t), bcl(dt_tok[:, t, :], 64), ALU.mult)
        if t < NT - 1:
            k.tt(xdte.rearrange("p (h d) -> p h d", h=16), xdt.rearrange("p (h d) -> p h d", h=16),
                 bcl(dte[:, t, :], 64), ALU.mult, e="pool")

    prologue(0)
    for t in range(NT):
        ts_ = slice(t * 128, (t + 1) * 128)
        py = [PB(3), PB(4)]
        for h in range(16):
            k.mm(py[h // 8][:, (h % 8) * 64:(h % 8 + 1) * 64], G[:, h, :], xdt[:, h * 64:(h + 1) * 64])
        po = [PB(5), PB(6)]
        if t > 0:
            for g in range(2):
                k.mm(po[g], CT[:, g, ts_], Sbf[:, g, :])
        for half in range(2):
            pz = PB(7) if half == 0 else PB(1)
            for kc in range(KC):
                k.mm(pz, hT[:, kc, ts_], wz[:, kc, half * 512:(half + 1) * 512], start=(kc == 0), stop=(kc == KC - 1))
            k.act(zs[:, half * 512:(half + 1) * 512], pz, AF.Silu)
        if t > 0:
            for g in range(2):
                k.tt(y2[:, g * 512:(g + 1) * 512].rearrange("p (h d) -> p h d", h=8),
                     po[g].rearrange("p (h d) -> p h d", h=8), bcl(ea[:, t, g * 8:(g + 1) * 8], 64), ALU.mult)
                k.tt(y[:, g * 512:(g + 1) * 512], y2[:, g * 512:(g + 1) * 512], py[g], ALU.add)
        else:
            for g in range(2):
                k.copy(y[:, g * 512:(g + 1) * 512], py[g])
        pst = [PB(3), PB(4)]
        if t < NT - 1:
            for g in range(2):
                k.mm(pst[g], B_tok[:, t, g * 128:(g + 1) * 128], xdte[:, g * 512:(g + 1) * 512])
        if t + 1 < NT:
            prologue(t + 1)
        if t < NT - 1:
            for g in range(2):
                if t == 0:
                    k.copy(S32[:, g, :], pst[g])
                else:
                    k.tt(S32[:, g, :].rearrange("p (h d) -> p h d", h=8),
                         S32[:, g, :].rearrange("p (h d) -> p h d", h=8),
                         bcl(dec[:, t, g * 8:(g + 1) * 8], 64), ALU.mult)
                    k.tt(S32[:, g, :], S32[:, g, :], pst[g], ALU.add)
                k.copy(Sbf[:, g, :], S32[:, g, :], e="act")
        k.tt(y2.rearrange("p (h d) -> p h d", h=16), xs3_(t), bcl(D_bc, 64), ALU.mult, e="pool")
        k.tt(y, y, y2, ALU.add)
        k.tt(y, y, zs, ALU.mult)
        for g in range(2):
            k.act(junk, y[:, g * 512:(g + 1) * 512], AF.Square, accum_out=ssq[:, g:g + 1])
        k.act(ssq[:, 2:4], ssq[:, 0:2], AF.Sqrt, bias=epsc[:], scale=1.0 / 512)
        k.recip(ssq[:, 2:4], ssq[:, 2:4])
        for g in range(2):
            k.stt(y[:, g * 512:(g + 1) * 512], y[:, g * 512:(g + 1) * 512], ssq[:, 2 + g:3 + g],
                  ng_bc[:, g * 512:(g + 1) * 512], ALU.mult, ALU.mult)
        if t == 3:
            DBG("ym2_%d" % l, y, [128, 1024])
        for half in range(2):
            pb = PB(5 + half)
            for j in range(4):
                c = half * 4 + j
                k.transpose(pb[:, j * 128:(j + 1) * 128], y[:, c * 128:(c + 1) * 128], ident[:])
            evac(ytT[:, half * 4:half * 4 + 4, :], pb.rearrange("p (a b) -> p a b", a=4))
        k.dma("sp", ybr_d["m2"][:, :, ts_].rearrange("c p s -> p c s"), ytT)


S = 2048
KC = 8
NG = 4
NT = 16
C_SB = 1792


def phase_sb(env):
    k = env["k"]; I = env["I"]; ar = env["ar"]; hT = env["hT"]; bank = env["bank"]; l = env["l"]
    load_w = env["load_w"]; DBG = env["DBG"]; evac = env["evac"]
    identb = env["identb"]; m_sl = env["m_sl"]; ybr_d = env["ybr_d"]
    W = I["w_in"][l]
    ar.reset()
    qT = ar.alloc([4, S], BF16)
    kT = ar.alloc([4, S], BF16)
    v_tok = ar.alloc([NT, 512], BF16)
    ysT = ar.alloc([4, S], BF16)
    wraw = [ar.alloc([S]), ar.alloc([S])]
    wts = [w_.bitcast(BF16).rearrange("p (a b) -> p a b", a=KC) for w_ in wraw]
    SBUFS = []
    for s_ in range(2):
        SBUFS.append(dict(Eb=[ar.alloc([S]), wraw[s_]], R=ar.alloc([S]), Wd=ar.alloc([S]), att=ar.alloc([S], BF16),
                          attT=ar.alloc([NT, 128], BF16)))
    Eb = SBUFS[0]["Eb"][0]
    onesS = ar.alloc([S])
    k.memset(onesS, 1.0)
    for which, dst in ((0, qT), (1, kT)):
        wt = wts[which]
        load_w(wt, W[:, C_SB + which * 512:C_SB + (which + 1) * 512])
        for c in range(4):
            for n in range(NG):
                pb = bank()
                for kc in range(KC):
                    k.mm(pb, wt[:, kc, c * 128:(c + 1) * 128], hT[:, kc, n * 512:(n + 1) * 512],
                         start=(kc == 0), stop=(kc == KC - 1))
                evac(dst[:, c, n * 512:(n + 1) * 512], pb)
    wt = wts[0]
    load_w(wt, W[:, C_SB + 1024:C_SB + 1536])
    for t in range(NT):
        pb = bank()
        for kc in range(KC):
            k.mm(pb, hT[:, kc, t * 128:(t + 1) * 128], wt[:, kc, :], start=(kc == 0), stop=(kc == KC - 1))
        evac(v_tok[:, t, :], pb)

    def unit(h, i, B, par):
        Eb = B["Eb"][par]; R = B["R"]; L = B["Wd"]; att = B["att"]; attT = B["attT"]
        c = h // 2
        po = (h % 2) * 64
        N = 128 * (i + 1)
        nb = (N + 511) // 512
        for b in range(nb):
            cols = min(512, N - b * 512)
            pb = bank()
            k.mm(pb[:, 0:cols], qT[po:po + 64, c, i * 128:(i + 1) * 128], kT[po:po + 64, c, b * 512:b * 512 + cols])
            k.act(Eb[:, b * 512:b * 512 + cols], pb[:, 0:cols], AF.Exp, scale=0.125)
        yield
        k.act(L[:, 0:N], Eb[:, 0:N], AF.Ln, bias=1.0)
        k.tt(L[:, N - 128:N], L[:, N - 128:N], m_sl[:], ALU.mult, e="pool")
        yield
        k.scan(R[:, N - 1::-1] if N > 128 else R[:, 127::-1], onesS[:, 0:N],
               L[:, N - 1::-1] if N > 128 else L[:, 127::-1], 0.0, ALU.mult, ALU.add)
        yield
        k.act(R[:, 0:N], R[:, 0:N], AF.Exp, scale=-1.0)
        yield
        k.tt(att[:, 0:N], Eb[:, 0:N], R[:, 0:N], ALU.mult)
        k.tt(att[:, N - 128:N], att[:, N - 128:N], m_sl[:], ALU.mult, e="pool")
        yield
        for g4 in range((i + 4) // 4):
            nblk = min(4, i + 1 - g4 * 4)
            pb = bank().bitcast(BF16)
            for j in range(nblk):
                kb = g4 * 4 + j
                k.transpose(pb[:, j * 128:(j + 1) * 128], att[:, kb * 128:(kb + 1) * 128], identb[:])
            evac(attT[:, g4 * 4:g4 * 4 + nblk, :], pb[:, 0:nblk * 128].rearrange("p (a b) -> p a b", a=nblk))
            if g4 % 2 == 1:
                yield
        yield
        po_b = bank()
        for kb in range(i + 1):
            k.mm(po_b[po:po + 64, 0:128], v_tok[:, kb, h * 64:(h + 1) * 64], attT[:, kb, :],
                 start=(kb == 0), stop=(kb == i))
        evac(ysT[po:po + 64, c, i * 128:(i + 1) * 128], po_b[po:po + 64, 0:128])

    pending = [(2 * c2, 2 * c2 + 1, i) for c2 in range(4) for i in range(NT)]
    active = []
    step = 0
    npair = 0
    late = []
    while pending or active or late:
        if late:
            active.append(late.pop(0))
        if pending and step % 3 == 0 and len(active) < 6:
            hA, hB, i_ = pending.pop(0)
            active.append(unit(hA, i_, SBUFS[0], npair % 2))
            late.append(unit(hB, i_, SBUFS[1], npair % 2))
            npair += 1
        for g_ in list(active):
            try:
                next(g_)
            except StopIteration:
                active.remove(g_)
        step += 1
    if env["dbg"] and ("ysb_%d" % l) in env["dbg"]:
        k.copy(Eb, ysT[:, 0, :])
        DBG("ysb_%d" % l, Eb, [128, S])
    for c in range(4):
        k.dma("sp" if c % 2 == 0 else "act", ybr_d["sb"][c], ysT[:, c, :])


S = 2048
KC = 8
NG = 4
NT = 16
GN_EPS = 64e-5
NEG_E05 = -0.6065306597126334


def phase_rw(env):
    k = env["k"]; I = env["I"]; ar = env["ar"]; hT = env["hT"]; bank = env["bank"]; l = env["l"]
    load_w = env["load_w"]; load_cols = env["load_cols"]; DBG = env["DBG"]; evac = env["evac"]
    ident = env["ident"]; identb = env["identb"]; m_su = env["m_su"]; m_siu = env["m_siu"]; m_sl = env["m_sl"]
    blk2 = env["blk2"]; ybr_d = env["ybr_d"]; bcast = env["bcast"]
    W = I["w_in"][l]
    ar.reset()
    F = [ar.alloc([S]) for _ in range(7)]
    F.append(ar.alloc([S + 4]))
    u = F[7]
    Rt = ar.alloc([S], BF16); At = ar.alloc([S], BF16); Bt = ar.alloc([S], BF16); Kt = ar.alloc([S], BF16)
    Bhf = ar.alloc([S], BF16); Khf = ar.alloc([S], BF16); Vb = ar.alloc([S], BF16)
    gT = ar.alloc([S], BF16); bonus = ar.alloc([S], BF16)
    Atok = ar.alloc([NT, 128], BF16); Bhtok = ar.alloc([NT, 128], BF16)
    Khtok = ar.alloc([NT, 128], BF16); Vtok = ar.alloc([NT, 128], BF16)
    walo = ar.alloc([S], BF16); gsig = ar.alloc([S], BF16)
    cmask = ar.alloc([S], BF16)
    yrwT = ar.alloc([S], BF16)
    w2a2 = ar.alloc([512], BF16); g2w = ar.alloc([512], BF16)
    lng = ar.alloc([512]); lnb = ar.alloc([512])
    wts = [ar.alloc([KC, 128], BF16), ar.alloc([KC, 128], BF16)]
    pc = ar.alloc([40])
    wC = ar.alloc([NT])
    msu32 = ar.alloc([128]); msl32 = ar.alloc([128]); msl64o = ar.alloc([128]); msl128o = ar.alloc([128])
    D32 = ar.alloc([128]); D64 = ar.alloc([128]); dtmp = ar.alloc([128])
    ones_ = env["ones"]
    for Dm, bs in ((D32, 32), (D64, 64)):
        for gb in range(128 // bs):
            cs2 = slice(gb * bs, (gb + 1) * bs)
            k.op("pool", lambda en: en.affine_select(dtmp[:, cs2], ones_[:, cs2], [[0, bs]], ALU.is_ge, 0.0,
                                                       base=-gb * bs, channel_multiplier=1), [ones_[:, cs2]], [dtmp[:, cs2]])
            k.op("pool", lambda en: en.affine_select(Dm[:, cs2], dtmp[:, cs2], [[0, bs]], ALU.is_ge, 0.0,
                                                       base=gb * bs + bs - 1, channel_multiplier=-1), [dtmp[:, cs2]], [Dm[:, cs2]])
    k.tt(msu32, D32, m_su[:], ALU.mult)
    k.tt(msl32, D32, m_sl[:], ALU.mult)
    k.tt(msl64o, D64, m_sl[:], ALU.mult)
    k.tt(msl128o, m_sl[:], msl64o, ALU.subtract)
    k.tt(msl64o, msl64o, msl32, ALU.subtract)
    Qz = [ar.alloc([128], BF16) for _ in range(4)]
    Pz = [ar.alloc([64], BF16) for _ in range(4)]
    fin = ar.alloc([128], BF16)
    for _b in Qz + Pz:
        k.memset(_b, 0.0)
    ST = ar.alloc([64], BF16)
    ynbs = [ar.alloc([128], BF16), ar.alloc([128], BF16)]
    gneps = ar.alloc([1])
    k.memset(gneps, GN_EPS)

    mu = pc[:, 0:14]
    load_cols(mu, I["rw_mu"][l], 14)
    om = pc[:, 14:28]
    k.ts(om, mu, -1.0, 1.0, ALU.mult, ALU.add)
    pc2 = ar.alloc([24])
    load_cols(pc2[:, 0:4], I["rw_w0"][l], 4)
    load_cols(pc2[:, 4:8], I["rw_a0"][l], 4)
    load_cols(pc2[:, 8:12], I["rw_k_k"][l], 4)
    load_cols(pc2[:, 12:16], I["rw_k_a"][l], 4)
    load_cols(pc2[:, 16:20], I["rw_r_k"][l], 4)
    k.ts(pc2[:, 20:24], pc2[:, 12:16], -1.0, 1.0, ALU.mult, ALU.add)
    bcast(lng, I["rw_ln_g"][l], 512)
    bcast(lnb, I["rw_ln_b"][l], 512)
    k.dma("pool", w2a2[0:64, :], I["rw_w2"][l])
    k.dma("pool", w2a2[64:128, :], I["rw_a2"][l])
    k.dma("pool", g2w, I["rw_g2"][l])
    k.memset(cmask, 1.0)
    k.memset(cmask[:, 0::128], 0.0)
    k.memset(u[:, 0:1], 0.0)

    wi = [0]

    def proj_lerp(col0, mucol, dst):
        wt = wts[wi[0] % 2]
        wi[0] += 1
        load_w(wt, W[:, col0:col0 + 128])
        k.memset(u[:, 0:1], 0.0)
        for n in range(NG):
            pb = bank()
            for kc in range(KC):
                k.mm(pb, wt[:, kc, :], hT[:, kc, n * 512:(n + 1) * 512], start=(kc == 0), stop=(kc == KC - 1))
            k.copy(u[:, 1 + n * 512:1 + (n + 1) * 512], pb, e="act")
            k.act(dst[:, n * 512:(n + 1) * 512], pb, AF.Identity, scale=om[:, mucol:mucol + 1])
        k.stt(dst, u[:, 0:S], mu[:, mucol:mucol + 1], dst, ALU.mult, ALU.add)

    proj_lerp(1536, 12, F[0])
    k.act(F[0][0:64, :], F[0][0:64, :], AF.Tanh)
    k.copy(walo, F[0])
    proj_lerp(1664, 13, F[0])
    k.act(gsig, F[0], AF.Sigmoid)

    for c in range(4):
        cs_ = slice(c * 128, (c + 1) * 128)
        r32, k32, v32 = F[0], F[1], F[2]
        proj_lerp(c * 128, c, r32)
        proj_lerp(512 + c * 128, 4 + c, k32)
        proj_lerp(1024 + c * 128, 8 + c, v32)
        lws, lw, a32, g32 = F[3], F[4], F[5], F[6]
        for n in range(NG):
            ns = slice(n * 512, (n + 1) * 512)
            pb = bank()
            k.mm(pb, w2a2[0:64, cs_], walo[0:64, ns])
            k.act(lws[:, ns], pb, AF.Sigmoid, bias=pc2[:, c:c + 1])
            pb = bank()
            k.mm(pb, w2a2[64:128, cs_], walo[64:128, ns])
            k.act(a32[:, ns], pb, AF.Sigmoid, bias=pc2[:, 4 + c:5 + c])
            pb = bank()
            k.mm(pb, g2w[:, cs_], gsig[:, ns])
            k.copy(gT[:, ns], pb, e="act")
        k.ts(lws, lws, NEG_E05, None, ALU.mult)
        k.copy(Vb, v32, e="pool")
        kk = F[6]
        k.ts(kk, k32, pc2[:, 8 + c:9 + c], None, ALU.mult)
        k.act(u[:, 0:S], kk, AF.Square)
        for n in range(NG):
            ns = slice(n * 512, (n + 1) * 512)
            pb = bank()
            k.mm(pb, blk2[:], u[:, ns])
            k.ts(u[:, ns], pb, 1e-24, None, ALU.max)
        k.act(u[:, 0:S], u[:, 0:S], AF.Sqrt)
        k.recip(u[:, 0:S], u[:, 0:S])
        k.tt(kk, kk, u[:, 0:S], ALU.mult)
        kp = F[7][:, 0:S]
        k.ts(kp, a32, pc2[:, 12 + c:13 + c], pc2[:, 20 + c:21 + c], ALU.mult, ALU.add)
        k.tt(kp, kp, k32, ALU.mult)
        beta = F[1]
        k.tt(beta, kk, a32, ALU.mult)
        tmp = F[5]
        k.tt(tmp, r32, kp, ALU.mult)
        k.ts(tmp, tmp, pc2[:, 16 + c:17 + c], None, ALU.mult)
        for n in range(NG):
            ns = slice(n * 512, (n + 1) * 512)
            pb = bank()
            k.mm(pb, blk2[:], tmp[:, ns])
            k.tt(bonus[:, ns], pb, v32[:, ns], ALU.mult)
        k.scan(lw, cmask, lws, 0.0, ALU.mult, ALU.add)
        lwx = F[3]
        k.tt(lwx, lw, lws, ALU.subtract)
        e = F[5]
        k.act(e, lw, AF.Exp)
        k.tt(Rt, r32, e, ALU.mult)
        k.copy(wC, e[:, 127::128])
        k.act(e, lwx, AF.Exp)
        k.stt(At, kk, -1.0, e, ALU.mult, ALU.mult)
        k.act(e, lw, AF.Exp, scale=-1.0)
        k.tt(Bt, beta, e, ALU.mult)
        k.tt(Kt, kp, e, ALU.mult)
        e3 = F[0]
        k.tt(e3.rearrange("p (a b) -> p a b", a=NT), bcl(lw[:, 127::128], 128),
             lw.rearrange("p (a b) -> p a b", a=NT), ALU.subtract)
        k.act(e3, e3, AF.Exp)
        k.tt(Bhf, beta, e3, ALU.mult)
        k.tt(Khf, kp, e3, ALU.mult)
        for src, dst in ((At, Atok), (Bhf, Bhtok), (Khf, Khtok), (Vb, Vtok)):
            for tq in range(4):
                pb = bank().bitcast(BF16)
                for j in range(4):
                    t = tq * 4 + j
                    k.transpose(pb[:, j * 128:(j + 1) * 128], src[:, t * 128:(t + 1) * 128], identb[:])
                evac(dst[:, tq * 4:tq * 4 + 4, :], pb[:, 0:512].rearrange("p (a b) -> p a b", a=4))

        if not hasattr(k, "rwch"):
            k.rwch = k.sb("rwch", [128, 4 * 9 * 128], F32)
        carved = []

        def carve(Fb):
            o = [0]

            def f32(n):
                v = Fb[:, o[0]:o[0] + n]
                o[0] += n
                return v

            def b16(n):
                v = Fb[:, o[0]:o[0] + n // 2].bitcast(BF16)
                o[0] += n // 2
                return v
            B = {}
            si = len(carved)
            carved.append(1)
            base = si * 9 * 128
            ch = [k.rwch[:, base + j_ * 128:base + (j_ + 1) * 128] for j_ in range(9)]
            B["X"] = [ch[0], ch[1]]
            B["XT"] = [ch[2], ch[3]]
            for j_, nm in enumerate(("T", "TT", "M1", "No64", "No128")):
                B[nm] = ch[4 + j_]
            for nm in ("Tb", "Mbr", "Mkr", "Nkb", "AZ", "AV"):
                B[nm] = b16(128)
            B["yn"] = f32(64)
            B["st"] = f32(8)
            B["junk"] = f32(64)
            B["yraw"] = f32(64)
            return B

        BS = [carve(F[3]), carve(F[4]), carve(F[5]), carve(F[6])]

        def unit(hh, tc):
            B = BS[hh + 2 * (tc % 2)]
            qi = hh + 2 * (tc % 2)
            ynb = ynbs[tc % 2]
            X = B["X"]; XT = B["XT"]; T = B["T"]; TT = B["TT"]; M1 = B["M1"]; No64 = B["No64"]; No128 = B["No128"]
            Tb = B["Tb"]; Mbr = B["Mbr"]; Mkr = B["Mkr"]; Nkb = B["Nkb"]; AZ = B["AZ"]; AV = B["AV"]
            yn = B["yn"]; st = B["st"]; junk = B["junk"]
            ts_ = slice(tc * 128, (tc + 1) * 128)
            h = 2 * c + hh
            po = hh * 64
            ps_ = slice(po, po + 64)
            p1 = bank()
            k.mm(p1[:, 0:128], Bt[ps_, ts_], At[ps_, ts_])
            k.mm(p1[:, 128:256], Bt[ps_, ts_], Rt[ps_, ts_])
            k.mm(p1[:, 256:384], Kt[ps_, ts_], At[ps_, ts_])
            k.mm(p1[:, 384:512], Kt[ps_, ts_], Rt[ps_, ts_])
            p2 = bank()
            k.mm(p2[:, 0:128], At[ps_, ts_], Bt[ps_, ts_])
            yield
            R_ = lambda ap_: ap_.bitcast(F32R)
            k.tt(R_(X[0]), p1[:, 0:128], msu32, ALU.mult)
            k.tt(R_(XT[0]), p2[:, 0:128], msl32, ALU.mult)
            k.tt(R_(T), X[0], ident[:], ALU.add)
            k.tt(R_(No64), p2[:, 0:128], msl64o, ALU.mult)
            k.tt(R_(No128), p2[:, 0:128], msl128o, ALU.mult)
            k.tt(Mbr, p1[:, 128:256], m_siu[:], ALU.mult)
            k.tt(Nkb, p1[:, 256:384], m_su[:], ALU.mult)
            k.tt(Mkr, p1[:, 384:512], m_siu[:], ALU.mult)
            yield
            cur = 0
            for lvl in range(4):
                nxt = 1 - cur
                pa = bank()
                if lvl < 3:
                    k.mm(pa[:, 0:128], R_(XT[cur]), R_(X[cur]))
                k.mm(pa[:, 128:256], R_(X[cur]), R_(XT[cur]))
                yield
                k.copy(R_(XT[nxt]), pa[:, 128:256], e="act")
                if lvl < 3:
                    k.copy(R_(X[nxt]), pa[:, 0:128], e="act")
                pt_ = bank()
                k.mm(pt_[:, 0:128], R_(XT[nxt]), R_(T))
                yield
                k.tt(R_(T), T, pt_[:, 0:128], ALU.add)
                cur = nxt
            for mi, No in enumerate((No64, No128)):
                ptt = bank()
                k.transpose(ptt[:, 0:128], T, ident[:])
                pm = bank()
                k.mm(pm[:, 0:128], R_(No), R_(T))
                yield
                k.copy(R_(TT), ptt[:, 0:128], e="act")
                k.copy(R_(M1), pm[:, 0:128], e="act")
                pt_ = bank()
                k.mm(pt_[:, 0:128], R_(TT), R_(M1))
                if mi == 1:
                    pzt = bank()
                    k.mm(pzt[:, 0:64], Nkb, Vtok[:, tc, ps_])
                yield
                if mi == 0:
                    k.tt(R_(T), T, pt_[:, 0:128], ALU.add)
                else:
                    k.tt(Tb, T, pt_[:, 0:128], ALU.add)
            k.copy(AZ[:, 64:128], pzt[:, 0:64], e="act")
            k.copy(AZ[:, 0:64], Atok[:, tc, ps_], e="pool")
            pav = bank()
            k.mm(pav[:, 0:128], Tb, AZ)
            yield
            k.copy(AV, pav[:, 0:128], e="act")
            pq = bank()
            k.mm(pq[ps_, 0:128], AV[:, 0:64], Mbr)
            pp = bank()
            k.mm(pp[ps_, 0:64], AV[:, 0:64], Bhtok[:, tc, ps_])
            yield
            k.tt(Qz[qi][ps_, :], pq[ps_, 0:128], Rt[ps_, ts_], ALU.add)
            k.stt(Pz[qi][ps_, :], ident[ps_, ps_], wC[ps_, tc:tc + 1], pp[ps_, 0:64], ALU.mult, ALU.add)
            py = bank()
            if tc > 0:
                k.mm(py[:, 0:64], Qz[qi], ST, start=True, stop=False)
            k.mm(py[:, 0:64], Mbr, AV[:, 64:128], start=(tc == 0), stop=False)
            k.mm(py[:, 0:64], Mkr, Vtok[:, tc, ps_], start=False, stop=True)
            if tc < NT - 1:
                pcn = bank()
                if tc > 0:
                    k.mm(pcn[ps_, 0:64], Pz[qi], ST, start=True, stop=False)
                k.mm(pcn[ps_, 0:64], Bhtok[:, tc, ps_], AV[:, 64:128], start=(tc == 0), stop=False)
                k.mm(pcn[ps_, 0:64], Khtok[:, tc, ps_], Vtok[:, tc, ps_], start=False, stop=True)
                k.copy(ST[ps_, :], pcn[ps_, 0:64], e="act")
            yield
            yraw = B["yraw"]
            k.act(yraw, py[:, 0:64], AF.Copy, accum_out=st[:, 0:1])
            k.ts(st[:, 1:2], st[:, 0:1], -1.0 / 64, None, ALU.mult)
            k.act(junk, py[:, 0:64], AF.Square, bias=st[:, 1:2], accum_out=st[:, 2:3])
            yield
            k.act(st[:, 3:4], st[:, 2:3], AF.Sqrt, bias=gneps, scale=1.0 / 64)
            k.recip(st[:, 4:5], st[:, 3:4])
            k.ts(yn, yraw, st[:, 1:2], st[:, 4:5], ALU.add, ALU.mult)
            k.tt(yn, yn, lng[:, h * 64:(h + 1) * 64], ALU.mult, e="pool")
            k.tt(ynb[:, po:po + 64], yn, lnb[:, h * 64:(h + 1) * 64], ALU.add, e="pool")

        for tcp in range(0, NT, 2):
            waiting = [unit(0, tcp), unit(1, tcp), unit(0, tcp + 1), unit(1, tcp + 1)]
            active = []
            while waiting or active:
                if waiting:
                    active.append(waiting.pop(0))
                for g_ in list(active):
                    try:
                        next(g_)
                    except StopIteration:
                        active.remove(g_)
            for tc in (tcp, tcp + 1):
                ts_ = slice(tc * 128, (tc + 1) * 128)
                ptr = bank().bitcast(BF16)
                k.transpose(ptr[:, 0:128], ynbs[tc % 2], identb[:])
                k.tt(fin, ptr[:, 0:128], bonus[:, ts_], ALU.add)
                k.tt(yrwT[:, ts_], fin, gT[:, ts_], ALU.mult, e="pool")
        k.dma("sp", ybr_d["rw"][c], yrwT)
        if c == 0 and env["dbg"] and ("yrw_%d" % l) in env["dbg"]:
            k.copy(F[0], yrwT)
            DBG("yrw_%d" % l, F[0], [128, S])


S = 2048
KC = 8
NG = 4
DFF = 2816
C_GATE = 5904


def phase_merge(env):
    k = env["k"]; I = env["I"]; ar = env["ar"]; hT = env["hT"]; bank = env["bank"]; l = env["l"]
    mod = env["mod"]; xT_d = env["xT_d"]; ybr_d = env["ybr_d"]; load_w = env["load_w"]; DBG = env["DBG"]
    ar.reset()
    yb = {"rw": ar.alloc([4, S], BF16), "sb": ar.alloc([4, S], BF16), "m2": ar.alloc([8, S], BF16)}
    mergedT = ar.alloc([8, S], BF16)
    wg = [ar.alloc([KC, 384], BF16), ar.alloc([KC, 384], BF16)]
    wo = [ar.alloc([16, 128], BF16), ar.alloc([16, 128], BF16)]
    gsb = [ar.alloc([512]), ar.alloc([512])]
    acc = ar.alloc([512])
    xc = [ar.alloc([S]), ar.alloc([S])]
    for name, nch in (("rw", 4), ("sb", 4), ("m2", 8)):
        for c in range(nch):
            k.dma("sp" if c % 2 == 0 else "act", yb[name][:, c, :], ybr_d[name][c])
    wsrc = {"rw": I["rw_wo"][l], "sb": I["sb_wo"][l], "m2": I["m2_wo"][l]}
    names = ("rw", "sb", "m2")
    kcs = (4, 4, 8)
    offs = (0, 4, 8)
    gi = 0
    for f in range(8):
        wgt = wg[f % 2]
        wot = wo[f % 2]
        for b in range(3):
            load_w(wgt[:, :, b * 128:(b + 1) * 128],
                   I["w_in"][l][:, C_GATE + b * 1024 + f * 128:C_GATE + b * 1024 + (f + 1) * 128])
            load_w(wot[:, offs[b]:offs[b] + kcs[b], :], wsrc[names[b]][:, f * 128:(f + 1) * 128])
        for n in range(NG):
            ns = slice(n * 512, (n + 1) * 512)
            for b in range(3):
                pg = bank()
                for kc in range(KC):
                    k.mm(pg, wgt[:, kc, b * 128:(b + 1) * 128], hT[:, kc, ns], start=(kc == 0), stop=(kc == KC - 1))
                g = gsb[gi % 2]
                gi += 1
                k.act(g, pg, AF.Sigmoid)
                py = bank()
                for kc in range(kcs[b]):
                    k.mm(py, wot[:, offs[b] + kc, :], yb[names[b]][:, kc, ns], start=(kc == 0), stop=(kc == kcs[b] - 1))
                if b == 0:
                    k.tt(acc, g, py, ALU.mult)
                elif b == 1:
                    k.tt(g, g, py, ALU.mult)
                    k.tt(acc, acc, g, ALU.add)
                else:
                    k.tt(g, g, py, ALU.mult)
                    k.tt(mergedT[:, f, ns], acc, g, ALU.add)
    wt2 = [wg[0][:, :, 0:128], wg[1][:, :, 0:128]]
    for f in range(8):
        wt = wt2[f % 2]
        load_w(wt, I["w_out"][l][:, f * 128:(f + 1) * 128])
        x = xc[f % 2]
        k.dma("sp", x, xT_d[f])
        for n in range(NG):
            ns = slice(n * 512, (n + 1) * 512)
            pb = bank()
            for kc in range(KC):
                k.mm(pb, wt[:, kc, :], mergedT[:, kc, ns], start=(kc == 0), stop=(kc == KC - 1))
            k.stt(x[:, ns], pb, mod[:, 16 + f:17 + f], x[:, ns], ALU.mult, ALU.add)
        k.dma("sp", xT_d[f], x)
        if f == 0:
            DBG("x1_%d" % l, x, [128, S])


def phase_ffn(env):
    k = env["k"]; I = env["I"]; ar = env["ar"]; hT = env["hT"]; bank = env["bank"]; l = env["l"]
    mod = env["mod"]; xT_d = env["xT_d"]; load_w = env["load_w"]; load_cols = env["load_cols"]; DBG = env["DBG"]
    ar.reset()
    aT = ar.alloc([22, S], BF16)
    wts = [ar.alloc([KC, 256], BF16), ar.alloc([KC, 256], BF16), ar.alloc([KC, 256], BF16)]
    u = [ar.alloc([S + 2]), ar.alloc([S + 2])]
    tgs = [ar.alloc([S]), ar.alloc([S])]
    tvs = [ar.alloc([S]), ar.alloc([S])]
    cw = ar.alloc([132])
    cb = ar.alloc([44])
    load_cols(cw, I["ffn_conv_w"][l].rearrange("a b -> (a b)"), 132)
    load_cols(cb, I["ffn_conv_b"][l], 44)
    k.memset(u[0][:, 0:2], 0.0)
    k.memset(u[1][:, 0:2], 0.0)
    def ld_up(jj):
        w_ = wts[jj % 3]
        load_w(w_[:, :, 0:128], I["ffn_w_up"][l][:, jj * 128:(jj + 1) * 128])
        load_w(w_[:, :, 128:256], I["ffn_w_up"][l][:, DFF + jj * 128:DFF + (jj + 1) * 128])

    ld_up(0)
    ld_up(1)
    for j in range(22):
        wt = wts[j % 3]
        if j + 2 < 22:
            ld_up(j + 2)
        tg = tgs[j % 2]
        tv = tvs[j % 2]
        for half in range(2):
            cc = j + 22 * half
            uu = u[half]
            dst = tg if half == 0 else tv
            for n in range(NG):
                pb = bank()
                for kc in range(KC):
                    k.mm(pb, wt[:, kc, half * 128:(half + 1) * 128], hT[:, kc, n * 512:(n + 1) * 512],
                         start=(kc == 0), stop=(kc == KC - 1))
                k.copy(uu[:, 2 + n * 512:2 + (n + 1) * 512], pb, e="act")
                k.act(dst[:, n * 512:(n + 1) * 512], pb, AF.Identity, bias=cb[:, cc:cc + 1], scale=cw[:, 88 + cc:89 + cc])
            k.stt(dst, uu[:, 1:S + 1], cw[:, 44 + cc:45 + cc], dst, ALU.mult, ALU.add)
            k.stt(dst, uu[:, 0:S], cw[:, cc:cc + 1], dst, ALU.mult, ALU.add)
        k.act(tg, tg, AF.Silu)
        k.tt(aT[:, j, :], tg, tv, ALU.mult)
    if env["dbg"] and ("aT%d" % l) in env["dbg"]:
        k.copy(tgs[0], aT[:, 5, :])
        DBG("aT%d" % l, tgs[0], [128, S])
    wd = [tgs[0].bitcast(BF16)[:, 0:22 * 128].rearrange("p (a b) -> p a b", a=22),
          tvs[0].bitcast(BF16)[:, 0:22 * 128].rearrange("p (a b) -> p a b", a=22)]
    xc = [u[0][:, 0:S], u[1][:, 0:S]]
    for f in range(8):
        wt = wd[f % 2]
        load_w(wt, I["ffn_w_down"][l][:, f * 128:(f + 1) * 128])
        x = xc[f % 2]
        k.dma("sp", x, xT_d[f])
        for n in range(NG):
            ns = slice(n * 512, (n + 1) * 512)
            pb = bank()
            for kc in range(22):
                k.mm(pb, wt[:, kc, :], aT[:, kc, ns], start=(kc == 0), stop=(kc == 21))
            k.stt(x[:, ns], pb, mod[:, 40 + f:41 + f], x[:, ns], ALU.mult, ALU.add)
        k.dma("sp", xT_d[f], x)
        if f == 0:
            DBG("x2_%d" % l, x, [128, S])


L_ = 2
D = 1024
S = 2048
KC = 8
NT = 16
NG = 4
EPS = 1e-6
N_IN = 8976
C_RW = 0
C_SB = 1792
C_M2 = 3328
C_GATE = 5904
DFF = 2816

SHAPES = [
    ("x", [S, D]), ("c", [D]), ("ada_w", [L_, D, 6 * D]), ("ada_b", [L_, 6 * D]), ("norm1_g", [L_, D]),
    ("norm2_g", [L_, D]), ("w_in", [L_, D, N_IN]), ("rw_mu", [L_, 1792]), ("rw_w0", [L_, 512]),
    ("rw_w2", [L_, 64, 512]), ("rw_a0", [L_, 512]), ("rw_a2", [L_, 64, 512]), ("rw_g2", [L_, 128, 512]),
    ("rw_k_k", [L_, 512]), ("rw_k_a", [L_, 512]), ("rw_r_k", [L_, 512]), ("rw_ln_g", [L_, 512]),
    ("rw_ln_b", [L_, 512]), ("rw_wo", [L_, 512, D]), ("sb_wo", [L_, 512, D]), ("m2_conv_w", [L_, 4, 1536]),
    ("m2_conv_b", [L_, 1536]), ("m2_dt_bias", [L_, 16]), ("m2_a_log", [L_, 16]), ("m2_d", [L_, 16]),
    ("m2_norm_g", [L_, D]), ("m2_wo", [L_, D, D]), ("w_out", [L_, D, D]), ("ffn_w_up", [L_, D, 2 * DFF]),
    ("ffn_conv_w", [L_, 3, 2 * DFF]), ("ffn_conv_b", [L_, 2 * DFF]), ("ffn_w_down", [L_, DFF, D]),
    ("final_norm_g", [D]),
]


def prod(s):
    r = 1
    for v in s:
        r *= v
    return r


class Arena:
    def __init__(self, k, words):
        self.t = k.sb("arena", [128, words], F32)
        self.words = words
        self.off = 0

    def reset(self):
        self.off = 0

    def alloc(self, shape, dt=F32):
        n = prod(shape)
        w = n if dt == F32 else (n + 1) // 2
        w = (w + 3) // 4 * 4
        assert self.off + w <= self.words, ("arena overflow", self.off, w, self.words)
        v = self.t[:, self.off:self.off + w]
        self.off += w
        if dt == BF16:
            v = v.bitcast(BF16)
        v = v[:, 0:n]
        if len(shape) == 2:
            v = v.rearrange("p (a b) -> p a b", a=shape[0])
        elif len(shape) == 3:
            v = v.rearrange("p (a b c) -> p a b c", a=shape[0], b=shape[1])
        return v


def build(stop_after=None, nlayers=L_, dbg=None):
    k = KB()
    nc = k.nc
    I = {}
    for name, shape in SHAPES:
        I[name] = nc.dram_tensor(name, shape, F32, kind="ExternalInput").ap()
    out = nc.dram_tensor("out", [S, D], F32, kind="ExternalOutput").ap()
    dbg_out = {}

    def DBG(name, ap_sb, shape):
        if dbg is None or name not in dbg:
            return
        o = nc.dram_tensor("dbg_" + name, list(shape), F32, kind="ExternalOutput").ap()
        dbg_out[name] = o
        k.dma("sp", o, ap_sb, is_output=True)

    ar = Arena(k, 38 * 1024)
    hT = k.sb("hT", [128, KC, S], BF16)
    psum = k.ps("psum", [128, 8, 512], F32)
    ident = k.sb("ident", [128, 128], F32)
    identb = k.sb("identb", [128, 128], BF16)
    ones = k.sb("ones", [128, 128], F32)
    onesb = k.sb("onesb", [128, 128], BF16)
    m_su = k.sb("m_su", [128, 128], F32)
    m_siu = k.sb("m_siu", [128, 128], F32)
    m_sl = k.sb("m_sl", [128, 128], F32)
    blk2 = k.sb("blk2", [128, 128], F32)
    colst = k.sb("colst", [128, 128], F32)
    cs = k.sb("cs", [128, KC], F32)
    mod = k.sb("mod", [128, 48], F32)
    modb = k.sb("modb", [128, 48], F32)
    ncoef = k.sb("ncoef", [128, 4 * KC], F32)
    gcol = k.sb("gcol", [128, 3 * KC], F32)
    xT_d = k.dram("xT_d", [KC, 128, S], F32)
    ybr_d = {"rw": k.dram("yrw_d", [4, 128, S], BF16), "sb": k.dram("ysb_d", [4, 128, S], BF16),
             "m2": k.dram("ym2_d", [8, 128, S], BF16)}

    state = {"bank": 0, "ev": 0, "wq": 0}

    def bank():
        b = state["bank"]
        state["bank"] = (b + 1) % 8
        return psum[:, b, :]

    def evac(out_ap, in_ap):
        state["ev"] ^= 1
        if state["ev"]:
            k.copy(out_ap, in_ap, e="act")
        else:
            k.copy(out_ap, in_ap, e="dve")

    k.memset(ones[:], 1.0)
    k.copy(onesb[:], ones[:])

    def aff(dst, pattern_step, cmul, base, cmp):
        k.op("pool", lambda en: en.affine_select(dst, ones[:], [[pattern_step, 128]], cmp, 0.0, base=base,
                                                   channel_multiplier=cmul), [ones[:]], [dst])

    aff(ident[:], 1, -1, 0, ALU.is_equal)
    aff(m_su[:], 1, -1, 0, ALU.is_gt)
    aff(m_siu[:], 1, -1, 0, ALU.is_ge)
    aff(m_sl[:], -1, 1, 0, ALU.is_gt)
    k.copy(identb[:], ident[:])
    k.memset(blk2[:], 0.0)
    k.memset(blk2[0:64, 0:64], 1.0)
    k.memset(blk2[64:128, 64:128], 1.0)

    def load_cols(dst, src_flat, n, q="sp"):
        done = 0
        while done < n:
            m = min(128, n - done)
            k.dma(q, colst[0:m, :], src_flat[done * 128:(done + m) * 128].rearrange("(c p) -> c p", p=128))
            pb = bank()
            k.transpose(pb[:, 0:m], colst[0:m, :], ident[0:m, 0:m])
            k.copy(dst[:, done:done + m], pb[:, 0:m])
            done += m

    def bcast(dst, src_flat, n, q="sp"):
        k.dma(q, dst, src_flat.partition_broadcast(128))

    ar.reset()
    xs = ar.alloc([KC, S])
    for t in range(NT):
        xtile = ar.alloc([D]) if t == 0 else xtile
        k.dma("sp", xtile, I["x"][t * 128:(t + 1) * 128, :])
        for half in range(2):
            pb = bank()
            for j in range(4):
                c = half * 4 + j
                k.transpose(pb[:, j * 128:(j + 1) * 128], xtile[:, c * 128:(c + 1) * 128], ident[:])
            evac(xs[:, half * 4:half * 4 + 4, t * 128:(t + 1) * 128], pb.rearrange("p (a b) -> p a b", a=4))
    for c in range(KC):
        k.dma("sp" if c % 2 == 0 else "act", xT_d[c], xs[:, c, :])
    load_cols(cs[:], I["c"], KC)
    k.act(cs[:], cs[:], AF.Silu)
    load_cols(gcol[:, 16:24], I["final_norm_g"], KC)

    def norm_to_hT(acol, bcol, final=False):
        ar.reset()
        xin = ar.alloc([KC, S])
        sq = ar.alloc([S])
        rstd = ar.alloc([S])
        pbs = [bank() for _ in range(NG)]
        for c in range(KC):
            k.dma("sp", xin[:, c, :], xT_d[c])
            k.act(sq, xin[:, c, :], AF.Square)
            for n in range(NG):
                k.mm(pbs[n], ones[:], sq[:, n * 512:(n + 1) * 512], start=(c == 0), stop=(c == KC - 1))
        for n in range(NG):
            k.act(rstd[:, n * 512:(n + 1) * 512], pbs[n], AF.Sqrt, bias=epsc[:], scale=1.0 / D)
        DBG('sqrt', rstd, [128, S])
        k.recip(rstd, rstd)
        DBG('rstd', rstd, [128, S])
        DBG('xin', xin[:, 0, :], [128, S])
        for c in range(KC):
            k.tt(xin[:, c, :], xin[:, c, :], rstd, ALU.mult)
            if final:
                k.act(xin[:, c, :], xin[:, c, :], AF.Identity, scale=acol[:, c:c + 1])
            else:
                k.act(hT[:, c, :], xin[:, c, :], AF.Identity, bias=bcol[:, c:c + 1], scale=acol[:, c:c + 1])
        return xin

    epsc = k.sb("epsc", [128, 1], F32)
    k.memset(epsc[:], EPS)

    def load_w(dst, src_rows, q="pool"):
        k.dma(q, dst, src_rows.rearrange("(c p) n -> p c n", p=128))

    for l in range(nlayers):
        ar.reset()
        wts = [ar.alloc([KC, 512]) for _ in range(6)]
        load_cols(modb[:], I["ada_b"][l], 48)
        pbm = bank()

        def ld_ada(g_):
            k.dma(("sp", "act", "sp", "pool")[g_ % 4], wts[g_ % 6],
                  I["ada_w"][l][:, g_ * 512:(g_ + 1) * 512].rearrange("(c p) n -> p c n", p=128))

        for g_ in range(5):
            ld_ada(g_)
        for g in range(12):
            wt = wts[g % 6]
            if g + 5 < 12:
                ld_ada(g + 5)
            for j in range(4):
                oc = g * 4 + j
                for kc in range(KC):
                    k.mm(pbm[:, oc:oc + 1], wt[:, kc, j * 128:(j + 1) * 128], cs[:, kc:kc + 1],
                         start=(kc == 0), stop=(kc == KC - 1))
        k.tt(mod[:], pbm[:, 0:48], modb[:], ALU.add)
        load_cols(gcol[:, 0:8], I["norm1_g"][l], KC)
        load_cols(gcol[:, 8:16], I["norm2_g"][l], KC)
        k.ts(ncoef[:, 0:8], mod[:, 8:16], 1.0, None, ALU.add)
        k.tt(ncoef[:, 0:8], ncoef[:, 0:8], gcol[:, 0:8], ALU.mult)
        k.copy(ncoef[:, 8:16], mod[:, 0:8])
        k.ts(ncoef[:, 16:24], mod[:, 32:40], 1.0, None, ALU.add)
        k.tt(ncoef[:, 16:24], ncoef[:, 16:24], gcol[:, 8:16], ALU.mult)
        k.copy(ncoef[:, 24:32], mod[:, 24:32])
        DBG("mod%d" % l, mod[:], [128, 48])

        norm_to_hT(ncoef[:, 0:8], ncoef[:, 8:16])
        if dbg and ("hT%d" % l) in dbg:
            ar.reset()
            tmp = ar.alloc([KC, S])
            k.copy(tmp, hT[:])
            DBG("hT%d" % l, tmp, [128, KC, S])
        if stop_after == "norm1":
            break

        env = dict(k=k, nc=nc, I=I, ar=ar, hT=hT, psum=psum, bank=bank, evac=evac, ident=ident, identb=identb,
                   ones=ones, onesb=onesb, m_su=m_su, m_siu=m_siu, m_sl=m_sl, blk2=blk2, load_cols=load_cols,
                   bcast=bcast, load_w=load_w, mod=mod, xT_d=xT_d, ybr_d=ybr_d, DBG=DBG, l=l, dbg=dbg,
                   epsc=epsc)
        if stop_after not in ("sb", "rw"):
            phase_m2(env)
        if stop_after == "m2":
            break
        if stop_after != "rw":
            phase_sb(env)
        if stop_after == "sb":
            break
        phase_rw(env)
        if stop_after == "rw":
            break
        phase_merge(env)
        if stop_after == "merge":
            break
        norm_to_hT(ncoef[:, 16:24], ncoef[:, 24:32])
        phase_ffn(env)
        if stop_after == "ffn":
            break

    if stop_after is None:
        xin = norm_to_hT(gcol[:, 16:24], None, final=True)
        otile = [ar.alloc([D]), ar.alloc([D])]
        for t in range(NT):
            ot = otile[t % 2]
            for half in range(2):
                pb = bank()
                for j in range(4):
                    c = half * 4 + j
                    k.transpose(pb[:, j * 128:(j + 1) * 128], xin[:, c, t * 128:(t + 1) * 128], ident[:])
                evac(ot[:, half * 512:(half + 1) * 512], pb)
            k.dma("sp" if t % 2 == 0 else "act", out[t * 128:(t + 1) * 128, :], ot, is_output=True)
    else:
        ar.reset()
        z = ar.alloc([D])
        k.memset(z, 0.0)
        for t in range(NT):
            k.dma("sp", out[t * 128:(t + 1) * 128, :], z, is_output=True)
    k.finish()
    return k, dbg_out


from concourse.bass_utils import run_bass_kernel_spmd

_CACHE = {}


def kernel(**inputs):
    n = 8
    if "k" not in _CACHE:
        _CACHE["k"] = build()[0]
    kb_ = _CACHE["k"]
    shared = {}
    for name, shape in SHAPES:
        if name in ("x", "c"):
            continue
        a = np.asarray(inputs[name], dtype=np.float32)
        shared[name] = np.ascontiguousarray(a.reshape(shape))
    x = np.asarray(inputs["x"], dtype=np.float32)
    c = np.asarray(inputs["c"], dtype=np.float32)
    in_maps = []
    for b in range(n):
        m = dict(shared)
        m["x"] = np.ascontiguousarray(x[b])
        m["c"] = np.ascontiguousarray(c[b])
        in_maps.append(m)
    res = run_bass_kernel_spmd(kb_.nc, in_maps, core_ids=list(range(n)))
    return np.stack([np.asarray(r["out"], dtype=np.float32) for r in res.results], axis=0)
```

```python
import numpy as np
import concourse.bass as bass
import concourse.mybir as mybir

F32 = mybir.dt.float32
BF16 = mybir.dt.bfloat16
F32R = mybir.dt.float32r
AF = mybir.ActivationFunctionType
ALU = mybir.AluOpType
AX = mybir.AxisListType

SAME_ENG_SYNC = True
NDSEM = 8


class KB:
    def __init__(self):
        self.nc = bass.Bass("TRN2", target_bir_lowering=False)
        nc = self.nc
        self.eng = {"pe": nc.tensor, "dve": nc.vector, "act": nc.scalar, "pool": nc.gpsimd, "sp": nc.sync}
        self._ctx = []
        self.sem = {}
        self.cnt = {}
        for e in self.eng:
            self.sem[e] = self._enter(nc.semaphore("c_" + e))
            self.cnt[e] = 0
        self.dsem = {}
        self.dcnt = {}
        for q in ("sp", "act", "pool"):
            self.dsem[q] = [self._enter(nc.semaphore("d_%s%d" % (q, i))) for i in range(NDSEM)]
            self.dcnt[q] = 0
        self.semname = {}
        for e in self.eng:
            self.semname[id(self.sem[e])] = e
        self.seen = {e: {} for e in self.eng}
        self.acc = {}
        self.semobj = {}
        for e in self.eng:
            self.semobj[e] = self.sem[e]
        for q in self.dsem:
            for i, s in enumerate(self.dsem[q]):
                self.semobj["d_%s%d" % (q, i)] = s
        self.n_inst = 0
        self.n_wait = 0
        self.K = {}
        self.out_events = []

    def _enter(self, cm):
        v = cm.__enter__()
        self._ctx.append(cm)
        return v

    def sb(self, name, shape, dt=F32):
        return self._enter(self.nc.sbuf_tensor(name, list(shape), dt))

    def ps(self, name, shape, dt=F32):
        return self._enter(self.nc.psum_tensor(name, list(shape), dt))

    def dram(self, name, shape, dt=F32, kind="Internal"):
        return self.nc.dram_tensor(name, list(shape), dt, kind=kind).ap()

    @staticmethod
    def region(ap):
        t = ap.tensor
        name = t.name
        space = str(ap.space)
        pat = ap.ap
        off = ap.offset
        ds = 2 if t.dtype == BF16 else 4
        lo = 0
        hi = 0
        if "DRAM" in space:
            for st, n in pat:
                if st >= 0:
                    hi += st * (n - 1)
                else:
                    lo += st * (n - 1)
            return name, 0, 1, (off + lo) * ds, (off + hi + 1) * ds
        row = 1
        for d in t.shape[1:]:
            row *= d
        p0 = off // row
        pst, pn = pat[0]
        pn_eff = pn if pst != 0 else 1
        foff = off - p0 * row
        for st, n in pat[1:]:
            if st >= 0:
                hi += st * (n - 1)
            else:
                lo += st * (n - 1)
        assert 0 <= foff + lo and foff + hi < row, (name, off, pat, p0, row)
        if "PSUM" in space:
            b0 = ((foff + lo) * ds) // 2048
            b1 = ((foff + hi + 1) * ds - 1) // 2048 + 1
            return name, 0, 128, b0 * 2048, b1 * 2048
        return name, p0, p0 + pn_eff, (foff + lo) * ds, (foff + hi + 1) * ds

    def _deps(self, reads, writes, e=None):
        ev = {}

        def add(k, v):
            if ev.get(k, 0) < v:
                ev[k] = v

        regs = []
        for ap in reads:
            regs.append((self.region(ap), False, "PSUM" in str(ap.space)))
        for ap in writes:
            regs.append((self.region(ap), True, "PSUM" in str(ap.space)))
        for (name, p0, p1, f0, f1), isw, isps in regs:
            for a in self.acc.get(name, ()):
                if a[1] <= p0 or a[0] >= p1 or a[3] <= f0 or a[2] >= f1:
                    continue
                if a[4] == e:
                    if a[6] and not isw:
                        add(a[4], a[5])
                elif isw or a[6] or isps:
                    add(a[4], a[5])
        return ev, regs

    def _record(self, regs, semkey, val):
        for (name, p0, p1, f0, f1), isw, isps in regs:
            lst = self.acc.setdefault(name, [])
            if isw or isps:
                keep = []
                for a in lst:
                    cov = a[0] >= p0 and a[1] <= p1 and a[2] >= f0 and a[3] <= f1
                    if cov and (isw or not a[6] or a[4] == semkey):
                        continue
                    keep.append(a)
                lst[:] = keep
            else:
                lst[:] = [a for a in lst if not (a[4] == semkey and not a[6] and a[0] >= p0 and a[1] <= p1 and a[2] >= f0 and a[3] <= f1)]
            lst.append([p0, p1, f0, f1, semkey, val, isw])

    def _waits(self, e, ev):
        engine = self.eng[e]
        seen = self.seen[e]
        for k, v in sorted(ev.items(), key=lambda kv: -kv[1]):
            if k == e and (not SAME_ENG_SYNC or e == 'pe'):
                continue
            if seen.get(k, 0) >= v:
                continue
            engine.wait_ge(self.semobj[k], v)
            self.n_wait += 1
            seen[k] = v
            snap = self.K.get((k, v))
            if snap:
                for k2, v2 in snap.items():
                    if seen.get(k2, 0) < v2:
                        seen[k2] = v2

    def op(self, e, fn, reads, writes):
        ev, regs = self._deps(reads, writes, e)
        self._waits(e, ev)
        ins = fn(self.eng[e])
        self.cnt[e] += 1
        ins.then_inc(self.sem[e], 1)
        self.K[(e, self.cnt[e])] = dict(self.seen[e])
        self._record(regs, e, self.cnt[e])
        self.n_inst += 1
        return ins

    def dma(self, q, out, in_, is_output=False, **kw):
        ev, regs = self._deps([in_], [out])
        i = self.dcnt[q]
        slot = i % NDSEM
        key = "d_%s%d" % (q, slot)
        if i >= NDSEM:
            ev_prev = {key: 16 * (i // NDSEM)}
            self._waits(q, ev_prev)
        self._waits(q, ev)
        ins = self.eng[q].dma_start(out=out, in_=in_, **kw)
        val = 16 * (i // NDSEM + 1)
        ins.then_inc(self.semobj[key], 16)
        self.K[(key, val)] = dict(self.seen[q])
        self.dcnt[q] += 1
        self._record(regs, key, val)
        self.n_inst += 1
        if is_output:
            self.out_events.append((key, val))
        return ins

    def finish(self):
        ev = {}
        for k, v in self.out_events:
            if ev.get(k, 0) < v:
                ev[k] = v
        for q in self.dsem:
            i = self.dcnt[q]
            for s in range(min(i, NDSEM)):
                n_uses = (i - 1 - s) // NDSEM + 1
                k = "d_%s%d" % (q, s)
                if ev.get(k, 0) < 16 * n_uses:
                    ev[k] = 16 * n_uses
        for e in self.eng:
            if e != "sp" and self.cnt[e] > 0:
                ev[e] = self.cnt[e]
        self._waits("sp", ev)

    def mm(self, out, lhsT, rhs, start=True, stop=True, **kw):
        return self.op("pe", lambda en: en.matmul(out, lhsT, rhs, start=start, stop=stop, **kw), [lhsT, rhs] + ([] if start else [out]), [out])

    def transpose(self, out, in_, ident):
        return self.op("pe", lambda en: en.transpose(out, in_, ident), [in_, ident], [out])

    def act(self, out, in_, func, bias=None, scale=None, accum_out=None, e="act"):
        kw = {}
        reads = [in_]
        writes = [out]
        if bias is not None:
            kw["bias"] = bias
            if not isinstance(bias, (int, float)):
                reads.append(bias)
        if scale is not None:
            kw["scale"] = scale
            if not isinstance(scale, (int, float)):
                reads.append(scale)
        if accum_out is not None:
            kw["accum_out"] = accum_out
            writes.append(accum_out)
        return self.op("act", lambda en: en.activation(out, in_, func, **kw), reads, writes)

    def tt(self, out, a, b, op, e="dve"):
        return self.op(e, lambda en: en.tensor_tensor(out, a, b, op), [a, b], [out])

    def ts(self, out, a, s1, s2=None, op0=ALU.mult, op1=None, e="dve", accum_out=None):
        reads = [a]
        if not isinstance(s1, (int, float)):
            reads.append(s1)
        if s2 is not None and not isinstance(s2, (int, float)):
            reads.append(s2)
        kw = {}
        writes = [out]
        if op1 is not None:
            kw["op1"] = op1
        if accum_out is not None:
            kw["accum_out"] = accum_out
            writes.append(accum_out)
        return self.op(e, lambda en: en.tensor_scalar(out, a, s1, s2, op0, **kw), reads, writes)

    def stt(self, out, a, s, b, op0, op1, accum_out=None):
        reads = [a, b]
        if not isinstance(s, (int, float)):
            reads.append(s)
        kw = {}
        writes = [out]
        if accum_out is not None:
            kw["accum_out"] = accum_out
            writes.append(accum_out)
        return self.op("dve", lambda en: en.scalar_tensor_tensor(out, a, s, b, op0, op1, **kw), reads, writes)

    def scan(self, out, d0, d1, init, op0, op1):
        reads = [d0, d1]
        if not isinstance(init, (int, float)):
            reads.append(init)
        return self.op("dve", lambda en: en.tensor_tensor_scan(out, d0, d1, init, op0, op1), reads, [out])

    def copy(self, out, in_, e="dve"):
        if e == "act":
            return self.op("act", lambda en: en.copy(out, in_), [in_], [out])
        return self.op(e, lambda en: en.tensor_copy(out, in_), [in_], [out])

    def memset(self, out, val, e="dve"):
        return self.op(e, lambda en: en.memset(out, val), [], [out])

    def reduce(self, out, in_, op=ALU.add, axis=AX.X):
        return self.op("dve", lambda en: en.tensor_reduce(out, in_, axis, op), [in_], [out])

    def recip(self, out, in_):
        return self.op("dve", lambda en: en.reciprocal(out, in_), [in_], [out])


S = 2048
KC = 8
NG = 4
NT = 16
C_M2 = 3328
EPS = 1e-6


def bcl(ap, m):
    pat = [list(p) for p in ap.ap]
    return bass.AP(ap.tensor, ap.offset, pat + [[0, m]])


def bcm(ap, m):
    pat = [list(p) for p in ap.ap]
    return bass.AP(ap.tensor, ap.offset, [pat[0], [0, m]] + pat[1:])


def phase_m2(env):
    k = env["k"]; I = env["I"]; ar = env["ar"]; hT = env["hT"]; bank = env["bank"]; l = env["l"]
    load_w = env["load_w"]; load_cols = env["load_cols"]; DBG = env["DBG"]; evac = env["evac"]
    ident = env["ident"]; m_siu = env["m_siu"]; m_sl = env["m_sl"]; ones = env["ones"]; ybr_d = env["ybr_d"]
    bcast = env["bcast"]; epsc = env["epsc"]
    W = I["w_in"][l]
    ar.reset()
    BT = ar.alloc([2, S], BF16)
    CT = ar.alloc([2, S], BF16)
    xs_tok = ar.alloc([NT, 1024], BF16)
    B_tok = ar.alloc([NT, 256], BF16)
    wz = ar.alloc([KC, 1024], BF16)
    u = ar.alloc([S + 3])
    t1 = ar.alloc([S])
    wts = [ar.alloc([KC, 128], BF16), ar.alloc([KC, 128], BF16)]
    wdt = ar.alloc([KC, 16], BF16)
    cw = ar.alloc([48])
    cb = ar.alloc([12])
    dtb_bc = ar.alloc([16])
    A_bc = ar.alloc([16])
    D_bc = ar.alloc([16])
    ng_bc = ar.alloc([1024])
    sel127 = ar.alloc([128])
    dt_tok = ar.alloc([NT, 16])
    la_tok = ar.alloc([NT, 16])
    acum = ar.alloc([NT, 16])
    aend = ar.alloc([NT, 16])
    dec = ar.alloc([NT, 16])
    ea = ar.alloc([NT, 16])
    dte = ar.alloc([NT, 16])
    rhsD = ar.alloc([16, 128])
    E = ar.alloc([16, 128])
    G = ar.alloc([16, 128], BF16)
    CBm = ar.alloc([2, 128])
    xdt = ar.alloc([1024], BF16)
    xdte = ar.alloc([1024], BF16)
    y = ar.alloc([1024])
    y2 = ar.alloc([1024])
    zs = ar.alloc([1024])
    S32 = ar.alloc([2, 512])
    Sbf = ar.alloc([2, 512], BF16)
    ytT = ar.alloc([8, 128], BF16)
    ssq = ar.alloc([4])
    junk = ar.alloc([512])

    load_cols(cw, I["m2_conv_w"][l].rearrange("a b -> (a b)"), 48)
    load_cols(cb, I["m2_conv_b"][l], 12)
    bcast(dtb_bc, I["m2_dt_bias"][l], 16)
    bcast(A_bc, I["m2_a_log"][l], 16)
    bcast(D_bc, I["m2_d"][l], 16)
    bcast(ng_bc, I["m2_norm_g"][l], 1024)
    k.act(A_bc, A_bc, AF.Exp)
    k.ts(A_bc, A_bc, -1.0, None, ALU.mult)
    k.op("pool", lambda en: en.affine_select(sel127, ones[:], [[0, 128]], ALU.is_equal, 0.0, base=-127,
                                               channel_multiplier=1), [ones[:]], [sel127])
    load_w(wz, W[:, C_M2:C_M2 + 1024])
    load_w(wdt, W[:, C_M2 + 2560:C_M2 + 2576])
    k.memset(u[:, 0:3], 0.0)

    for cc in range(12):
        wt = wts[cc % 2]
        c0 = C_M2 + 1024 + cc * 128
        load_w(wt, W[:, c0:c0 + 128])
        for n in range(NG):
            pb = bank()
            for kc in range(KC):
                k.mm(pb, wt[:, kc, :], hT[:, kc, n * 512:(n + 1) * 512], start=(kc == 0), stop=(kc == KC - 1))
            k.copy(u[:, 3 + n * 512:3 + (n + 1) * 512], pb, e="act")
            k.act(t1[:, n * 512:(n + 1) * 512], pb, AF.Identity, bias=cb[:, cc:cc + 1], scale=cw[:, 36 + cc:37 + cc])
        k.stt(t1, u[:, 2:S + 2], cw[:, 24 + cc:25 + cc], t1, ALU.mult, ALU.add)
        k.stt(t1, u[:, 1:S + 1], cw[:, 12 + cc:13 + cc], t1, ALU.mult, ALU.add)
        k.stt(t1, u[:, 0:S], cw[:, cc:cc + 1], t1, ALU.mult, ALU.add)
        if cc < 10:
            k.act(t1, t1, AF.Silu)
            if cc >= 8:
                k.copy(BT[:, cc - 8, :], t1, e="pool")
            for tq in range(4):
                pb = bank()
                for j in range(4):
                    t = tq * 4 + j
                    k.transpose(pb[:, j * 128:(j + 1) * 128], t1[:, t * 128:(t + 1) * 128], ident[:])
                src = pb.rearrange("p (a b) -> p a b", a=4)
                if cc < 8:
                    evac(xs_tok[:, tq * 4:tq * 4 + 4, cc * 128:(cc + 1) * 128], src)
                else:
                    evac(B_tok[:, tq * 4:tq * 4 + 4, (cc - 8) * 128:(cc - 7) * 128], src)
        else:
            k.act(CT[:, cc - 10, :], t1, AF.Silu)

    pb = bank()
    for t in range(NT):
        for kc in range(KC):
            k.mm(pb[:, t * 16:(t + 1) * 16], hT[:, kc, t * 128:(t + 1) * 128], wdt[:, kc, :],
                 start=(kc == 0), stop=(kc == KC - 1))
    dt2 = dt_tok.rearrange("p a b -> p (a b)")
    la2 = la_tok.rearrange("p a b -> p (a b)")
    k.tt(dt_tok, pb[:, 0:256].rearrange("p (a b) -> p a b", a=NT), bcm(dtb_bc, NT), ALU.add)
    k.act(dt2, dt2, AF.Exp)
    k.act(dt2, dt2, AF.Ln, bias=1.0)
    k.tt(la_tok, dt_tok, bcm(A_bc, NT), ALU.mult)
    pb = bank()
    k.mm(pb[:, 0:256], m_siu[:], la2)
    ac2 = acum.rearrange("p a b -> p (a b)")
    k.copy(ac2, pb[:, 0:256])
    pb = bank()
    k.mm(pb[:, 0:256], sel127, ac2)
    ae2 = aend.rearrange("p a b -> p (a b)")
    k.copy(ae2, pb[:, 0:256])
    k.act(dec.rearrange("p a b -> p (a b)"), ae2, AF.Exp)
    k.act(ea.rearrange("p a b -> p (a b)"), ac2, AF.Exp)
    k.tt(ae2, ae2, ac2, ALU.subtract)
    k.act(dte.rearrange("p a b -> p (a b)"), ae2, AF.Exp)

    psum = env["psum"]

    def PB(i):
        return psum[:, i, :]

    def xs3_(t):
        return xs_tok[:, t, :].rearrange("p (h d) -> p h d", h=16)

    def prologue(t):
        ts_ = slice(t * 128, (t + 1) * 128)
        pb = PB(0)
        for g in range(2):
            k.mm(pb[:, g * 128:(g + 1) * 128], BT[:, g, ts_], CT[:, g, ts_])
        for g in range(2):
            k.tt(CBm[:, g, :], pb[:, g * 128:(g + 1) * 128], m_siu[:], ALU.mult)
        k.tt(rhsD, bcm(m_siu[:], 16), bcl(la_tok[:, t, :], 128), ALU.mult)
        for b4 in range(4):
            pb = PB(1 + (b4 % 2))
            k.mm(pb, m_sl[:], rhsD[:, b4 * 4:b4 * 4 + 4, :].rearrange("p a b -> p (a b)"))
            k.act(E[:, b4 * 4:b4 * 4 + 4, :].rearrange("p a b -> p (a b)"), pb, AF.Exp)
        for g in range(2):
            k.tt(G[:, g * 8:(g + 1) * 8, :], E[:, g * 8:(g + 1) * 8, :], bcm(CBm[:, g, :], 8), ALU.mult)
        k.tt(xdt.rearrange("p (h d) -> p h d", h=16), xs3_(t), bcl(dt_tok[:, t, :], 64), ALU.mult)
        if t < NT - 1:
            k.tt(xdte.rearrange("p (h d) -> p h d", h=16), xdt.rearrange("p (h d) -> p h d", h=16),
                 bcl(dte[:, t, :], 64), ALU.mult, e="pool")

    prologue(0)
    for t in range(NT):
        ts_ = slice(t * 128, (t + 1) * 128)
        py = [PB(3), PB(4)]
        for h in range(16):
            k.mm(py[h // 8][:, (h % 8) * 64:(h % 8 + 1) * 64], G[:, h, :], xdt[:, h * 64:(h + 1) * 64])
        po = [PB(5), PB(6)]
        if t > 0:
            for g in range(2):
                k.mm(po[g], CT[:, g, ts_], Sbf[:, g, :])
        for half in range(2):
            pz = PB(7) if half == 0 else PB(1)
            for kc in range(KC):
                k.mm(pz, hT[:, kc, ts_], wz[:, kc, half * 512:(half + 1) * 512], start=(kc == 0), stop=(kc == KC - 1))
            k.act(zs[:, half * 512:(half + 1) * 512], pz, AF.Silu)
        if t > 0:
            for g in range(2):
                k.tt(y2[:, g * 512:(g + 1) * 512].rearrange("p (h d) -> p h d", h=8),
                     po[g].rearrange("p (h d) -> p h d", h=8), bcl(ea[:, t, g * 8:(g + 1) * 8], 64), ALU.mult)
                k.tt(y[:, g * 512:(g + 1) * 512], y2[:, g * 512:(g + 1) * 512], py[g], ALU.add)
        else:
            for g in range(2):
                k.copy(y[:, g * 512:(g + 1) * 512], py[g])
        pst = [PB(3), PB(4)]
        if t < NT - 1:
            for g in range(2):
                k.mm(pst[g], B_tok[:, t, g * 128:(g + 1) * 128], xdte[:, g * 512:(g + 1) * 512])
        if t + 1 < NT:
            prologue(t + 1)
        if t < NT - 1:
            for g in range(2):
                if t == 0:
                    k.copy(S32[:, g, :], pst[g])
                else:
                    k.tt(S32[:, g, :].rearrange("p (h d) -> p h d", h=8),
                         S32[:, g, :].rearrange("p (h d) -> p h d", h=8),
                         bcl(dec[:, t, g * 8:(g + 1) * 8], 64), ALU.mult)
                    k.tt(S32[:, g, :], S32[:, g, :], pst[g], ALU.add)
                k.copy(Sbf[:, g, :], S32[:, g, :], e="act")
        k.tt(y2.rearrange("p (h d) -> p h d", h=16), xs3_(t), bcl(D_bc, 64), ALU.mult, e="pool")
        k.tt(y, y, y2, ALU.add)
        k.tt(y, y, zs, ALU.mult)
        for g in range(2):
            k.act(junk, y[:, g * 512:(g + 1) * 512], AF.Square, accum_out=ssq[:, g:g + 1])
        k.act(ssq[:, 2:4], ssq[:, 0:2], AF.Sqrt, bias=epsc[:], scale=1.0 / 512)
        k.recip(ssq[:, 2:4], ssq[:, 2:4])
        for g in range(2):
            k.stt(y[:, g * 512:(g + 1) * 512], y[:, g * 512:(g + 1) * 512], ssq[:, 2 + g:3 + g],
                  ng_bc[:, g * 512:(g + 1) * 512], ALU.mult, ALU.mult)
        if t == 3:
            DBG("ym2_%d" % l, y, [128, 1024])
        for half in range(2):
            pb = PB(5 + half)
            for j in range(4):
                c = half * 4 + j
                k.transpose(pb[:, j * 128:(j + 1) * 128], y[:, c * 128:(c + 1) * 128], ident[:])
            evac(ytT[:, half * 4:half * 4 + 4, :], pb.rearrange("p (a b) -> p a b", a=4))
        k.dma("sp", ybr_d["m2"][:, :, ts_].rearrange("c p s -> p c s"), ytT)


S = 2048
KC = 8
NG = 4
NT = 16
C_SB = 1792


def phase_sb(env):
    k = env["k"]; I = env["I"]; ar = env["ar"]; hT = env["hT"]; bank = env["bank"]; l = env["l"]
    load_w = env["load_w"]; DBG = env["DBG"]; evac = env["evac"]
    identb = env["identb"]; m_sl = env["m_sl"]; ybr_d = env["ybr_d"]
    W = I["w_in"][l]
    ar.reset()
    qT = ar.alloc([4, S], BF16)
    kT = ar.alloc([4, S], BF16)
    v_tok = ar.alloc([NT, 512], BF16)
    ysT = ar.alloc([4, S], BF16)
    wraw = [ar.alloc([S]), ar.alloc([S])]
    wts = [w_.bitcast(BF16).rearrange("p (a b) -> p a b", a=KC) for w_ in wraw]
    SBUFS = []
    for s_ in range(2):
        SBUFS.append(dict(Eb=[ar.alloc([S]), wraw[s_]], R=ar.alloc([S]), Wd=ar.alloc([S]), att=ar.alloc([S], BF16),
                          attT=ar.alloc([NT, 128], BF16)))
    Eb = SBUFS[0]["Eb"][0]
    onesS = ar.alloc([S])
    k.memset(onesS, 1.0)
    for which, dst in ((0, qT), (1, kT)):
        wt = wts[which]
        load_w(wt, W[:, C_SB + which * 512:C_SB + (which + 1) * 512])
        for c in range(4):
            for n in range(NG):
                pb = bank()
                for kc in range(KC):
                    k.mm(pb, wt[:, kc, c * 128:(c + 1) * 128], hT[:, kc, n * 512:(n + 1) * 512],
                         start=(kc == 0), stop=(kc == KC - 1))
                evac(dst[:, c, n * 512:(n + 1) * 512], pb)
    wt = wts[0]
    load_w(wt, W[:, C_SB + 1024:C_SB + 1536])
    for t in range(NT):
        pb = bank()
        for kc in range(KC):
            k.mm(pb, hT[:, kc, t * 128:(t + 1) * 128], wt[:, kc, :], start=(kc == 0), stop=(kc == KC - 1))
        evac(v_tok[:, t, :], pb)

    def unit(h, i, B, par):
        Eb = B["Eb"][par]; R = B["R"]; L = B["Wd"]; att = B["att"]; attT = B["attT"]
        c = h // 2
        po = (h % 2) * 64
        N = 128 * (i + 1)
        nb = (N + 511) // 512
        for b in range(nb):
            cols = min(512, N - b * 512)
            pb = bank()
            k.mm(pb[:, 0:cols], qT[po:po + 64, c, i * 128:(i + 1) * 128], kT[po:po + 64, c, b * 512:b * 512 + cols])
            k.act(Eb[:, b * 512:b * 512 + cols], pb[:, 0:cols], AF.Exp, scale=0.125)
        yield
        k.act(L[:, 0:N], Eb[:, 0:N], AF.Ln, bias=1.0)
        k.tt(L[:, N - 128:N], L[:, N - 128:N], m_sl[:], ALU.mult, e="pool")
        yield
        k.scan(R[:, N - 1::-1] if N > 128 else R[:, 127::-1], onesS[:, 0:N],
               L[:, N - 1::-1] if N > 128 else L[:, 127::-1], 0.0, ALU.mult, ALU.add)
        yield
        k.act(R[:, 0:N], R[:, 0:N], AF.Exp, scale=-1.0)
        yield
        k.tt(att[:, 0:N], Eb[:, 0:N], R[:, 0:N], ALU.mult)
        k.tt(att[:, N - 128:N], att[:, N - 128:N], m_sl[:], ALU.mult, e="pool")
        yield
        for g4 in range((i + 4) // 4):
            nblk = min(4, i + 1 - g4 * 4)
            pb = bank().bitcast(BF16)
            for j in range(nblk):
                kb = g4 * 4 + j
                k.transpose(pb[:, j * 128:(j + 1) * 128], att[:, kb * 128:(kb + 1) * 128], identb[:])
            evac(attT[:, g4 * 4:g4 * 4 + nblk, :], pb[:, 0:nblk * 128].rearrange("p (a b) -> p a b", a=nblk))
            if g4 % 2 == 1:
                yield
        yield
        po_b = bank()
        for kb in range(i + 1):
            k.mm(po_b[po:po + 64, 0:128], v_tok[:, kb, h * 64:(h + 1) * 64], attT[:, kb, :],
                 start=(kb == 0), stop=(kb == i))
        evac(ysT[po:po + 64, c, i * 128:(i + 1) * 128], po_b[po:po + 64, 0:128])

    pending = [(2 * c2, 2 * c2 + 1, i) for c2 in range(4) for i in range(NT)]
    active = []
    step = 0
    npair = 0
    late = []
    while pending or active or late:
        if late:
            active.append(late.pop(0))
        if pending and step % 3 == 0 and len(active) < 6:
            hA, hB, i_ = pending.pop(0)
            active.append(unit(hA, i_, SBUFS[0], npair % 2))
            late.append(unit(hB, i_, SBUFS[1], npair % 2))
            npair += 1
        for g_ in list(active):
            try:
                next(g_)
            except StopIteration:
                active.remove(g_)
        step += 1
    if env["dbg"] and ("ysb_%d" % l) in env["dbg"]:
        k.copy(Eb, ysT[:, 0, :])
        DBG("ysb_%d" % l, Eb, [128, S])
    for c in range(4):
        k.dma("sp" if c % 2 == 0 else "act", ybr_d["sb"][c], ysT[:, c, :])


S = 2048
KC = 8
NG = 4
NT = 16
GN_EPS = 64e-5
NEG_E05 = -0.6065306597126334


def phase_rw(env):
    k = env["k"]; I = env["I"]; ar = env["ar"]; hT = env["hT"]; bank = env["bank"]; l = env["l"]
    load_w = env["load_w"]; load_cols = env["load_cols"]; DBG = env["DBG"]; evac = env["evac"]
    ident = env["ident"]; identb = env["identb"]; m_su = env["m_su"]; m_siu = env["m_siu"]; m_sl = env["m_sl"]
    blk2 = env["blk2"]; ybr_d = env["ybr_d"]; bcast = env["bcast"]
    W = I["w_in"][l]
    ar.reset()
    F = [ar.alloc([S]) for _ in range(7)]
    F.append(ar.alloc([S + 4]))
    u = F[7]
    Rt = ar.alloc([S], BF16); At = ar.alloc([S], BF16); Bt = ar.alloc([S], BF16); Kt = ar.alloc([S], BF16)
    Bhf = ar.alloc([S], BF16); Khf = ar.alloc([S], BF16); Vb = ar.alloc([S], BF16)
    gT = ar.alloc([S], BF16); bonus = ar.alloc([S], BF16)
    Atok = ar.alloc([NT, 128], BF16); Bhtok = ar.alloc([NT, 128], BF16)
    Khtok = ar.alloc([NT, 128], BF16); Vtok = ar.alloc([NT, 128], BF16)
    walo = ar.alloc([S], BF16); gsig = ar.alloc([S], BF16)
    cmask = ar.alloc([S], BF16)
    yrwT = ar.alloc([S], BF16)
    w2a2 = ar.alloc([512], BF16); g2w = ar.alloc([512], BF16)
    lng = ar.alloc([512]); lnb = ar.alloc([512])
    wts = [ar.alloc([KC, 128], BF16), ar.alloc([KC, 128], BF16)]
    pc = ar.alloc([40])
    wC = ar.alloc([NT])
    msu32 = ar.alloc([128]); msl32 = ar.alloc([128]); msl64o = ar.alloc([128]); msl128o = ar.alloc([128])
    D32 = ar.alloc([128]); D64 = ar.alloc([128]); dtmp = ar.alloc([128])
    ones_ = env["ones"]
    for Dm, bs in ((D32, 32), (D64, 64)):
        for gb in range(128 // bs):
            cs2 = slice(gb * bs, (gb + 1) * bs)
            k.op("pool", lambda en: en.affine_select(dtmp[:, cs2], ones_[:, cs2], [[0, bs]], ALU.is_ge, 0.0,
                                                       base=-gb * bs, channel_multiplier=1), [ones_[:, cs2]], [dtmp[:, cs2]])
            k.op("pool", lambda en: en.affine_select(Dm[:, cs2], dtmp[:, cs2], [[0, bs]], ALU.is_ge, 0.0,
                                                       base=gb * bs + bs - 1, channel_multiplier=-1), [dtmp[:, cs2]], [Dm[:, cs2]])
    k.tt(msu32, D32, m_su[:], ALU.mult)
    k.tt(msl32, D32, m_sl[:], ALU.mult)
    k.tt(msl64o, D64, m_sl[:], ALU.mult)
    k.tt(msl128o, m_sl[:], msl64o, ALU.subtract)
    k.tt(msl64o, msl64o, msl32, ALU.subtract)
    Qz = [ar.alloc([128], BF16) for _ in range(4)]
    Pz = [ar.alloc([64], BF16) for _ in range(4)]
    fin = ar.alloc([128], BF16)
    for _b in Qz + Pz:
        k.memset(_b, 0.0)
    ST = ar.alloc([64], BF16)
    ynbs = [ar.alloc([128], BF16) for _ in range(4)]
    gneps = ar.alloc([1])
    k.memset(gneps, GN_EPS)

    mu = pc[:, 0:14]
    load_cols(mu, I["rw_mu"][l], 14)
    om = pc[:, 14:28]
    k.ts(om, mu, -1.0, 1.0, ALU.mult, ALU.add)
    pc2 = ar.alloc([24])
    load_cols(pc2[:, 0:4], I["rw_w0"][l], 4)
    load_cols(pc2[:, 4:8], I["rw_a0"][l], 4)
    load_cols(pc2[:, 8:12], I["rw_k_k"][l], 4)
    load_cols(pc2[:, 12:16], I["rw_k_a"][l], 4)
    load_cols(pc2[:, 16:20], I["rw_r_k"][l], 4)
    k.ts(pc2[:, 20:24], pc2[:, 12:16], -1.0, 1.0, ALU.mult, ALU.add)
    bcast(lng, I["rw_ln_g"][l], 512)
    bcast(lnb, I["rw_ln_b"][l], 512)
    k.dma("pool", w2a2[0:64, :], I["rw_w2"][l])
    k.dma("pool", w2a2[64:128, :], I["rw_a2"][l])
    k.dma("pool", g2w, I["rw_g2"][l])
    k.memset(cmask, 1.0)
    k.memset(cmask[:, 0::128], 0.0)
    k.memset(u[:, 0:1], 0.0)

    wi = [0]

    def proj_lerp(col0, mucol, dst):
        wt = wts[wi[0] % 2]
        wi[0] += 1
        load_w(wt, W[:, col0:col0 + 128])
        k.memset(u[:, 0:1], 0.0)
        for n in range(NG):
            pb = bank()
            for kc in range(KC):
                k.mm(pb, wt[:, kc, :], hT[:, kc, n * 512:(n + 1) * 512], start=(kc == 0), stop=(kc == KC - 1))
            k.copy(u[:, 1 + n * 512:1 + (n + 1) * 512], pb, e="act")
            k.act(dst[:, n * 512:(n + 1) * 512], pb, AF.Identity, scale=om[:, mucol:mucol + 1])
        k.stt(dst, u[:, 0:S], mu[:, mucol:mucol + 1], dst, ALU.mult, ALU.add)

    proj_lerp(1536, 12, F[0])
    k.act(F[0][0:64, :], F[0][0:64, :], AF.Tanh)
    k.copy(walo, F[0])
    proj_lerp(1664, 13, F[0])
    k.act(gsig, F[0], AF.Sigmoid)

    for c in range(4):
        cs_ = slice(c * 128, (c + 1) * 128)
        r32, k32, v32 = F[0], F[1], F[2]
        proj_lerp(c * 128, c, r32)
        proj_lerp(512 + c * 128, 4 + c, k32)
        proj_lerp(1024 + c * 128, 8 + c, v32)
        lws, lw, a32, g32 = F[3], F[4], F[5], F[6]
        for n in range(NG):
            ns = slice(n * 512, (n + 1) * 512)
            pb = bank()
            k.mm(pb, w2a2[0:64, cs_], walo[0:64, ns])
            k.act(lws[:, ns], pb, AF.Sigmoid, bias=pc2[:, c:c + 1])
            pb = bank()
            k.mm(pb, w2a2[64:128, cs_], walo[64:128, ns])
            k.act(a32[:, ns], pb, AF.Sigmoid, bias=pc2[:, 4 + c:5 + c])
            pb = bank()
            k.mm(pb, g2w[:, cs_], gsig[:, ns])
            k.copy(gT[:, ns], pb, e="act")
        k.ts(lws, lws, NEG_E05, None, ALU.mult)
        k.copy(Vb, v32, e="pool")
        kk = F[6]
        k.ts(kk, k32, pc2[:, 8 + c:9 + c], None, ALU.mult)
        k.act(u[:, 0:S], kk, AF.Square)
        for n in range(NG):
            ns = slice(n * 512, (n + 1) * 512)
            pb = bank()
            k.mm(pb, blk2[:], u[:, ns])
            k.ts(u[:, ns], pb, 1e-24, None, ALU.max)
        k.act(u[:, 0:S], u[:, 0:S], AF.Sqrt)
        k.recip(u[:, 0:S], u[:, 0:S])
        k.tt(kk, kk, u[:, 0:S], ALU.mult)
        kp = F[7][:, 0:S]
        k.ts(kp, a32, pc2[:, 12 + c:13 + c], pc2[:, 20 + c:21 + c], ALU.mult, ALU.add)
        k.tt(kp, kp, k32, ALU.mult)
        beta = F[1]
        k.tt(beta, kk, a32, ALU.mult)
        tmp = F[5]
        k.tt(tmp, r32, kp, ALU.mult)
        k.ts(tmp, tmp, pc2[:, 16 + c:17 + c], None, ALU.mult)
        for n in range(NG):
            ns = slice(n * 512, (n + 1) * 512)
            pb = bank()
            k.mm(pb, blk2[:], tmp[:, ns])
            k.tt(bonus[:, ns], pb, v32[:, ns], ALU.mult)
        k.scan(lw, cmask, lws, 0.0, ALU.mult, ALU.add)
        lwx = F[3]
        k.tt(lwx, lw, lws, ALU.subtract)
        e = F[5]
        k.act(e, lw, AF.Exp)
        k.tt(Rt, r32, e, ALU.mult)
        k.copy(wC, e[:, 127::128])
        k.act(e, lwx, AF.Exp)
        k.stt(At, kk, -1.0, e, ALU.mult, ALU.mult)
        k.act(e, lw, AF.Exp, scale=-1.0)
        k.tt(Bt, beta, e, ALU.mult)
        k.tt(Kt, kp, e, ALU.mult)
        e3 = F[0]
        k.tt(e3.rearrange("p (a b) -> p a b", a=NT), bcl(lw[:, 127::128], 128),
             lw.rearrange("p (a b) -> p a b", a=NT), ALU.subtract)
        k.act(e3, e3, AF.Exp)
        k.tt(Bhf, beta, e3, ALU.mult)
        k.tt(Khf, kp, e3, ALU.mult)
        for src, dst in ((At, Atok), (Bhf, Bhtok), (Khf, Khtok), (Vb, Vtok)):
            for tq in range(4):
                pb = bank().bitcast(BF16)
                for j in range(4):
                    t = tq * 4 + j
                    k.transpose(pb[:, j * 128:(j + 1) * 128], src[:, t * 128:(t + 1) * 128], identb[:])
                evac(dst[:, tq * 4:tq * 4 + 4, :], pb[:, 0:512].rearrange("p (a b) -> p a b", a=4))

        if not hasattr(k, "rwch"):
            k.rwch = k.sb("rwch", [128, 4 * 9 * 128], F32)
        carved = []

        def carve(Fb):
            o = [0]

            def f32(n):
                v = Fb[:, o[0]:o[0] + n]
                o[0] += n
                return v

            def b16(n):
                v = Fb[:, o[0]:o[0] + n // 2].bitcast(BF16)
                o[0] += n // 2
                return v
            B = {}
            si = len(carved)
            carved.append(1)
            base = si * 9 * 128
            ch = [k.rwch[:, base + j_ * 128:base + (j_ + 1) * 128] for j_ in range(9)]
            B["X"] = [ch[0], ch[1]]
            B["XT"] = [ch[2], ch[3]]
            for j_, nm in enumerate(("T", "TT", "M1", "No64", "No128")):
                B[nm] = ch[4 + j_]
            for nm in ("Tb", "Mbr", "Mkr", "Nkb", "AZ", "AV"):
                B[nm] = b16(128)
            B["yn"] = f32(64)
            B["st"] = f32(8)
            B["junk"] = f32(64)
            B["yraw"] = f32(64)
            return B

        BS = [carve(F[3]), carve(F[4]), carve(F[5]), carve(F[6])]

        def unit(hh, tc):
            B = BS[hh + 2 * (tc % 2)]
            qi = hh + 2 * (tc % 2)
            ynb = ynbs[tc % 4]
            X = B["X"]; XT = B["XT"]; T = B["T"]; TT = B["TT"]; M1 = B["M1"]; No64 = B["No64"]; No128 = B["No128"]
            Tb = B["Tb"]; Mbr = B["Mbr"]; Mkr = B["Mkr"]; Nkb = B["Nkb"]; AZ = B["AZ"]; AV = B["AV"]
            yn = B["yn"]; st = B["st"]; junk = B["junk"]
            ts_ = slice(tc * 128, (tc + 1) * 128)
            h = 2 * c + hh
            po = hh * 64
            ps_ = slice(po, po + 64)
            p1 = bank()
            k.mm(p1[:, 0:128], Bt[ps_, ts_], At[ps_, ts_])
            k.mm(p1[:, 128:256], Bt[ps_, ts_], Rt[ps_, ts_])
            k.mm(p1[:, 256:384], Kt[ps_, ts_], At[ps_, ts_])
            k.mm(p1[:, 384:512], Kt[ps_, ts_], Rt[ps_, ts_])
            p2 = bank()
            k.mm(p2[:, 0:128], At[ps_, ts_], Bt[ps_, ts_])
            yield
            R_ = lambda ap_: ap_.bitcast(F32R)
            k.tt(R_(X[0]), p1[:, 0:128], msu32, ALU.mult)
            k.tt(R_(XT[0]), p2[:, 0:128], msl32, ALU.mult)
            k.tt(R_(T), X[0], ident[:], ALU.add)
            k.tt(R_(No64), p2[:, 0:128], msl64o, ALU.mult)
            k.tt(R_(No128), p2[:, 0:128], msl128o, ALU.mult)
            k.tt(Mbr, p1[:, 128:256], m_siu[:], ALU.mult)
            k.tt(Nkb, p1[:, 256:384], m_su[:], ALU.mult)
            k.tt(Mkr, p1[:, 384:512], m_siu[:], ALU.mult)
            yield
            cur = 0
            for lvl in range(4):
                nxt = 1 - cur
                pa = bank()
                if lvl < 3:
                    k.mm(pa[:, 0:128], R_(XT[cur]), R_(X[cur]))
                k.mm(pa[:, 128:256], R_(X[cur]), R_(XT[cur]))
                yield
                k.copy(R_(XT[nxt]), pa[:, 128:256], e="act")
                if lvl < 3:
                    k.copy(R_(X[nxt]), pa[:, 0:128], e="act")
                pt_ = bank()
                k.mm(pt_[:, 0:128], R_(XT[nxt]), R_(T))
                yield
                k.tt(R_(T), T, pt_[:, 0:128], ALU.add)
                cur = nxt
            for mi, No in enumerate((No64, No128)):
                ptt = bank()
                k.transpose(ptt[:, 0:128], T, ident[:])
                pm = bank()
                k.mm(pm[:, 0:128], R_(No), R_(T))
                yield
                k.copy(R_(TT), ptt[:, 0:128], e="act")
                k.copy(R_(M1), pm[:, 0:128], e="act")
                pt_ = bank()
                k.mm(pt_[:, 0:128], R_(TT), R_(M1))
                if mi == 1:
                    pzt = bank()
                    k.mm(pzt[:, 0:64], Nkb, Vtok[:, tc, ps_])
                yield
                if mi == 0:
                    k.tt(R_(T), T, pt_[:, 0:128], ALU.add)
                else:
                    k.tt(Tb, T, pt_[:, 0:128], ALU.add)
            k.copy(AZ[:, 64:128], pzt[:, 0:64], e="act")
            k.copy(AZ[:, 0:64], Atok[:, tc, ps_], e="pool")
            pav = bank()
            k.mm(pav[:, 0:128], Tb, AZ)
            yield
            k.copy(AV, pav[:, 0:128], e="act")
            pq = bank()
            k.mm(pq[ps_, 0:128], AV[:, 0:64], Mbr)
            pp = bank()
            k.mm(pp[ps_, 0:64], AV[:, 0:64], Bhtok[:, tc, ps_])
            yield
            k.tt(Qz[qi][ps_, :], pq[ps_, 0:128], Rt[ps_, ts_], ALU.add)
            k.stt(Pz[qi][ps_, :], ident[ps_, ps_], wC[ps_, tc:tc + 1], pp[ps_, 0:64], ALU.mult, ALU.add)
            py = bank()
            if tc > 0:
                k.mm(py[:, 0:64], Qz[qi], ST, start=True, stop=False)
            k.mm(py[:, 0:64], Mbr, AV[:, 64:128], start=(tc == 0), stop=False)
            k.mm(py[:, 0:64], Mkr, Vtok[:, tc, ps_], start=False, stop=True)
            if tc < NT - 1:
                pcn = bank()
                if tc > 0:
                    k.mm(pcn[ps_, 0:64], Pz[qi], ST, start=True, stop=False)
                k.mm(pcn[ps_, 0:64], Bhtok[:, tc, ps_], AV[:, 64:128], start=(tc == 0), stop=False)
                k.mm(pcn[ps_, 0:64], Khtok[:, tc, ps_], Vtok[:, tc, ps_], start=False, stop=True)
                k.copy(ST[ps_, :], pcn[ps_, 0:64], e="act")
            yield
            yraw = B["yraw"]
            k.act(yraw, py[:, 0:64], AF.Copy, accum_out=st[:, 0:1])
            k.ts(st[:, 1:2], st[:, 0:1], -1.0 / 64, None, ALU.mult)
            k.act(junk, py[:, 0:64], AF.Square, bias=st[:, 1:2], accum_out=st[:, 2:3])
            yield
            k.act(st[:, 3:4], st[:, 2:3], AF.Sqrt, bias=gneps, scale=1.0 / 64)
            k.recip(st[:, 4:5], st[:, 3:4])
            k.ts(yn, yraw, st[:, 1:2], st[:, 4:5], ALU.add, ALU.mult)
            k.tt(yn, yn, lng[:, h * 64:(h + 1) * 64], ALU.mult, e="pool")
            k.tt(ynb[:, po:po + 64], yn, lnb[:, h * 64:(h + 1) * 64], ALU.add, e="pool")

        for tcp in range(0, NT, 4):
            waiting = [unit(hh_, tcp + d_) for d_ in range(4) for hh_ in range(2)]
            active = []
            while waiting or active:
                if waiting and len(active) < 4:
                    active.append(waiting.pop(0))
                for g_ in list(active):
                    try:
                        next(g_)
                    except StopIteration:
                        active.remove(g_)
            for tc in range(tcp, tcp + 4):
                ts_ = slice(tc * 128, (tc + 1) * 128)
                ptr = bank().bitcast(BF16)
                k.transpose(ptr[:, 0:128], ynbs[tc % 4], identb[:])
                k.tt(fin, ptr[:, 0:128], bonus[:, ts_], ALU.add)
                k.tt(yrwT[:, ts_], fin, gT[:, ts_], ALU.mult, e="pool")
        k.dma("sp", ybr_d["rw"][c], yrwT)
        if c == 0 and env["dbg"] and ("yrw_%d" % l) in env["dbg"]:
            k.copy(F[0], yrwT)
            DBG("yrw_%d" % l, F[0], [128, S])


S = 2048
KC = 8
NG = 4
DFF = 2816
C_GATE = 5904


def phase_merge(env):
    k = env["k"]; I = env["I"]; ar = env["ar"]; hT = env["hT"]; bank = env["bank"]; l = env["l"]
    mod = env["mod"]; xT_d = env["xT_d"]; ybr_d = env["ybr_d"]; load_w = env["load_w"]; DBG = env["DBG"]
    ar.reset()
    yb = {"rw": ar.alloc([4, S], BF16), "sb": ar.alloc([4, S], BF16), "m2": ar.alloc([8, S], BF16)}
    mergedT = ar.alloc([8, S], BF16)
    wg = [ar.alloc([KC, 384], BF16), ar.alloc([KC, 384], BF16)]
    wo = [ar.alloc([16, 128], BF16), ar.alloc([16, 128], BF16)]
    gsb = [ar.alloc([512]), ar.alloc([512])]
    acc = ar.alloc([512])
    xc = [ar.alloc([S]), ar.alloc([S])]
    for name, nch in (("rw", 4), ("sb", 4), ("m2", 8)):
        for c in range(nch):
            k.dma("sp" if c % 2 == 0 else "act", yb[name][:, c, :], ybr_d[name][c])
    wsrc = {"rw": I["rw_wo"][l], "sb": I["sb_wo"][l], "m2": I["m2_wo"][l]}
    names = ("rw", "sb", "m2")
    kcs = (4, 4, 8)
    offs = (0, 4, 8)
    gi = 0
    for f in range(8):
        wgt = wg[f % 2]
        wot = wo[f % 2]
        for b in range(3):
            load_w(wgt[:, :, b * 128:(b + 1) * 128],
                   I["w_in"][l][:, C_GATE + b * 1024 + f * 128:C_GATE + b * 1024 + (f + 1) * 128])
            load_w(wot[:, offs[b]:offs[b] + kcs[b], :], wsrc[names[b]][:, f * 128:(f + 1) * 128])
        for n in range(NG):
            ns = slice(n * 512, (n + 1) * 512)
            for b in range(3):
                pg = bank()
                for kc in range(KC):
                    k.mm(pg, wgt[:, kc, b * 128:(b + 1) * 128], hT[:, kc, ns], start=(kc == 0), stop=(kc == KC - 1))
                g = gsb[gi % 2]
                gi += 1
                k.act(g, pg, AF.Sigmoid)
                py = bank()
                for kc in range(kcs[b]):
                    k.mm(py, wot[:, offs[b] + kc, :], yb[names[b]][:, kc, ns], start=(kc == 0), stop=(kc == kcs[b] - 1))
                if b == 0:
                    k.tt(acc, g, py, ALU.mult)
                elif b == 1:
                    k.tt(g, g, py, ALU.mult)
                    k.tt(acc, acc, g, ALU.add)
                else:
                    k.tt(g, g, py, ALU.mult)
                    k.tt(mergedT[:, f, ns], acc, g, ALU.add)
    wt2 = [wg[0][:, :, 0:128], wg[1][:, :, 0:128]]
    for f in range(8):
        wt = wt2[f % 2]
        load_w(wt, I["w_out"][l][:, f * 128:(f + 1) * 128])
        x = xc[f % 2]
        k.dma("sp", x, xT_d[f])
        for n in range(NG):
            ns = slice(n * 512, (n + 1) * 512)
            pb = bank()
            for kc in range(KC):
                k.mm(pb, wt[:, kc, :], mergedT[:, kc, ns], start=(kc == 0), stop=(kc == KC - 1))
            k.stt(x[:, ns], pb, mod[:, 16 + f:17 + f], x[:, ns], ALU.mult, ALU.add)
        k.dma("sp", xT_d[f], x)
        if f == 0:
            DBG("x1_%d" % l, x, [128, S])


def phase_ffn(env):
    k = env["k"]; I = env["I"]; ar = env["ar"]; hT = env["hT"]; bank = env["bank"]; l = env["l"]
    mod = env["mod"]; xT_d = env["xT_d"]; load_w = env["load_w"]; load_cols = env["load_cols"]; DBG = env["DBG"]
    ar.reset()
    aT = ar.alloc([22, S], BF16)
    wts = [ar.alloc([KC, 256], BF16), ar.alloc([KC, 256], BF16), ar.alloc([KC, 256], BF16)]
    u = [ar.alloc([S + 2]), ar.alloc([S + 2])]
    tgs = [ar.alloc([S]), ar.alloc([S])]
    tvs = [ar.alloc([S]), ar.alloc([S])]
    cw = ar.alloc([132])
    cb = ar.alloc([44])
    load_cols(cw, I["ffn_conv_w"][l].rearrange("a b -> (a b)"), 132)
    load_cols(cb, I["ffn_conv_b"][l], 44)
    k.memset(u[0][:, 0:2], 0.0)
    k.memset(u[1][:, 0:2], 0.0)
    def ld_up(jj):
        w_ = wts[jj % 3]
        load_w(w_[:, :, 0:128], I["ffn_w_up"][l][:, jj * 128:(jj + 1) * 128])
        load_w(w_[:, :, 128:256], I["ffn_w_up"][l][:, DFF + jj * 128:DFF + (jj + 1) * 128])

    ld_up(0)
    ld_up(1)
    for j in range(22):
        wt = wts[j % 3]
        if j + 2 < 22:
            ld_up(j + 2)
        tg = tgs[j % 2]
        tv = tvs[j % 2]
        for half in range(2):
            cc = j + 22 * half
            uu = u[half]
            dst = tg if half == 0 else tv
            for n in range(NG):
                pb = bank()
                for kc in range(KC):
                    k.mm(pb, wt[:, kc, half * 128:(half + 1) * 128], hT[:, kc, n * 512:(n + 1) * 512],
                         start=(kc == 0), stop=(kc == KC - 1))
                k.copy(uu[:, 2 + n * 512:2 + (n + 1) * 512], pb, e="act")
                k.act(dst[:, n * 512:(n + 1) * 512], pb, AF.Identity, bias=cb[:, cc:cc + 1], scale=cw[:, 88 + cc:89 + cc])
            k.stt(dst, uu[:, 1:S + 1], cw[:, 44 + cc:45 + cc], dst, ALU.mult, ALU.add)
            k.stt(dst, uu[:, 0:S], cw[:, cc:cc + 1], dst, ALU.mult, ALU.add)
        k.act(tg, tg, AF.Silu)
        k.tt(aT[:, j, :], tg, tv, ALU.mult)
    if env["dbg"] and ("aT%d" % l) in env["dbg"]:
        k.copy(tgs[0], aT[:, 5, :])
        DBG("aT%d" % l, tgs[0], [128, S])
    wd = [tgs[0].bitcast(BF16)[:, 0:22 * 128].rearrange("p (a b) -> p a b", a=22),
          tvs[0].bitcast(BF16)[:, 0:22 * 128].rearrange("p (a b) -> p a b", a=22)]
    xc = [u[0][:, 0:S], u[1][:, 0:S]]
    for f in range(8):
        wt = wd[f % 2]
        load_w(wt, I["ffn_w_down"][l][:, f * 128:(f + 1) * 128])
        x = xc[f % 2]
        k.dma("sp", x, xT_d[f])
        for n in range(NG):
            ns = slice(n * 512, (n + 1) * 512)
            pb = bank()
            for kc in range(22):
                k.mm(pb, wt[:, kc, :], aT[:, kc, ns], start=(kc == 0), stop=(kc == 21))
            k.stt(x[:, ns], pb, mod[:, 40 + f:41 + f], x[:, ns], ALU.mult, ALU.add)
        k.dma("sp", xT_d[f], x)
        if f == 0:
            DBG("x2_%d" % l, x, [128, S])


L_ = 2
D = 1024
S = 2048
KC = 8
NT = 16
NG = 4
EPS = 1e-6
N_IN = 8976
C_RW = 0
C_SB = 1792
C_M2 = 3328
C_GATE = 5904
DFF = 2816

SHAPES = [
    ("x", [S, D]), ("c", [D]), ("ada_w", [L_, D, 6 * D]), ("ada_b", [L_, 6 * D]), ("norm1_g", [L_, D]),
    ("norm2_g", [L_, D]), ("w_in", [L_, D, N_IN]), ("rw_mu", [L_, 1792]), ("rw_w0", [L_, 512]),
    ("rw_w2", [L_, 64, 512]), ("rw_a0", [L_, 512]), ("rw_a2", [L_, 64, 512]), ("rw_g2", [L_, 128, 512]),
    ("rw_k_k", [L_, 512]), ("rw_k_a", [L_, 512]), ("rw_r_k", [L_, 512]), ("rw_ln_g", [L_, 512]),
    ("rw_ln_b", [L_, 512]), ("rw_wo", [L_, 512, D]), ("sb_wo", [L_, 512, D]), ("m2_conv_w", [L_, 4, 1536]),
    ("m2_conv_b", [L_, 1536]), ("m2_dt_bias", [L_, 16]), ("m2_a_log", [L_, 16]), ("m2_d", [L_, 16]),
    ("m2_norm_g", [L_, D]), ("m2_wo", [L_, D, D]), ("w_out", [L_, D, D]), ("ffn_w_up", [L_, D, 2 * DFF]),
    ("ffn_conv_w", [L_, 3, 2 * DFF]), ("ffn_conv_b", [L_, 2 * DFF]), ("ffn_w_down", [L_, DFF, D]),
    ("final_norm_g", [D]),
]


def prod(s):
    r = 1
    for v in s:
        r *= v
    return r


class Arena:
    def __init__(self, k, words):
        self.t = k.sb("arena", [128, words], F32)
        self.words = words
        self.off = 0

    def reset(self):
        self.off = 0

    def alloc(self, shape, dt=F32):
        n = prod(shape)
        w = n if dt == F32 else (n + 1) // 2
        w = (w + 3) // 4 * 4
        assert self.off + w <= self.words, ("arena overflow", self.off, w, self.words)
        v = self.t[:, self.off:self.off + w]
        self.off += w
        if dt == BF16:
            v = v.bitcast(BF16)
        v = v[:, 0:n]
        if len(shape) == 2:
            v = v.rearrange("p (a b) -> p a b", a=shape[0])
        elif len(shape) == 3:
            v = v.rearrange("p (a b c) -> p a b c", a=shape[0], b=shape[1])
        return v


def build(stop_after=None, nlayers=L_, dbg=None):
    k = KB()
    nc = k.nc
    I = {}
    for name, shape in SHAPES:
        I[name] = nc.dram_tensor(name, shape, F32, kind="ExternalInput").ap()
    out = nc.dram_tensor("out", [S, D], F32, kind="ExternalOutput").ap()
    dbg_out = {}

    def DBG(name, ap_sb, shape):
        if dbg is None or name not in dbg:
            return
        o = nc.dram_tensor("dbg_" + name, list(shape), F32, kind="ExternalOutput").ap()
        dbg_out[name] = o
        k.dma("sp", o, ap_sb, is_output=True)

    ar = Arena(k, 38 * 1024)
    hT = k.sb("hT", [128, KC, S], BF16)
    psum = k.ps("psum", [128, 8, 512], F32)
    ident = k.sb("ident", [128, 128], F32)
    identb = k.sb("identb", [128, 128], BF16)
    ones = k.sb("ones", [128, 128], F32)
    onesb = k.sb("onesb", [128, 128], BF16)
    m_su = k.sb("m_su", [128, 128], F32)
    m_siu = k.sb("m_siu", [128, 128], F32)
    m_sl = k.sb("m_sl", [128, 128], F32)
    blk2 = k.sb("blk2", [128, 128], F32)
    colst = k.sb("colst", [128, 128], F32)
    cs = k.sb("cs", [128, KC], F32)
    mod = k.sb("mod", [128, 48], F32)
    modb = k.sb("modb", [128, 48], F32)
    ncoef = k.sb("ncoef", [128, 4 * KC], F32)
    gcol = k.sb("gcol", [128, 3 * KC], F32)
    xT_d = k.dram("xT_d", [KC, 128, S], F32)
    ybr_d = {"rw": k.dram("yrw_d", [4, 128, S], BF16), "sb": k.dram("ysb_d", [4, 128, S], BF16),
             "m2": k.dram("ym2_d", [8, 128, S], BF16)}

    state = {"bank": 0, "ev": 0, "wq": 0}

    def bank():
        b = state["bank"]
        state["bank"] = (b + 1) % 8
        return psum[:, b, :]

    def evac(out_ap, in_ap):
        state["ev"] ^= 1
        if state["ev"]:
            k.copy(out_ap, in_ap, e="act")
        else:
            k.copy(out_ap, in_ap, e="dve")

    k.memset(ones[:], 1.0)
    k.copy(onesb[:], ones[:])

    def aff(dst, pattern_step, cmul, base, cmp):
        k.op("pool", lambda en: en.affine_select(dst, ones[:], [[pattern_step, 128]], cmp, 0.0, base=base,
                                                   channel_multiplier=cmul), [ones[:]], [dst])

    aff(ident[:], 1, -1, 0, ALU.is_equal)
    aff(m_su[:], 1, -1, 0, ALU.is_gt)
    aff(m_siu[:], 1, -1, 0, ALU.is_ge)
    aff(m_sl[:], -1, 1, 0, ALU.is_gt)
    k.copy(identb[:], ident[:])
    k.memset(blk2[:], 0.0)
    k.memset(blk2[0:64, 0:64], 1.0)
    k.memset(blk2[64:128, 64:128], 1.0)

    def load_cols(dst, src_flat, n, q="sp"):
        done = 0
        while done < n:
            m = min(128, n - done)
            k.dma(q, colst[0:m, :], src_flat[done * 128:(done + m) * 128].rearrange("(c p) -> c p", p=128))
            pb = bank()
            k.transpose(pb[:, 0:m], colst[0:m, :], ident[0:m, 0:m])
            k.copy(dst[:, done:done + m], pb[:, 0:m])
            done += m

    def bcast(dst, src_flat, n, q="sp"):
        k.dma(q, dst, src_flat.partition_broadcast(128))

    ar.reset()
    xs = ar.alloc([KC, S])
    for t in range(NT):
        xtile = ar.alloc([D]) if t == 0 else xtile
        k.dma("sp", xtile, I["x"][t * 128:(t + 1) * 128, :])
        for half in range(2):
            pb = bank()
            for j in range(4):
                c = half * 4 + j
                k.transpose(pb[:, j * 128:(j + 1) * 128], xtile[:, c * 128:(c + 1) * 128], ident[:])
            evac(xs[:, half * 4:half * 4 + 4, t * 128:(t + 1) * 128], pb.rearrange("p (a b) -> p a b", a=4))
    for c in range(KC):
        k.dma("sp" if c % 2 == 0 else "act", xT_d[c], xs[:, c, :])
    load_cols(cs[:], I["c"], KC)
    k.act(cs[:], cs[:], AF.Silu)
    load_cols(gcol[:, 16:24], I["final_norm_g"], KC)

    def norm_to_hT(acol, bcol, final=False):
        ar.reset()
        xin = ar.alloc([KC, S])
        sq = ar.alloc([S])
        rstd = ar.alloc([S])
        pbs = [bank() for _ in range(NG)]
        for c in range(KC):
            k.dma("sp", xin[:, c, :], xT_d[c])
            k.act(sq, xin[:, c, :], AF.Square)
            for n in range(NG):
                k.mm(pbs[n], ones[:], sq[:, n * 512:(n + 1) * 512], start=(c == 0), stop=(c == KC - 1))
        for n in range(NG):
            k.act(rstd[:, n * 512:(n + 1) * 512], pbs[n], AF.Sqrt, bias=epsc[:], scale=1.0 / D)
        DBG('sqrt', rstd, [128, S])
        k.recip(rstd, rstd)
        DBG('rstd', rstd, [128, S])
        DBG('xin', xin[:, 0, :], [128, S])
        for c in range(KC):
            k.tt(xin[:, c, :], xin[:, c, :], rstd, ALU.mult)
            if final:
                k.act(xin[:, c, :], xin[:, c, :], AF.Identity, scale=acol[:, c:c + 1])
            else:
                k.act(hT[:, c, :], xin[:, c, :], AF.Identity, bias=bcol[:, c:c + 1], scale=acol[:, c:c + 1])
        return xin

    epsc = k.sb("epsc", [128, 1], F32)
    k.memset(epsc[:], EPS)

    def load_w(dst, src_rows, q="pool"):
        k.dma(q, dst, src_rows.rearrange("(c p) n -> p c n", p=128))

    for l in range(nlayers):
        ar.reset()
        wts = [ar.alloc([KC, 512]) for _ in range(6)]
        load_cols(modb[:], I["ada_b"][l], 48)
        pbm = bank()

        def ld_ada(g_):
            k.dma(("sp", "act", "sp", "pool")[g_ % 4], wts[g_ % 6],
                  I["ada_w"][l][:, g_ * 512:(g_ + 1) * 512].rearrange("(c p) n -> p c n", p=128))

        for g_ in range(5):
            ld_ada(g_)
        for g in range(12):
            wt = wts[g % 6]
            if g + 5 < 12:
                ld_ada(g + 5)
            for j in range(4):
                oc = g * 4 + j
                for kc in range(KC):
                    k.mm(pbm[:, oc:oc + 1], wt[:, kc, j * 128:(j + 1) * 128], cs[:, kc:kc + 1],
                         start=(kc == 0), stop=(kc == KC - 1))
        k.tt(mod[:], pbm[:, 0:48], modb[:], ALU.add)
        load_cols(gcol[:, 0:8], I["norm1_g"][l], KC)
        load_cols(gcol[:, 8:16], I["norm2_g"][l], KC)
        k.ts(ncoef[:, 0:8], mod[:, 8:16], 1.0, None, ALU.add)
        k.tt(ncoef[:, 0:8], ncoef[:, 0:8], gcol[:, 0:8], ALU.mult)
        k.copy(ncoef[:, 8:16], mod[:, 0:8])
        k.ts(ncoef[:, 16:24], mod[:, 32:40], 1.0, None, ALU.add)
        k.tt(ncoef[:, 16:24], ncoef[:, 16:24], gcol[:, 8:16], ALU.mult)
        k.copy(ncoef[:, 24:32], mod[:, 24:32])
        DBG("mod%d" % l, mod[:], [128, 48])

        norm_to_hT(ncoef[:, 0:8], ncoef[:, 8:16])
        if dbg and ("hT%d" % l) in dbg:
            ar.reset()
            tmp = ar.alloc([KC, S])
            k.copy(tmp, hT[:])
            DBG("hT%d" % l, tmp, [128, KC, S])
        if stop_after == "norm1":
            break

        env = dict(k=k, nc=nc, I=I, ar=ar, hT=hT, psum=psum, bank=bank, evac=evac, ident=ident, identb=identb,
                   ones=ones, onesb=onesb, m_su=m_su, m_siu=m_siu, m_sl=m_sl, blk2=blk2, load_cols=load_cols,
                   bcast=bcast, load_w=load_w, mod=mod, xT_d=xT_d, ybr_d=ybr_d, DBG=DBG, l=l, dbg=dbg,
                   epsc=epsc)
        if stop_after not in ("sb", "rw"):
            phase_m2(env)
        if stop_after == "m2":
            break
        if stop_after != "rw":
            phase_sb(env)
        if stop_after == "sb":
            break
        phase_rw(env)
        if stop_after == "rw":
            break
        phase_merge(env)
        if stop_after == "merge":
            break
        norm_to_hT(ncoef[:, 16:24], ncoef[:, 24:32])
        phase_ffn(env)
        if stop_after == "ffn":
            break

    if stop_after is None:
        xin = norm_to_hT(gcol[:, 16:24], None, final=True)
        otile = [ar.alloc([D]), ar.alloc([D])]
        for t in range(NT):
            ot = otile[t % 2]
            for half in range(2):
                pb = bank()
                for j in range(4):
                    c = half * 4 + j
                    k.transpose(pb[:, j * 128:(j + 1) * 128], xin[:, c, t * 128:(t + 1) * 128], ident[:])
                evac(ot[:, half * 512:(half + 1) * 512], pb)
            k.dma("sp" if t % 2 == 0 else "act", out[t * 128:(t + 1) * 128, :], ot, is_output=True)
    else:
        ar.reset()
        z = ar.alloc([D])
        k.memset(z, 0.0)
        for t in range(NT):
            k.dma("sp", out[t * 128:(t + 1) * 128, :], z, is_output=True)
    k.finish()
    return k, dbg_out


from concourse.bass_utils import run_bass_kernel_spmd

_CACHE = {}


def kernel(**inputs):
    n = 8
    if "k" not in _CACHE:
        _CACHE["k"] = build()[0]
    kb_ = _CACHE["k"]
    shared = {}
    for name, shape in SHAPES:
        if name in ("x", "c"):
            continue
        a = np.asarray(inputs[name], dtype=np.float32)
        shared[name] = np.ascontiguousarray(a.reshape(shape))
    x = np.asarray(inputs["x"], dtype=np.float32)
    c = np.asarray(inputs["c"], dtype=np.float32)
    in_maps = []
    for b in range(n):
        m = dict(shared)
        m["x"] = np.ascontiguousarray(x[b])
        m["c"] = np.ascontiguousarray(c[b])
        in_maps.append(m)
    res = run_bass_kernel_spmd(kb_.nc, in_maps, core_ids=list(range(n)))
    return np.stack([np.asarray(r["out"], dtype=np.float32) for r in res.results], axis=0)
```
